# Optimizing a Trainium2 kernel written in Bass

```python
import jax, jax.numpy as jnp
from jax import lax
import numpy as np

D_MODEL = 1024
BATCH = 4
SEQ = 4096
DEPTH = 2

RWKV_HEADS = 8
RWKV_HEAD_DIM = 64
RWKV_DIM = RWKV_HEADS * RWKV_HEAD_DIM
DECAY_LORA = 64
ICLR_LORA = 64
GATE_LORA = 128
LNX_EPS = 64e-5
MLA_HEADS = 8
QK_NOPE_DIM = 64
QK_ROPE_DIM = 32
V_HEAD_DIM = 64
Q_LORA_RANK = 256
KV_LORA_RANK = 128
ROPE_THETA = 10000.0
Q_BLOCK = 128
NEG_INF = -1e30
FFN_DIM = 2816
CONV_WIDTH = 3
NORM_EPS = 1e-6
MAX_POS_OFFSET = 1024

SHIFT_COLS = 3 * RWKV_DIM + DECAY_LORA + ICLR_LORA + GATE_LORA
MLA_COLS = Q_LORA_RANK + KV_LORA_RANK + QK_ROPE_DIM
GATE_COLS = 2 * D_MODEL
IN_COLS = SHIFT_COLS + MLA_COLS + GATE_COLS

kernel_name = 'hybrid_rwkv7_mla_convffn'


def _rmsnorm(x, g, eps=NORM_EPS):
    xf = x.astype(jnp.float32)
    xf = xf * lax.rsqrt(jnp.mean(xf * xf, axis=-1, keepdims=True) + eps)
    return xf.astype(x.dtype) * g


def _token_shift(x):
    return jnp.pad(x, ((0, 0), (1, 0), (0, 0)))[:, :-1]


def _rope_tables(positions, dtype):
    inv_freq = jnp.power(ROPE_THETA, -jnp.arange(0, QK_ROPE_DIM, 2, dtype=jnp.float32) / QK_ROPE_DIM)
    ang = positions.astype(jnp.float32)[..., None] * inv_freq
    return jnp.cos(ang).astype(dtype), jnp.sin(ang).astype(dtype)


def _rope(x, cos, sin):
    half = x.shape[-1] // 2
    x1, x2 = x[..., :half], x[..., half:]
    return jnp.concatenate([x1 * cos - x2 * sin, x1 * sin + x2 * cos], axis=-1)


def _wkv7_scan(r, w, k, v, a, b):
    B, S, H, N = r.shape

    def step(state, inp):
        r_t, w_t, k_t, v_t, a_t, b_t = inp
        sa = jnp.einsum('bhij,bhj->bhi', state, a_t)
        state = (state * w_t[:, :, None, :]
                 + sa[..., None] * b_t[:, :, None, :]
                 + v_t[..., None] * k_t[:, :, None, :])
        y_t = jnp.einsum('bhij,bhj->bhi', state, r_t)
        return state, y_t

    xs = tuple(jnp.moveaxis(t, 1, 0) for t in (r, w, k, v, a, b))
    s0 = jnp.zeros((B, H, N, N), jnp.float32)
    _, y = lax.scan(step, s0, xs)
    return jnp.moveaxis(y, 0, 1)


def _rwkv7_time_mix(p_r, p_k, p_v, p_w, p_a, p_g, decay_base, w_decay_up, iclr_base,
                    w_iclr_up, w_gate_up, k_k, k_a, r_k, lnx_w, lnx_b):
    B, S, _ = p_r.shape
    H, N = RWKV_HEADS, RWKV_HEAD_DIM
    dt = p_r.dtype

    def heads(t):
        return t.reshape(B, S, H, N)

    log_w = -jax.nn.softplus(-(decay_base + jnp.tanh(p_w) @ w_decay_up)) - 0.5
    decay = jnp.exp(-jnp.exp(log_w.astype(jnp.float32)))
    iclr = jax.nn.sigmoid(iclr_base + p_a @ w_iclr_up)
    gate = jax.nn.sigmoid(p_g) @ w_gate_up
    kk = heads((p_k * k_k).astype(jnp.float32))
    kk = kk * lax.rsqrt(jnp.sum(kk * kk, axis=-1, keepdims=True) + 1e-12)
    k = p_k * (1.0 + (iclr - 1.0) * k_a)
    r_h, k_h, v_h = heads(p_r), heads(k), heads(p_v)
    a_h = heads(iclr).astype(jnp.float32)
    y = _wkv7_scan(r_h.astype(jnp.float32), heads(decay), k_h.astype(jnp.float32),
                   v_h.astype(jnp.float32), -kk, kk * a_h)
    mu = jnp.mean(y, axis=-1, keepdims=True)
    var = jnp.mean(jnp.square(y - mu), axis=-1, keepdims=True)
    y = ((y - mu) * lax.rsqrt(var + LNX_EPS)).reshape(B, S, RWKV_DIM).astype(dt) * lnx_w + lnx_b
    bonus = (jnp.sum(r_h * k_h * r_k, axis=-1, keepdims=True) * v_h).reshape(B, S, RWKV_DIM)
    return (y + bonus) * gate


def _mla(q_lat, kv_lat, k_pe, cos, sin, q_norm, w_q_up, kv_norm, w_kv_up):
    B, S, _ = q_lat.shape
    H = MLA_HEADS
    q = (_rmsnorm(q_lat, q_norm) @ w_q_up).reshape(B, S, H, QK_NOPE_DIM + QK_ROPE_DIM)
    q_nope, q_pe = q[..., :QK_NOPE_DIM], q[..., QK_NOPE_DIM:]
    q_pe = _rope(q_pe, cos[:, :, None, :], sin[:, :, None, :])
    k_pe = _rope(k_pe, cos, sin)
    kv = (_rmsnorm(kv_lat, kv_norm) @ w_kv_up).reshape(B, S, H, QK_NOPE_DIM + V_HEAD_DIM)
    k_nope, v = kv[..., :QK_NOPE_DIM], kv[..., QK_NOPE_DIM:]

    n_blocks = S // Q_BLOCK
    qn_blocks = jnp.moveaxis(q_nope.reshape(B, n_blocks, Q_BLOCK, H, QK_NOPE_DIM), 1, 0)
    qr_blocks = jnp.moveaxis(q_pe.reshape(B, n_blocks, Q_BLOCK, H, QK_ROPE_DIM), 1, 0)
    starts = jnp.arange(n_blocks, dtype=jnp.int32) * Q_BLOCK
    key_pos = jnp.arange(S, dtype=jnp.int32)
    scale = (QK_NOPE_DIM + QK_ROPE_DIM) ** -0.5

    def attend(args):
        qn, qr, q0 = args
        s = (jnp.einsum('bqhd,bkhd->bhqk', qn, k_nope)
             + jnp.einsum('bqhr,bkr->bhqk', qr, k_pe)).astype(jnp.float32) * scale
        q_pos = q0 + jnp.arange(Q_BLOCK, dtype=jnp.int32)
        s = jnp.where(key_pos[None, :] <= q_pos[:, None], s, NEG_INF)
        p = jax.nn.softmax(s, axis=-1).astype(v.dtype)
        return jnp.einsum('bhqk,bkhd->bqhd', p, v)

    o = lax.map(attend, (qn_blocks, qr_blocks, starts))
    return jnp.moveaxis(o, 0, 1).reshape(B, S, H * V_HEAD_DIM)


def _mixer_block(n, cos, sin, w_in, mu_shift, decay_base, w_decay_up, iclr_base, w_iclr_up,
                 w_gate_up, k_k, k_a, r_k, lnx_w, lnx_b, w_out_rwkv, q_norm, w_q_up, kv_norm,
                 w_kv_up, w_out_mla, w_out):
    p = n @ w_in
    p_rw = p[..., :SHIFT_COLS]
    p_rw = p_rw + (_token_shift(p_rw) - p_rw) * mu_shift
    rw_splits = [RWKV_DIM, 2 * RWKV_DIM, 3 * RWKV_DIM, 3 * RWKV_DIM + DECAY_LORA,
                 3 * RWKV_DIM + DECAY_LORA + ICLR_LORA]
    p_r, p_k, p_v, p_w, p_a, p_g = jnp.split(p_rw, rw_splits, axis=-1)
    p_mla = p[..., SHIFT_COLS:SHIFT_COLS + MLA_COLS]
    q_lat, kv_lat, k_pe = jnp.split(p_mla, [Q_LORA_RANK, Q_LORA_RANK + KV_LORA_RANK], axis=-1)
    gate_a, gate_b = jnp.split(p[..., SHIFT_COLS + MLA_COLS:], 2, axis=-1)

    y_a = _rwkv7_time_mix(p_r, p_k, p_v, p_w, p_a, p_g, decay_base, w_decay_up, iclr_base,
                          w_iclr_up, w_gate_up, k_k, k_a, r_k, lnx_w, lnx_b) @ w_out_rwkv
    y_b = _mla(q_lat, kv_lat, k_pe, cos, sin, q_norm, w_q_up, kv_norm, w_kv_up) @ w_out_mla
    merged = jax.nn.sigmoid(gate_a) * y_a + jax.nn.sigmoid(gate_b) * y_b
    return merged @ w_out


def _conv_ffn(n, w_ffn_up, conv_w, conv_b, w_ffn_down):
    S = n.shape[1]
    u_gate, u_val = jnp.split(n @ w_ffn_up, 2, axis=-1)
    up = jnp.pad(u_gate, ((0, 0), (CONV_WIDTH - 1, 0), (0, 0)))
    c = conv_b
    for j in range(CONV_WIDTH):
        c = c + conv_w[j] * up[:, j:j + S]
    return (jax.nn.gelu(c, approximate=False) * u_val) @ w_ffn_down


def setup_inputs(seed: int = 0) -> dict:
    key = jax.random.key(seed)
    ks = jax.random.split(key, 32)
    f32 = jnp.float32

    def nrm(k, shape, fan_in, s=1.0):
        return jax.random.normal(k, shape, f32) * (s * fan_in ** -0.5)

    def gain(k, shape):
        return 1.0 + 0.02 * jax.random.normal(k, shape, f32)

    H, N = RWKV_HEADS, RWKV_HEAD_DIM
    offs = jax.random.randint(ks[1], (BATCH, 1), 0, MAX_POS_OFFSET, dtype=jnp.int32)
    positions = offs + jnp.arange(SEQ, dtype=jnp.int32)[None, :]
    return {
        'x': jax.random.normal(ks[0], (BATCH, SEQ, D_MODEL), f32),
        'positions': positions,
        'attn_norm': gain(ks[2], (DEPTH, D_MODEL)),
        'w_in': nrm(ks[3], (DEPTH, D_MODEL, IN_COLS), D_MODEL),
        'mu_shift': jax.random.uniform(ks[4], (DEPTH, SHIFT_COLS), f32),
        'decay_base': jax.random.uniform(ks[5], (DEPTH, RWKV_DIM), f32, minval=-4.0, maxval=1.0),
        'w_decay_up': nrm(ks[6], (DEPTH, DECAY_LORA, RWKV_DIM), DECAY_LORA),
        'iclr_base': 0.1 * jax.random.normal(ks[7], (DEPTH, RWKV_DIM), f32),
        'w_iclr_up': nrm(ks[8], (DEPTH, ICLR_LORA, RWKV_DIM), ICLR_LORA),
        'w_gate_up': nrm(ks[9], (DEPTH, GATE_LORA, RWKV_DIM), GATE_LORA),
        'k_k': 0.85 + 0.05 * jax.random.normal(ks[10], (DEPTH, RWKV_DIM), f32),
        'k_a': 1.0 + 0.05 * jax.random.normal(ks[11], (DEPTH, RWKV_DIM), f32),
        'r_k': 0.1 * jax.random.normal(ks[12], (DEPTH, H, N), f32),
        'lnx_w': gain(ks[13], (DEPTH, RWKV_DIM)),
        'lnx_b': 0.02 * jax.random.normal(ks[14], (DEPTH, RWKV_DIM), f32),
        'w_out_rwkv': nrm(ks[15], (DEPTH, RWKV_DIM, D_MODEL), RWKV_DIM),
        'q_norm': gain(ks[16], (DEPTH, Q_LORA_RANK)),
        'w_q_up': nrm(ks[17], (DEPTH, Q_LORA_RANK, MLA_HEADS * (QK_NOPE_DIM + QK_ROPE_DIM)), Q_LORA_RANK),
        'kv_norm': gain(ks[18], (DEPTH, KV_LORA_RANK)),
        'w_kv_up': nrm(ks[19], (DEPTH, KV_LORA_RANK, MLA_HEADS * (QK_NOPE_DIM + V_HEAD_DIM)), KV_LORA_RANK),
        'w_out_mla': nrm(ks[20], (DEPTH, MLA_HEADS * V_HEAD_DIM, D_MODEL), MLA_HEADS * V_HEAD_DIM),
        'w_out': nrm(ks[21], (DEPTH, D_MODEL, D_MODEL), D_MODEL),
        'ffn_norm': gain(ks[22], (DEPTH, D_MODEL)),
        'w_ffn_up': nrm(ks[23], (DEPTH, D_MODEL, 2 * FFN_DIM), D_MODEL),
        'conv_w': nrm(ks[24], (DEPTH, CONV_WIDTH, FFN_DIM), CONV_WIDTH),
        'conv_b': 0.02 * jax.random.normal(ks[25], (DEPTH, FFN_DIM), f32),
        'w_ffn_down': nrm(ks[26], (DEPTH, FFN_DIM, D_MODEL), FFN_DIM),
        'final_norm': gain(ks[27], (D_MODEL,)),
    }


def reference(x, positions, attn_norm, w_in, mu_shift, decay_base, w_decay_up, iclr_base,
              w_iclr_up, w_gate_up, k_k, k_a, r_k, lnx_w, lnx_b, w_out_rwkv, q_norm, w_q_up,
              kv_norm, w_kv_up, w_out_mla, w_out, ffn_norm, w_ffn_up, conv_w, conv_b,
              w_ffn_down, final_norm):
    cos, sin = _rope_tables(positions, x.dtype)
    h = x
    for l in range(DEPTH):
        n = _rmsnorm(h, attn_norm[l])
        h = h + _mixer_block(n, cos, sin, w_in[l], mu_shift[l], decay_base[l], w_decay_up[l],
                             iclr_base[l], w_iclr_up[l], w_gate_up[l], k_k[l], k_a[l], r_k[l],
                             lnx_w[l], lnx_b[l], w_out_rwkv[l], q_norm[l], w_q_up[l], kv_norm[l],
                             w_kv_up[l], w_out_mla[l], w_out[l])
        n = _rmsnorm(h, ffn_norm[l])
        h = h + _conv_ffn(n, w_ffn_up[l], conv_w[l], conv_b[l], w_ffn_down[l])
    return _rmsnorm(h, final_norm)
```

```python
import math
from contextlib import ExitStack

import numpy as np
import concourse.bass as bass
import concourse.mybir as mybir
from concourse.bass_utils import run_bass_kernel_spmd

F32 = mybir.dt.float32
BF16 = mybir.dt.bfloat16
I32 = mybir.dt.int32
AF = mybir.ActivationFunctionType
ALU = mybir.AluOpType
AX = mybir.AxisListType

ENGS = ("pe", "act", "dve", "pool", "sp")

D = 1024
NCOL = 134
NCST = 1794
WIN_COLS = 4288
EPS = 1e-6
LNX_EPS = 64e-5
SCALE = 96.0 ** -0.5
NEG = -30000.0
R_R, R_K, R_V, R_W, R_A, R_G = 0, 512, 1024, 1536, 1600, 1664
R_QL, R_KVL, R_KPE, R_KPES, R_GA, R_GB = 1792, 2048, 2176, 2208, 2240, 3264
C_MU, C_DB, C_IB, C_KK, C_KA, C_RK, C_LW, C_LB, C_QN, C_KVN, C_CW, C_CB = 0, 15, 19, 23, 27, 31, 35, 39, 43, 45, 46, 112
K_ID, K_MS, K_M3, K_MST, K_CM, K_BO, K_ON, K_BA, K_RM, K_IF, K_SG = 0, 128, 256, 640, 768, 896, 1024, 1152, 1280, 1792, 1793


class Buf:
    __slots__ = ("name", "w", "r", "ap")

    def __init__(self, name, ap=None):
        self.name = name
        self.w = None
        self.r = []
        self.ap = ap

    def __getitem__(self, k):
        return self.ap[k]


class Ev:
    __slots__ = ("key", "val", "clock")

    def __init__(self, key, val, clock):
        self.key = key
        self.val = val
        self.clock = clock


class Rot:
    def __init__(self, bufs):
        self.bufs = bufs
        self.i = 0

    def next(self):
        b = self.bufs[self.i % len(self.bufs)]
        self.i += 1
        return b


class Prog:
    def __init__(self, nc, ndma=(("sp", 24), ("pool", 24), ("act", 8))):
        self.nc = nc
        self.root = ExitStack()
        self.esem = {}
        for e in ENGS:
            self.esem[e] = self.root.enter_context(nc.semaphore("es_" + e))
        self.dsem = {}
        self.dnext = {}
        self.dlast = {}
        for q, n in ndma:
            self.dsem[q] = [self.root.enter_context(nc.semaphore("ds_%s%d" % (q, i))) for i in range(n)]
            self.dnext[q] = 0
            self.dlast[q] = [None] * n
        self.cnt = {e: 0 for e in ENGS}
        self.clock = {e: {} for e in ENGS}
        self.ops = {e: [] for e in ENGS}
        self.pending_dma = []
        self.es = None
        self.uid = 0
        self.ninst = 0
        self.cache = {}
        self.nep = 0

    def sb(self, name, shape, dt=F32):
        self.uid += 1
        t = self.es.enter_context(self.nc.sbuf_tensor("%s_%d" % (name, self.uid), list(shape), dt))
        return Buf(name, t)

    def ps(self, name, shape, dt=F32):
        self.uid += 1
        t = self.es.enter_context(self.nc.psum_tensor("%s_%d" % (name, self.uid), list(shape), dt))
        return Buf(name, t)

    def sbc(self, name, shape, dt=F32, n=2):
        if name not in self.cache:
            self.cache[name] = self.rot(name, shape, dt, n)
        return self.cache[name].next()

    def rot(self, name, shape, dt, n, psum=False):
        return Rot([(self.ps if psum else self.sb)(name, shape, dt) for _ in range(n)])

    def _deps(self, e, reads, writes):
        deps = []
        for b in reads:
            if b.w is not None:
                deps.append(b.w)
        for b in writes:
            if b.w is not None:
                deps.append(b.w)
            deps.extend(b.r)
        ck = self.clock[e]
        best = {}
        for ev in deps:
            if ev.key == "pe" and e == "pe":
                continue
            if ck.get(ev.key, 0) >= ev.val:
                continue
            best[ev.key] = max(best.get(ev.key, 0), ev.val)
            for k, v in ev.clock.items():
                if ck.get(k, 0) < v:
                    ck[k] = v
            ck[ev.key] = ev.val
        return list(best.items())

    def _commit(self, ev, reads, writes):
        for b in reads:
            b.r.append(ev)
        for b in writes:
            b.w = ev
            b.r = []

    def op(self, e, fn, reads=(), writes=(), sig=True):
        waits = self._deps(e, reads, writes)
        if sig:
            self.cnt[e] += 1
            ev = Ev(e, self.cnt[e], dict(self.clock[e]))
            self.ops[e].append((waits, fn, ("e", e)))
        else:
            ev = Ev(e, self.cnt[e] + 1, dict(self.clock[e]))
            self.ops[e].append((waits, fn, ("n",)))
        self._commit(ev, reads, writes)
        return ev

    def dma(self, q, out, in_, reads=(), writes=(), slow=False):
        waits = self._deps(q, reads, writes)
        i = self.dnext[q]
        n = len(self.dsem[q])
        self.dnext[q] = (i + 1) % n
        prev = self.dlast[q][i]
        key = ("d", q, i)
        if prev is not None and self.clock[q].get(key, 0) < prev:
            waits.append((key, prev))
            self.clock[q][key] = prev
        val = (prev or 0) + 16
        self.dlast[q][i] = val
        ev = Ev(key, val, dict(self.clock[q]))
        self._commit(ev, reads, writes)
        if slow:
            self.ops[q].append((waits, lambda eng: eng.dma_start(out=out, in_=in_, allow_slow_non_contiguous=True), ("d", q, i)))
        else:
            self.ops[q].append((waits, lambda eng: eng.dma_start(out=out, in_=in_), ("d", q, i)))
        self.pending_dma.append(ev)
        return ev

    def _sem(self, key, esems=None):
        if isinstance(key, tuple):
            return self.dsem[key[1]][key[2]]
        return (esems or self.esem)[key]

    def emit(self):
        nc = self.nc
        best = {}
        for ev in self.pending_dma:
            best[ev.key] = max(best.get(ev.key, 0), ev.val)
        final_waits = list(best.items())
        for e in ("pe", "act", "dve", "pool"):
            if self.cnt[e] > 0:
                final_waits.append((e, self.cnt[e]))
        self.pending_dma = []
        ops = self.ops
        self.ops = {e: [] for e in ENGS}
        engmap = {"pe": "tensor", "act": "scalar", "dve": "vector", "pool": "gpsimd", "sp": "sync"}
        with nc.Block() as block:
            for e in ENGS:
                lst = ops[e]
                if e == "sp":
                    lst = lst + [(final_waits, None, None)]
                if not lst:
                    continue
                self.ninst += len(lst)

                def body(eng, lst=lst, e=e, esem_e=self.esem[e], esems=dict(self.esem)):
                    for waits, fn, kind in lst:
                        for k, v in waits:
                            eng.wait_ge(self._sem(k, esems), v)
                        if fn is None:
                            continue
                        ins = fn(eng)
                        if kind[0] == "e":
                            ins.then_inc(esem_e, 1)
                        elif kind[0] == "d":
                            ins.then_inc(self.dsem[kind[1]][kind[2]], 16)

                getattr(block, engmap[e])(body)
        full = {}
        for e in ENGS:
            full[e] = self.cnt[e]
        for q in self.dsem:
            for i, v in enumerate(self.dlast[q]):
                if v:
                    full[("d", q, i)] = v
        for e in ENGS:
            self.clock[e] = dict(full)

    def new_epoch(self):
        for e in ENGS:
            self.esem[e] = self.root.enter_context(self.nc.semaphore("es%d_%s" % (self.nep, e)))
            self.cnt[e] = 0
        self.nep += 1
        for e in ENGS:
            ck = {k: v for k, v in self.clock[e].items() if isinstance(k, tuple)}
            self.clock[e] = ck

    def phase(self, body):
        with ExitStack() as es:
            self.es = es
            self.cache = {}
            body()
            self.emit()
        self.es = None

    def act(self, out, in_, func, r, w, **kw):
        return self.op("act", lambda e: e.activation(out=out, in_=in_, func=func, **kw), r, w)

    def cp(self, eng, out, in_, r, w):
        if eng == "act":
            return self.op("act", lambda e: e.copy(out=out, in_=in_), r, w)
        return self.op(eng, lambda e: e.tensor_copy(out=out, in_=in_), r, w)

    def tt(self, eng, out, a, b, op, r, w):
        return self.op(eng, lambda e: e.tensor_tensor(out=out, in0=a, in1=b, op=op), r, w)

    def ts(self, eng, out, a, s1, s2, op0, op1, r, w):
        if s2 is None:
            return self.op(eng, lambda e: e.tensor_scalar(out=out, in0=a, scalar1=s1, scalar2=None, op0=op0), r, w)
        return self.op(eng, lambda e: e.tensor_scalar(out=out, in0=a, scalar1=s1, scalar2=s2, op0=op0, op1=op1), r, w)

    def stt(self, eng, out, a, s, b, op0, op1, r, w):
        return self.op(eng, lambda e: e.scalar_tensor_tensor(out=out, in0=a, scalar=s, in1=b, op0=op0, op1=op1), r, w)

    def mm(self, out, lhsT, rhs, start, stop, r, w):
        return self.op("pe", lambda e: e.matmul(out, lhsT=lhsT, rhs=rhs, start=start, stop=stop), r, w, sig=bool(stop))

    def tr(self, out, in_, ident, r, w, sig=True):
        return self.op("pe", lambda e: e.transpose(out=out, in_=in_, identity=ident), r, w, sig=sig)

    def memset(self, eng, ap, val, w):
        return self.op(eng, lambda e: e.memset(ap, val), [], w)

    def recip(self, out, in_, r, w):
        return self.op("dve", lambda e: e.reciprocal(out=out, in_=in_), r, w)

    def rsqrt(self, out, in_, r, w, scale=1.0, bias=0.0):
        self.act(out, in_, AF.Sqrt, r, w, bias=bias, scale=scale)
        self.recip(out, out, w, w)


def build(T, depth=2, dbg=(), stop=None, reps=1, pad_mb=0):
    nc = bass.Bass("TRN2", target_bir_lowering=False)
    NT = T // 128
    NB = T // 512
    assert T % 512 == 0

    def din(name, shape, dt=F32):
        return nc.dram_tensor(name, list(shape), dt, kind="ExternalInput").ap()

    def dscr(name, shape, dt=F32):
        kind = "ExternalOutput" if name in dbg else "Internal"
        return nc.dram_tensor(name, list(shape), dt, kind=kind).ap()

    x_in = din("x", [T, D])
    pos_in = din("pos", [T], I32)
    cst_in = din("cst", [128, NCST])
    cols_in = din("cols", [depth, 128, NCOL])
    rows_in = din("rows", [2 * depth + 1, D])
    wspec = [("win", D, WIN_COLS), ("wdu", 64, 512), ("wiu", 64, 512), ("wgu", 128, 512), ("woa", 512, D),
             ("wqu", 256, 1024), ("wkvu", 128, 1024), ("wob", 512, D), ("wout", D, D), ("wup", D, 5632),
             ("wdn", 2816, D)]
    wf = {}
    wb = {}
    for n, r, c in wspec:
        wf[n] = din(n, [depth, r, c])
        wb[n] = dscr(n + "_b", [depth, r, c], BF16)
    out_ap = nc.dram_tensor("out", [T, D], F32, kind="ExternalOutput").ap()

    hD = dscr("h", [T, D])
    pT = dscr("pT", [WIN_COLS, T + 1])
    CCd = dscr("CC", [128, T])
    SSd = dscr("SS", [128, T])
    rtD = dscr("rt", [512, T], BF16)
    atD = dscr("at", [512, T], BF16)
    btD = dscr("bt", [512, T], BF16)
    ktD = dscr("kt", [512, T], BF16)
    BhD = dscr("Bh", [T, 512], BF16)
    KhD = dscr("Kh", [T, 512], BF16)
    VtD = dscr("Vt", [T, 512], BF16)
    gCD = dscr("gC", [512, NT])
    bonD = dscr("bon", [512, T])
    gateD = dscr("gate", [512, T])
    yTD = dscr("yT", [512, T])
    qnD = dscr("qn", [512, T], BF16)
    qrD = dscr("qr", [256, T], BF16)
    knD = dscr("kn", [512, T], BF16)
    krD = dscr("kr", [32, T], BF16)
    vtokD = dscr("vtok", [T, 512], BF16)
    oTD = dscr("oT", [512, T], BF16)

    P = Prog(nc)
    padD = dscr("padD", [pad_mb * 2048, 128]) if pad_mb else None
    LD = "sp"
    ST = "sp"

    def done(name):
        return stop == name

    def phase_prep():
        stg = P.rot("stg", [128, 2048], F32, 3)
        ob = P.rot("ob", [128, 2048], BF16, 3)
        engs = ["dve", "act", "pool"]
        k = 0
        for l in range(depth):
            for n, r, c in wspec:
                for r0 in range(0, r, 128):
                    rr = min(128, r - r0)
                    for c0 in range(0, c, 2048):
                        cc = min(2048, c - c0)
                        s = stg.next()
                        o = ob.next()
                        P.dma(LD, s[0:rr, 0:cc], wf[n][l, r0:r0 + rr, c0:c0 + cc], writes=[s])
                        P.cp(engs[k % 3], o[0:rr, 0:cc], s[0:rr, 0:cc], [s], [o])
                        k += 1
                        P.dma(ST, wb[n][l, r0:r0 + rr, c0:c0 + cc], o[0:rr, 0:cc], reads=[o])
        if padD is not None:
            zt = P.sb("zt", [128, 128])
            P.memset("dve", zt[:], 0.0, [zt])
            P.dma(ST, padD[pad_mb * 2048 - 128:pad_mb * 2048, :], zt[:], reads=[zt])
        cst = P.sb("cst", [128, 2])
        P.dma(LD, cst[:], cst_in[:, K_IF:K_IF + 2], writes=[cst])
        C1 = 6.28125
        C2 = 2 * math.pi - C1
        for c0 in range(0, T, 512):
            pi_ = P.sbc("pi", [128, 512], I32)
            pf = P.sbc("pf", [128, 512])
            P.dma(LD, pi_[:], pos_in[c0:c0 + 512].partition_broadcast(128), writes=[pi_])
            P.cp("dve", pf[:], pi_[:], [pi_], [pf])
            ang = P.sbc("ang", [128, 512])
            P.ts("dve", ang[:], pf[:], cst[:, 0:1], None, ALU.mult, None, [pf, cst], [ang])
            for which, dst in ((0, SSd), (1, CCd)):
                a2 = P.sbc("a2", [128, 512])
                ki = P.sbc("ki", [128, 512], I32)
                kf = P.sbc("kf", [128, 512])
                m = P.sbc("m", [128, 512])
                w_ = P.sbc("w_", [128, 512])
                res = P.sbc("res", [128, 512])
                P.ts("dve", a2[:], ang[:], (math.pi / 2) if which else 0.0, None, ALU.add, None, [ang], [a2])
                P.ts("dve", ki[:], a2[:], 1.0 / (2 * math.pi), None, ALU.mult, None, [a2], [ki])
                P.cp("dve", kf[:], ki[:], [ki], [kf])
                P.stt("dve", m[:], kf[:], -C1, a2[:], ALU.mult, ALU.add, [kf, a2], [m])
                P.stt("dve", m[:], kf[:], -C2, m[:], ALU.mult, ALU.add, [kf, m], [m])
                P.ts("dve", w_[:], m[:], math.pi, -2 * math.pi, ALU.is_gt, ALU.mult, [m], [w_])
                P.tt("dve", m[:], m[:], w_[:], ALU.add, [m, w_], [m])
                P.ts("dve", w_[:], m[:], -math.pi, 2 * math.pi, ALU.is_lt, ALU.mult, [m], [w_])
                P.tt("dve", m[:], m[:], w_[:], ALU.add, [m, w_], [m])
                P.act(res[:], m[:], AF.Sin, [m], [res])
                if which == 0:
                    P.ts("dve", res[:], res[:], cst[:, 1:2], None, ALU.mult, None, [res, cst], [res])
                P.dma(ST, dst[:, c0:c0 + 512], res[:], reads=[res])

    P.phase(phase_prep)

    def load_consts(names):
        spec = {"ident": (K_ID, 128), "mstrict": (K_MS, 128), "mask3": (K_M3, 384), "mstrictT": (K_MST, 128),
                "cmask": (K_CM, 128), "bones": (K_BO, 128), "ones": (K_ON, 128), "bavg": (K_BA, 128),
                "rmask": (K_RM, 512)}
        out = {}
        for n in names:
            bf = n.endswith("_b")
            base = n[:-2] if bf else n
            o, w = spec[base]
            t = P.sb("k_" + base, [128, w])
            P.dma(LD, t[:], cst_in[:, o:o + w], writes=[t])
            if bf:
                tb = P.sb("kb_" + base, [128, w], BF16)
                P.cp("dve", tb[:], t[:], [t], [tb])
                out[n] = tb
            else:
                out[n] = t
        return out

    def norm_transpose(src_ap_fn, gb, identb, nT, b, ht_keep=None):
        for i in range(4):
            ht = ht_keep[i] if ht_keep is not None else norm_transpose.ht.next()
            P.dma(LD, ht[:], src_ap_fn(b * 4 + i), writes=[ht])
            junk = norm_transpose.junk.next()
            ssq = norm_transpose.ssq.next()
            P.act(junk[:], ht[:], AF.Square, [ht], [junk, ssq], accum_out=ssq[:, 0:1])
            P.rsqrt(ssq[:, 0:1], ssq[:, 0:1], [ssq], [ssq], scale=1.0 / D, bias=EPS)
            hs = norm_transpose.hs.next()
            P.stt("dve", hs[:], ht[:], ssq[:, 0:1], gb[:], ALU.mult, ALU.mult, [ht, ssq, gb], [hs])
            pt = norm_transpose.pt.next()
            for kc in range(8):
                P.tr(pt[:, kc * 128:(kc + 1) * 128], hs[:, kc * 128:(kc + 1) * 128], identb[:], [hs, identb], [pt], sig=(kc == 7))
            P.cp("act", nT[:, :, i * 128:(i + 1) * 128], pt[:].rearrange("p (k t) -> p k t", t=128), [pt], [nT])

    def norm_alloc():
        norm_transpose.ht = P.rot("ht", [128, D], F32, 2)
        norm_transpose.junk = P.rot("junk", [128, D], BF16, 1)
        norm_transpose.ssq = P.rot("ssq", [128, 1], F32, 2)
        norm_transpose.hs = P.rot("hs", [128, D], BF16, 2)
        norm_transpose.pt = P.rot("pt", [128, D], BF16, 2, psum=True)

    for lidx in range(depth * reps):
        l = lidx % depth
        h_src = x_in if lidx == 0 else hD
        is_last = (lidx == depth * reps - 1)
        P.new_epoch()

        def phase_a1():
            K = load_consts(["ident_b"])
            Wb = P.sb("Wb", [128, 8, WIN_COLS], BF16)
            for kc in range(8):
                P.dma(LD, Wb[:, kc, :], wb["win"][l, kc * 128:(kc + 1) * 128, :], writes=[Wb])
            gb = P.sb("gb", [128, D])
            P.dma(LD, gb[:], rows_in[2 * l, :].partition_broadcast(128), writes=[gb])
            zc = P.sb("zc", [128, 1])
            P.memset("dve", zc[:], 0.0, [zc])
            for r0 in range(0, 1792, 128):
                P.dma(ST, pT[r0:r0 + 128, 0:1], zc[:], reads=[zc], slow=True)
            norm_alloc()
            nTr = P.rot("nT", [128, 8, 512], BF16, 2)
            psr = P.rot("ps", [128, 512], F32, 4, psum=True)
            stg = P.rot("stg", [128, 512], F32, 4)
            chunks = []
            for r0 in range(0, 1536, 128):
                chunks.append((r0, 128, 0))
            chunks += [(R_W, 64, 0), (R_A, 64, 0), (R_G, 128, 0), (R_QL, 128, 0), (R_QL + 128, 128, 0),
                       (R_KVL, 128, 0), (R_KPE, 32, 0), (R_KPES, 32, 0)]
            for r0 in range(R_GA, WIN_COLS, 128):
                chunks.append((r0, 128, 1))
            k = 0
            for b in range(NB):
                nT = nTr.next()
                norm_transpose(lambda t: h_src[t * 128:(t + 1) * 128, :], gb, K["ident_b"], nT, b)
                for r0, M, sig in chunks:
                    ps = psr.next()
                    for kc in range(8):
                        P.mm(ps[0:M, :], Wb[:, kc, r0:r0 + M], nT[:, kc, :], kc == 0, kc == 7, [Wb, nT], [ps])
                    s = stg.next()
                    if sig:
                        P.act(s[0:M, :], ps[0:M, :], AF.Sigmoid, [ps], [s])
                    elif k % 2 == 0:
                        P.cp("dve", s[0:M, :], ps[0:M, :], [ps], [s])
                    else:
                        P.cp("act", s[0:M, :], ps[0:M, :], [ps], [s])
                    k += 1
                    P.dma(ST, pT[r0:r0 + M, 1 + b * 512:1 + (b + 1) * 512], s[0:M, :], reads=[s])

        P.phase(phase_a1)
        if done("a1_%d" % l):
            break

        def phase_a2():
            K = load_consts(["ident_b", "bones", "rmask"])
            identb = K["ident_b"]
            cols = P.sb("cols", [128, NCOL])
            P.dma(LD, cols[:], cols_in[l, :, :], writes=[cols])
            omk = P.sb("omk", [128, 4])
            P.ts("dve", omk[:], cols[:, C_KA:C_KA + 4], -1.0, 1.0, ALU.mult, ALU.add, [cols], [omk])
            wdu = P.sb("wdu", [64, 512], BF16)
            wiu = P.sb("wiu", [64, 512], BF16)
            wgu = P.sb("wgu", [128, 512], BF16)
            P.dma(LD, wdu[:], wb["wdu"][l, :, :], writes=[wdu])
            P.dma(LD, wiu[:], wb["wiu"][l, :, :], writes=[wiu])
            P.dma(LD, wgu[:], wb["wgu"][l, :, :], writes=[wgu])
            shr = P.rot("sh", [128, 513], F32, 4)
            dtmp = P.rot("dtmp", [128, 512], F32, 2)
            psr = P.rot("ps", [128, 512], F32, 4, psum=True)
            ptr = P.rot("ptT", [128, 512], BF16, 2, psum=True)
            stb = P.rot("stb", [128, 512], BF16, 6)
            stf = P.rot("stf", [128, 512], F32, 4)
            sttok = P.rot("sttok", [128, 4, 128], BF16, 3)

            def shift_mix(row0, M, mucol, b, out):
                sh = shr.next()
                P.dma(LD, sh[0:M, :], pT[row0:row0 + M, b * 512:b * 512 + 513], writes=[sh])
                d = dtmp.next()
                P.tt("dve", d[0:M, :], sh[0:M, 0:512], sh[0:M, 1:513], ALU.subtract, [sh], [d])
                P.stt("dve", out[0:M, :], d[0:M, :], cols[0:M, mucol:mucol + 1], sh[0:M, 1:513], ALU.mult, ALU.add,
                      [d, cols, sh], [out])

            def store_feat(dst, c, b, src):
                P.dma(ST, dst[c * 128:(c + 1) * 128, b * 512:(b + 1) * 512], src[:], reads=[src])

            def store_tok(dst, c, b, srcb):
                pt = ptr.next()
                for i in range(4):
                    P.tr(pt[:, i * 128:(i + 1) * 128], srcb[:, i * 128:(i + 1) * 128], identb[:], [srcb, identb], [pt], sig=(i == 3))
                s = sttok.next()
                P.cp("act", s[:], pt[:].rearrange("p (i c) -> p i c", c=128), [pt], [s])
                P.dma(ST, dst[b * 512:(b + 1) * 512, c * 128:(c + 1) * 128].rearrange("(i p) c -> p i c", p=128), s[:],
                      reads=[s])

            for b in range(NB):
                mw = P.sbc("mw", [64, 512])
                ma = P.sbc("ma", [64, 512])
                mg = P.sbc("mg", [128, 512])
                shift_mix(R_W, 64, C_MU + 12, b, mw)
                shift_mix(R_A, 64, C_MU + 13, b, ma)
                shift_mix(R_G, 128, C_MU + 14, b, mg)
                tw = P.sbc("tw", [64, 512], BF16)
                pab = P.sbc("pab", [64, 512], BF16)
                sgg = P.sbc("sgg", [128, 512], BF16)
                P.act(tw[:], mw[:], AF.Tanh, [mw], [tw])
                P.cp("dve", pab[:], ma[:], [ma], [pab])
                P.act(sgg[:], mg[:], AF.Sigmoid, [mg], [sgg])
                for c in range(4):
                    cs = slice(c * 128, (c + 1) * 128)
                    rm = P.sbc("rm", [128, 512])
                    km = P.sbc("km", [128, 512])
                    vm = P.sbc("vm", [128, 512])
                    shift_mix(R_R + c * 128, 128, C_MU + c, b, rm)
                    shift_mix(R_K + c * 128, 128, C_MU + 4 + c, b, km)
                    shift_mix(R_V + c * 128, 128, C_MU + 8 + c, b, vm)
                    ps = psr.next()
                    P.mm(ps[:], wdu[:, cs], tw[:], True, True, [wdu, tw], [ps])
                    ld = P.sbc("ld", [128, 512])
                    P.act(ld[:], ps[:], AF.Sigmoid, [ps, cols], [ld], bias=cols[:, C_DB + c:C_DB + c + 1], scale=1.0)
                    P.ts("dve", ld[:], ld[:], -math.exp(-0.5), None, ALU.mult, None, [ld], [ld])
                    cum = P.sbc("cum", [128, 512])
                    P.op("dve", lambda e, cum=cum, ld=ld: e.tensor_tensor_scan(
                        out=cum[:], data0=K["rmask"][:], data1=ld[:], initial=0.0, op0=ALU.mult, op1=ALU.add),
                        [K["rmask"], ld], [cum])
                    G = P.sbc("G", [128, 512])
                    Gi = P.sbc("Gi", [128, 512])
                    Gp = P.sbc("Gp", [128, 512])
                    Ge = P.sbc("Ge", [128, 512])
                    P.act(G[:], cum[:], AF.Exp, [cum], [G])
                    P.act(Gi[:], cum[:], AF.Exp, [cum], [Gi], scale=-1.0)
                    P.tt("dve", Gp[:], cum[:], ld[:], ALU.subtract, [cum, ld], [Gp])
                    P.act(Gp[:], Gp[:], AF.Exp, [Gp], [Gp])
                    cum3 = cum[:].rearrange("p (a t) -> p a t", t=128)
                    P.tt("dve", Ge[:].rearrange("p (a t) -> p a t", t=128), cum3[:, :, 127:128].to_broadcast([128, 4, 128]),
                         cum3, ALU.subtract, [cum], [Ge])
                    P.act(Ge[:], Ge[:], AF.Exp, [Ge], [Ge])
                    gc = P.sbc("gc", [128, 4])
                    P.act(gc[:].rearrange("p (a o) -> p a o", o=1), cum3[:, :, 127:128], AF.Exp, [cum], [gc])
                    P.dma(ST, gCD[cs, b * 4:(b + 1) * 4], gc[:], reads=[gc])
                    ps = psr.next()
                    P.mm(ps[:], wiu[:, cs], pab[:], True, True, [wiu, pab], [ps])
                    ic = P.sbc("ic", [128, 512])
                    P.act(ic[:], ps[:], AF.Sigmoid, [ps, cols], [ic], bias=cols[:, C_IB + c:C_IB + c + 1], scale=1.0)
                    ps = psr.next()
                    P.mm(ps[:], wgu[:, cs], sgg[:], True, True, [wgu, sgg], [ps])
                    gt = stf.next()
                    P.cp("act", gt[:], ps[:], [ps], [gt])
                    store_feat(gateD, c, b, gt)
                    kk = P.sbc("kk", [128, 512])
                    sq = P.sbc("sq", [128, 512])
                    P.ts("dve", kk[:], km[:], cols[:, C_KK + c:C_KK + c + 1], None, ALU.mult, None, [km, cols], [kk])
                    P.tt("pool", sq[:], kk[:], kk[:], ALU.mult, [kk], [sq])
                    ps = psr.next()
                    P.mm(ps[:], K["bones"][:], sq[:], True, True, [K["bones"], sq], [ps])
                    rn = P.sbc("rn", [128, 512])
                    P.rsqrt(rn[:], ps[:], [ps], [rn], scale=1.0, bias=1e-12)
                    P.tt("dve", kk[:], kk[:], rn[:], ALU.mult, [kk, rn], [kk])
                    tk = P.sbc("tk", [128, 512])
                    P.ts("dve", tk[:], ic[:], cols[:, C_KA + c:C_KA + c + 1], omk[:, c:c + 1], ALU.mult, ALU.add,
                         [ic, cols, omk], [tk])
                    kf = P.sbc("kf", [128, 512])
                    P.tt("dve", kf[:], km[:], tk[:], ALU.mult, [km, tk], [kf])
                    bb = P.sbc("bb", [128, 512])
                    P.tt("pool", bb[:], kk[:], ic[:], ALU.mult, [kk, ic], [bb])
                    o = stb.next()
                    P.tt("dve", o[:], rm[:], G[:], ALU.mult, [rm, G], [o])
                    store_feat(rtD, c, b, o)
                    o = stb.next()
                    P.stt("dve", o[:], kk[:], -1.0, Gp[:], ALU.mult, ALU.mult, [kk, Gp], [o])
                    store_feat(atD, c, b, o)
                    o = stb.next()
                    P.tt("dve", o[:], bb[:], Gi[:], ALU.mult, [bb, Gi], [o])
                    store_feat(btD, c, b, o)
                    o = stb.next()
                    P.tt("dve", o[:], kf[:], Gi[:], ALU.mult, [kf, Gi], [o])
                    store_feat(ktD, c, b, o)
                    o = stb.next()
                    P.tt("dve", o[:], bb[:], Ge[:], ALU.mult, [bb, Ge], [o])
                    store_tok(BhD, c, b, o)
                    o = stb.next()
                    P.tt("pool", o[:], kf[:], Ge[:], ALU.mult, [kf, Ge], [o])
                    store_tok(KhD, c, b, o)
                    o = stb.next()
                    P.cp("pool", o[:], vm[:], [vm], [o])
                    store_tok(VtD, c, b, o)
                    rk = P.sbc("rk", [128, 512])
                    P.stt("dve", rk[:], rm[:], cols[:, C_RK + c:C_RK + c + 1], kf[:], ALU.mult, ALU.mult, [rm, cols, kf], [rk])
                    ps = psr.next()
                    P.mm(ps[:], K["bones"][:], rk[:], True, True, [K["bones"], rk], [ps])
                    bo = stf.next()
                    P.tt("dve", bo[:], ps[:], vm[:], ALU.mult, [ps, vm], [bo])
                    store_feat(bonD, c, b, bo)

        P.phase(phase_a2)
        if done("a2_%d" % l):
            break

        def phase_scan():
            K = load_consts(["ident", "mstrict", "mask3", "mstrictT"])
            Hf = P.sb("Hf", [64, 8, 64])
            Hb = P.sb("Hb", [64, 8, 64], BF16)
            Ht = P.sb("Htmp", [64, 8, 64])
            P.memset("dve", Hf[:], 0.0, [Hf])
            P.memset("dve", Hb[:], 0.0, [Hb])
            artr = P.rot("art", [64, 8, 256], BF16, 2)
            btr = P.rot("btl", [64, 8, 128], BF16, 2)
            ktr = P.rot("ktl", [64, 8, 128], BF16, 2)
            Bhr = P.rot("Bht", [128, 512], BF16, 2)
            Khr = P.rot("Kht", [128, 512], BF16, 2)
            Vr = P.rot("Vtt", [128, 512], BF16, 2)
            gcr = P.rot("gct", [64, 8], F32, 2)
            PAr = P.rot("PA", [128, 512], F32, 2, psum=True)
            PX = P.ps("PX", [128, 512])
            PDt = P.ps("PD", [128, 512])
            PD = PDt
            PB = PDt
            PW = P.ps("PW", [128, 512])
            PU = P.ps("PU", [128, 512])
            PY = P.ps("PY", [64, 512])
            PH = P.ps("PH", [64, 512])
            AX3r = P.rot("AX3", [128, 8, 384], BF16, 2)
            Ttr = P.rot("Tt", [128, 8, 128], BF16, 2)
            Nfr = P.rot("Nf", [128, 128], F32, 2)
            Afr = P.rot("Af", [128, 128], F32, 2)
            NAr = P.rot("NA", [128, 256], F32, 3)
            Pmr = P.rot("Pm", [128, 128], F32, 3)
            Wsr = P.rot("Wsb", [128, 512], BF16, 2)
            Ubr = P.rot("Ub", [128, 512], BF16, 2)
            Ysr = P.rot("Ysb", [64, 8, 128], F32, 2)

            for n in range(NT):
                ts_ = slice(n * 128, (n + 1) * 128)
                art = artr.next()
                btl = btr.next()
                ktl = ktr.next()
                Bht = Bhr.next()
                Kht = Khr.next()
                Vtt = Vr.next()
                gct = gcr.next()
                P.dma(LD, art[:, :, 0:128], atD[:, ts_].rearrange("(h j) t -> j h t", j=64), writes=[art])
                P.dma(LD, art[:, :, 128:256], rtD[:, ts_].rearrange("(h j) t -> j h t", j=64), writes=[art])
                P.dma(LD, btl[:], btD[:, ts_].rearrange("(h j) t -> j h t", j=64), writes=[btl])
                P.dma(LD, ktl[:], ktD[:, ts_].rearrange("(h j) t -> j h t", j=64), writes=[ktl])
                P.dma(LD, Bht[:], BhD[ts_, :], writes=[Bht])
                P.dma(LD, Kht[:], KhD[ts_, :], writes=[Kht])
                P.dma(LD, Vtt[:], VtD[ts_, :], writes=[Vtt])
                P.dma(LD, gct[:], gCD[:, n:n + 1].rearrange("(h j) o -> j (h o)", j=64), writes=[gct], slow=True)
                AX3 = AX3r.next()
                Tt = Ttr.next()
                for h in range(8):
                    PA = PAr.next()
                    P.mm(PA[:, 0:256], btl[:, h, :], art[:, h, :], True, True, [btl, art], [PA])
                    P.mm(PA[:, 256:512], ktl[:, h, :], art[:, h, :], True, True, [ktl, art], [PA])
                    P.mm(PB[:, 128:256], art[:, h, 0:128], btl[:, h, :], True, True, [art, btl], [PB])
                    Nf = Nfr.next()
                    Af = Afr.next()
                    P.tt("dve", Nf[:], PA[:, 0:128], K["mstrict"][:], ALU.mult, [PA, K["mstrict"]], [Nf])
                    P.tt("dve", AX3[:, h, :], PA[:, 128:512], K["mask3"][:], ALU.mult, [PA, K["mask3"]], [AX3])
                    P.tt("dve", Af[:], PB[:, 128:256], K["mstrictT"][:], ALU.mult, [PB, K["mstrictT"]], [Af])
                    Pm = Pmr.next()
                    P.tt("pool", Pm[:], Nf[:], K["ident"][:], ALU.add, [Nf, K["ident"]], [Pm])
                    Nc, Ac, Nb_, Ab_ = Nf[:], Af[:], Nf, Af
                    for lv in range(6):
                        last = lv == 5
                        if not last:
                            P.mm(PX[:, 0:128], Ac, Nc, True, True, [Ab_, Nb_], [PX])
                        P.mm(PX[:, 128:256], Nc, Ac, True, True, [Ab_, Nb_], [PX])
                        NA = NAr.next()
                        if last:
                            P.cp("act", NA[:, 128:256], PX[:, 128:256], [PX], [NA])
                        else:
                            P.cp("act", NA[:, 0:256], PX[:, 0:256], [PX], [NA])
                        P.mm(PD[:, 0:128], NA[:, 128:256], Pm[:], True, True, [NA, Pm], [PD])
                        Pn = Pmr.next()
                        if last:
                            P.tt("dve", Tt[:, h, :], Pm[:], PD[:, 0:128], ALU.add, [Pm, PD], [Tt])
                        else:
                            P.tt("dve", Pn[:], Pm[:], PD[:, 0:128], ALU.add, [Pm, PD], [Pn])
                        Pm = Pn
                        Nc, Ac, Nb_, Ab_ = NA[:, 0:128], NA[:, 128:256], NA, NA
                for h in range(8):
                    hs = slice(h * 64, (h + 1) * 64)
                    P.mm(PW[:, hs], art[:, h, 0:128], Hb[:, h, :], True, False, [art, Hb], [PW])
                    P.mm(PW[:, hs], AX3[:, h, 128:256], Vtt[:, hs], False, True, [AX3, Vtt], [PW])
                Wsb = Wsr.next()
                P.cp("act", Wsb[:], PW[:], [PW], [Wsb])
                for h in range(8):
                    hs = slice(h * 64, (h + 1) * 64)
                    P.mm(PU[:, hs], Tt[:, h, :], Wsb[:, hs], True, True, [Tt, Wsb], [PU])
                Ub = Ubr.next()
                P.cp("dve", Ub[:], PU[:], [PU], [Ub])
                Ysb = Ysr.next()
                for half in range(2):
                    for hh in range(4):
                        h = half * 4 + hh
                        hs = slice(h * 64, (h + 1) * 64)
                        yo = PY[:, hh * 128:(hh + 1) * 128]
                        P.mm(yo, Hb[:, h, :], art[:, h, 128:256], True, False, [Hb, art], [PY])
                        P.mm(yo, Ub[:, hs], AX3[:, h, 0:128], False, False, [Ub, AX3], [PY])
                        P.mm(yo, Vtt[:, hs], AX3[:, h, 256:384], False, True, [Vtt, AX3], [PY])
                    P.cp("act", Ysb[:, half * 4:(half + 1) * 4, :], PY[:].rearrange("p (h t) -> p h t", t=128), [PY], [Ysb])
                P.dma(ST, yTD[:, ts_].rearrange("(h i) t -> i h t", i=64), Ysb[:], reads=[Ysb])
                for h in range(8):
                    hs = slice(h * 64, (h + 1) * 64)
                    P.mm(PH[:, hs], Bht[:, hs], Ub[:, hs], True, False, [Bht, Ub], [PH])
                    P.mm(PH[:, hs], Kht[:, hs], Vtt[:, hs], False, True, [Kht, Vtt], [PH])
                P.tt("dve", Ht[:], Hf[:], gct[:].rearrange("p (h o) -> p h o", o=1).to_broadcast([64, 8, 64]), ALU.mult,
                     [Hf, gct], [Ht])
                P.tt("dve", Hf[:], Ht[:], PH[:].rearrange("p (h i) -> p h i", i=64), ALU.add, [Ht, PH], [Hf])
                P.cp("act", Hb[:], Hf[:], [Hf], [Hb])

        P.phase(phase_scan)
        if done("scan_%d" % l):
            break

        def phase_m0():
            K = load_consts(["ones"])
            ones = K["ones"]
            cols = P.sb("cols", [128, NCOL])
            P.dma(LD, cols[:], cols_in[l, :, :], writes=[cols])
            wqu = P.sb("wqu", [128, 2, 1024], BF16)
            wkvu = P.sb("wkvu", [128, 1024], BF16)
            for c in range(2):
                P.dma(LD, wqu[:, c, :], wb["wqu"][l, c * 128:(c + 1) * 128, :], writes=[wqu])
            P.dma(LD, wkvu[:], wb["wkvu"][l, :, :], writes=[wkvu])
            psr = P.rot("ps", [128, 512], F32, 6, psum=True)
            stb = P.rot("stb", [128, 512], BF16, 4)
            for b in range(NB):
                bs = slice(1 + b * 512, 1 + (b + 1) * 512)
                bs0 = slice(b * 512, (b + 1) * 512)
                CC = P.sbc("CC", [128, 512])
                SS = P.sbc("SS", [128, 512])
                P.dma(LD, CC[:], CCd[:, bs0], writes=[CC])
                P.dma(LD, SS[:], SSd[:, bs0], writes=[SS])
                ql = [P.sbc("ql%d" % c, [128, 512]) for c in range(2)]
                kvl = P.sbc("kvl", [128, 512])
                kpe = P.sbc("kpe", [32, 512])
                kpes = P.sbc("kpes", [32, 512])
                for c in range(2):
                    P.dma(LD, ql[c][:], pT[R_QL + c * 128:R_QL + (c + 1) * 128, bs], writes=[ql[c]])
                P.dma(LD, kvl[:], pT[R_KVL:R_KVL + 128, bs], writes=[kvl])
                P.dma(LD, kpe[:], pT[R_KPE:R_KPE + 32, bs], writes=[kpe])
                P.dma(LD, kpes[:], pT[R_KPES:R_KPES + 32, bs], writes=[kpes])
                ps = psr.next()
                for c in range(2):
                    sq = P.sbc("sq", [128, 512])
                    P.act(sq[:], ql[c][:], AF.Square, [ql[c]], [sq])
                    P.mm(ps[:], ones[:], sq[:], c == 0, c == 1, [ones, sq], [ps])
                rs = P.sbc("rs", [128, 512])
                P.rsqrt(rs[:], ps[:], [ps], [rs], scale=1.0 / 256, bias=EPS)
                qn = [P.sbc("qn%d" % c, [128, 512], BF16) for c in range(2)]
                for c in range(2):
                    P.stt("dve", qn[c][:], ql[c][:], cols[:, C_QN + c:C_QN + c + 1], rs[:], ALU.mult, ALU.mult,
                          [ql[c], cols, rs], [qn[c]])
                for hp in range(4):
                    ps = psr.next()
                    for c in range(2):
                        P.mm(ps[:], wqu[:, c, hp * 128:(hp + 1) * 128], qn[c][:], c == 0, c == 1, [wqu, qn[c]], [ps])
                    o = stb.next()
                    P.cp("act", o[:], ps[:], [ps], [o])
                    P.dma(ST, qnD[hp * 128:(hp + 1) * 128, bs0], o[:], reads=[o])
                for g in range(2):
                    ps1 = psr.next()
                    ps2 = psr.next()
                    for c in range(2):
                        P.mm(ps1[:], wqu[:, c, 512 + g * 128:512 + (g + 1) * 128], qn[c][:], c == 0, c == 1, [wqu, qn[c]], [ps1])
                    for c in range(2):
                        P.mm(ps2[:], wqu[:, c, 768 + g * 128:768 + (g + 1) * 128], qn[c][:], c == 0, c == 1, [wqu, qn[c]], [ps2])
                    t1 = P.sbc("t1", [128, 512])
                    t2 = P.sbc("t2", [128, 512])
                    P.tt("dve", t1[:], ps1[:], CC[:], ALU.mult, [ps1, CC], [t1])
                    P.tt("dve", t2[:], ps2[:], SS[:], ALU.mult, [ps2, SS], [t2])
                    o = stb.next()
                    P.tt("pool", o[:], t1[:], t2[:], ALU.add, [t1, t2], [o])
                    P.dma(ST, qrD[g * 128:(g + 1) * 128, bs0], o[:], reads=[o])
                sq = P.sbc("sq", [128, 512])
                P.act(sq[:], kvl[:], AF.Square, [kvl], [sq])
                ps = psr.next()
                P.mm(ps[:], ones[:], sq[:], True, True, [ones, sq], [ps])
                rs2 = P.sbc("rs2", [128, 512])
                P.rsqrt(rs2[:], ps[:], [ps], [rs2], scale=1.0 / 128, bias=EPS)
                kvn = P.sbc("kvn", [128, 512], BF16)
                P.stt("dve", kvn[:], kvl[:], cols[:, C_KVN:C_KVN + 1], rs2[:], ALU.mult, ALU.mult, [kvl, cols, rs2], [kvn])
                for hp in range(4):
                    ps = psr.next()
                    P.mm(ps[:], wkvu[:, hp * 128:(hp + 1) * 128], kvn[:], True, True, [wkvu, kvn], [ps])
                    o = stb.next()
                    P.cp("act", o[:], ps[:], [ps], [o])
                    P.dma(ST, knD[hp * 128:(hp + 1) * 128, bs0], o[:], reads=[o])
                for i in range(4):
                    ps = psr.next()
                    P.mm(ps[:], kvn[:, i * 128:(i + 1) * 128], wkvu[:, 512:1024], True, True, [kvn, wkvu], [ps])
                    o = stb.next()
                    P.cp("act", o[:], ps[:], [ps], [o])
                    P.dma(ST, vtokD[b * 512 + i * 128:b * 512 + (i + 1) * 128, :], o[:], reads=[o])
                t1 = P.sbc("t1", [128, 512])
                t2 = P.sbc("t2", [128, 512])
                P.tt("dve", t1[0:32, :], kpe[:], CC[0:32, :], ALU.mult, [kpe, CC], [t1])
                P.tt("dve", t2[0:32, :], kpes[:], SS[0:32, :], ALU.mult, [kpes, SS], [t2])
                o = stb.next()
                P.tt("pool", o[0:32, :], t1[0:32, :], t2[0:32, :], ALU.add, [t1, t2], [o])
                P.dma(ST, krD[:, bs0], o[0:32, :], reads=[o])

        P.phase(phase_m0)
        if done("m0_%d" % l):
            break

        def phase_attn():
            K = load_consts(["ident_b", "cmask_b"])
            identb = K["ident_b"]
            cmaskb = K["cmask_b"]
            Kn = P.sb("Kn", [64, 8, T], BF16)
            Kr = P.sb("Kr", [32, T], BF16)
            V = P.sb("V", [128, NT, 512], BF16)
            for h in range(8):
                P.dma(LD, Kn[:, h, :], knD[h * 64:(h + 1) * 64, :], writes=[Kn])
            P.dma(LD, Kr[:], krD[:, :], writes=[Kr])
            for n in range(NT):
                P.dma(LD, V[:, n, :], vtokD[n * 128:(n + 1) * 128, :], writes=[V])
            Qnr = P.rot("Qn", [64, 8, 128], BF16, 2)
            Qrr = P.rot("Qr", [32, 8, 128], BF16, 2)
            PS1 = P.rot("PS1", [128, 512], F32, 2, psum=True)
            PS2 = P.rot("PS2", [128, 512], F32, 2, psum=True)
            PTr = P.rot("PT", [128, 512], BF16, 2, psum=True)
            POr = P.rot("PO", [128, 64], F32, 1, psum=True)
            PTo = P.ps("PTo", [128, 512], BF16)
            Pbr = P.rot("Pb", [128, 512], BF16, 3)
            PTsr = P.rot("PTs", [128, 512], BF16, 3)
            mxr = P.rot("mx", [128, 8], F32, 2)
            rsr = P.rot("rs", [128, 8], F32, 2)
            smr = P.rot("sm", [128, 4], F32, 2)
            Osr = P.rot("Os", [128, 512], F32, 2)
            Obr = P.rot("Obf", [128, 512], BF16, 2)
            OTr = P.rot("OT", [128, 4, 128], BF16, 2)

            def scores(ps, Qn, Qr, h, qi, ch):
                k0 = ch * 4
                k1 = min(k0 + 4, qi + 1)
                w = (k1 - k0) * 128
                diag = (k1 == qi + 1)
                P.mm(ps[:, 0:w], Qn[:, h, :], Kn[:, h, k0 * 128:k0 * 128 + w], True, False, [Qn, Kn], [ps])
                P.mm(ps[:, 0:w], Qr[:, h, :], Kr[:, k0 * 128:k0 * 128 + w], False, not diag, [Qr, Kr], [ps])
                if diag:
                    P.mm(ps[:, w - 128:w], identb[:], cmaskb[:], False, True, [identb, cmaskb], [ps])
                return w, k0, k1

            for qi in range(NT):
                Qn = Qnr.next()
                Qr = Qrr.next()
                qs = slice(qi * 128, (qi + 1) * 128)
                P.dma(LD, Qn[:], qnD[:, qs].rearrange("(h c) t -> c h t", c=64), writes=[Qn])
                P.dma(LD, Qr[:], qrD[:, qs].rearrange("(h c) t -> c h t", c=32), writes=[Qr])
                nch = (qi + 4) // 4
                Os = Osr.next()
                for h in range(8):
                    mx = mxr.next()
                    rs = rsr.next()
                    sm = smr.next()
                    for ch in range(nch):
                        ps = PS1.next()
                        w, k0, k1 = scores(ps, Qn, Qr, h, qi, ch)
                        P.op("dve", lambda e, ps=ps, mx=mx, ch=ch, w=w: e.reduce_max(out=mx[:, ch:ch + 1], in_=ps[:, 0:w], axis=AX.X),
                             [ps], [mx])
                    P.op("dve", lambda e, mx=mx, sm=sm, nch=nch: e.reduce_max(out=sm[:, 0:1], in_=mx[:, 0:nch], axis=AX.X), [mx], [sm])
                    P.ts("dve", sm[:, 1:2], sm[:, 0:1], -SCALE, None, ALU.mult, None, [sm], [sm])
                    PO = POr.next()
                    for ch in range(nch):
                        ps = PS2.next()
                        w, k0, k1 = scores(ps, Qn, Qr, h, qi, ch)
                        Pb = Pbr.next()
                        P.act(Pb[:, 0:w], ps[:, 0:w], AF.Exp, [ps, sm], [Pb, rs], bias=sm[:, 1:2], scale=SCALE,
                              accum_out=rs[:, ch:ch + 1])
                        PT = PTr.next()
                        for j in range(k1 - k0):
                            P.tr(PT[:, j * 128:(j + 1) * 128], Pb[:, j * 128:(j + 1) * 128], identb[:], [Pb, identb], [PT], sig=(j == k1 - k0 - 1))
                        PTs = PTsr.next()
                        if ch % 2 == 0:
                            P.cp("dve", PTs[:, 0:w], PT[:, 0:w], [PT], [PTs])
                        else:
                            P.cp("act", PTs[:, 0:w], PT[:, 0:w], [PT], [PTs])
                        for j in range(k1 - k0):
                            kt = k0 + j
                            P.mm(PO[:, :], PTs[:, j * 128:(j + 1) * 128], V[:, kt, h * 64:(h + 1) * 64], kt == 0, kt == qi,
                                 [PTs, V], [PO])
                    P.op("dve", lambda e, rs=rs, sm=sm, nch=nch: e.reduce_sum(out=sm[:, 2:3], in_=rs[:, 0:nch], axis=AX.X), [rs], [sm])
                    P.recip(sm[:, 3:4], sm[:, 2:3], [sm], [sm])
                    P.ts("dve", Os[:, h * 64:(h + 1) * 64], PO[:, :], sm[:, 3:4], None, ALU.mult, None, [PO, sm], [Os])
                Ob = Obr.next()
                P.cp("act", Ob[:], Os[:], [Os], [Ob])
                for c in range(4):
                    P.tr(PTo[:, c * 128:(c + 1) * 128], Ob[:, c * 128:(c + 1) * 128], identb[:], [Ob, identb], [PTo], sig=(c == 3))
                OT = OTr.next()
                P.cp("dve", OT[:], PTo[:].rearrange("p (c t) -> p c t", t=128), [PTo], [OT])
                P.dma(ST, oTD[:, qs].rearrange("(c e) t -> e c t", e=128), OT[:], reads=[OT])

        P.phase(phase_attn)
        if done("attn_%d" % l):
            break

        def phase_out():
            K = load_consts(["bavg"])
            bavg = K["bavg"]
            cols = P.sb("cols", [128, NCOL])
            P.dma(LD, cols[:], cols_in[l, :, :], writes=[cols])
            woa = P.sb("woa", [128, 4, D], BF16)
            wob = P.sb("wob", [128, 4, D], BF16)
            wout = P.sb("wout", [128, 8, D], BF16)
            for c in range(4):
                P.dma(LD, woa[:, c, :], wb["woa"][l, c * 128:(c + 1) * 128, :], writes=[woa])
                P.dma(LD, wob[:, c, :], wb["wob"][l, c * 128:(c + 1) * 128, :], writes=[wob])
            for c in range(8):
                P.dma(LD, wout[:, c, :], wb["wout"][l, c * 128:(c + 1) * 128, :], writes=[wout])
            psr = P.rot("ps", [128, 512], F32, 6, psum=True)
            ldr = P.rot("ldf", [128, 512], F32, 6)
            tmpr = P.rot("tmp", [128, 512], F32, 6)
            roTr = P.rot("roT", [128, 4, 512], BF16, 2)
            oTr = P.rot("oTb", [128, 4, 512], BF16, 2)
            mgr = P.rot("mg", [128, 8, 512], BF16, 2)
            htr = P.rot("ht", [128, D], F32, 2)
            hnr = P.rot("hn", [128, D], F32, 2)
            for b in range(NB):
                bs0 = slice(b * 512, (b + 1) * 512)
                bs = slice(1 + b * 512, 1 + (b + 1) * 512)
                roT = roTr.next()
                for c in range(4):
                    cs = slice(c * 128, (c + 1) * 128)
                    y = ldr.next()
                    bo = ldr.next()
                    gt = ldr.next()
                    P.dma(LD, y[:], yTD[cs, bs0], writes=[y])
                    P.dma(LD, bo[:], bonD[cs, bs0], writes=[bo])
                    P.dma(LD, gt[:], gateD[cs, bs0], writes=[gt])
                    ps = psr.next()
                    P.mm(ps[:], bavg[:], y[:], True, True, [bavg, y], [ps])
                    cen = tmpr.next()
                    P.tt("dve", cen[:], y[:], ps[:], ALU.subtract, [y, ps], [cen])
                    sq = tmpr.next()
                    P.act(sq[:], cen[:], AF.Square, [cen], [sq])
                    ps = psr.next()
                    P.mm(ps[:], bavg[:], sq[:], True, True, [bavg, sq], [ps])
                    rs = tmpr.next()
                    P.rsqrt(rs[:], ps[:], [ps], [rs], scale=1.0, bias=LNX_EPS)
                    P.tt("dve", cen[:], cen[:], rs[:], ALU.mult, [cen, rs], [cen])
                    P.ts("dve", cen[:], cen[:], cols[:, C_LW + c:C_LW + c + 1], cols[:, C_LB + c:C_LB + c + 1], ALU.mult, ALU.add,
                         [cen, cols], [cen])
                    P.tt("pool", cen[:], cen[:], bo[:], ALU.add, [cen, bo], [cen])
                    P.tt("dve", roT[:, c, :], cen[:], gt[:], ALU.mult, [cen, gt], [roT])
                oTb = oTr.next()
                for c in range(4):
                    P.dma(LD, oTb[:, c, :], oTD[c * 128:(c + 1) * 128, bs0], writes=[oTb])
                mg = mgr.next()
                for dc in range(8):
                    ds_ = slice(dc * 128, (dc + 1) * 128)
                    sga = ldr.next()
                    sgb = ldr.next()
                    P.dma(LD, sga[:], pT[R_GA + dc * 128:R_GA + (dc + 1) * 128, bs], writes=[sga])
                    P.dma(LD, sgb[:], pT[R_GB + dc * 128:R_GB + (dc + 1) * 128, bs], writes=[sgb])
                    pa = psr.next()
                    pb = psr.next()
                    for c in range(4):
                        P.mm(pa[:], woa[:, c, ds_], roT[:, c, :], c == 0, c == 3, [woa, roT], [pa])
                    for c in range(4):
                        P.mm(pb[:], wob[:, c, ds_], oTb[:, c, :], c == 0, c == 3, [wob, oTb], [pb])
                    m1 = tmpr.next()
                    m2 = tmpr.next()
                    P.tt("dve", m1[:], pa[:], sga[:], ALU.mult, [pa, sga], [m1])
                    P.tt("dve", m2[:], pb[:], sgb[:], ALU.mult, [pb, sgb], [m2])
                    P.tt("pool", mg[:, dc, :], m1[:], m2[:], ALU.add, [m1, m2], [mg])
                for i in range(4):
                    t0 = b * 512 + i * 128
                    ht = htr.next()
                    hn = hnr.next()
                    P.dma(LD, ht[:], h_src[t0:t0 + 128, :], writes=[ht])
                    for half in range(2):
                        pd = psr.next()
                        for dc in range(8):
                            P.mm(pd[:], mg[:, dc, i * 128:(i + 1) * 128], wout[:, dc, half * 512:(half + 1) * 512], dc == 0, dc == 7,
                                 [mg, wout], [pd])
                        P.tt("dve", hn[:, half * 512:(half + 1) * 512], ht[:, half * 512:(half + 1) * 512], pd[:], ALU.add,
                             [ht, pd], [hn])
                    P.dma(ST, hD[t0:t0 + 128, :], hn[:], reads=[hn])

        P.phase(phase_out)
        if done("out_%d" % l):
            break

        def phase_ffn():
            lastl = is_last
            K = load_consts(["ident_b"])
            identb = K["ident_b"]
            cols = P.sb("cols", [128, NCOL])
            P.dma(LD, cols[:], cols_in[l, :, :], writes=[cols])
            wdn = P.sb("wdn", [128, 22, D], BF16)
            for f in range(22):
                P.dma(LD, wdn[:, f, :], wb["wdn"][l, f * 128:(f + 1) * 128, :], writes=[wdn])
            fb = P.sb("fb", [128, D])
            P.dma(LD, fb[:], rows_in[2 * l + 1, :].partition_broadcast(128), writes=[fb])
            fnb = None
            if lastl:
                fnb = P.sb("fnb", [128, D])
                P.dma(LD, fnb[:], rows_in[2 * depth, :].partition_broadcast(128), writes=[fnb])
            carry = P.sb("carry", [128, 22, 2])
            P.memset("dve", carry[:], 0.0, [carry])
            norm_alloc()
            hts = [P.sb("htk%d" % i, [128, D]) for i in range(4)]
            nTr = P.rot("nT", [128, 8, 512], BF16, 1)
            hmr = P.rot("hm", [128, 22, 512], BF16, 1)
            wgr = P.rot("wg", [128, 8, 128], BF16, 3)
            wvr = P.rot("wv", [128, 8, 128], BF16, 3)
            pgr = P.rot("pg", [128, 512], F32, 2, psum=True)
            pvr = P.rot("pv", [128, 512], F32, 2, psum=True)
            pdr = P.rot("pd", [128, 512], F32, 2, psum=True)
            ugr = P.rot("ug", [128, 514], F32, 2)
            cvr = P.rot("cv", [128, 512], F32, 2)
            hnr = P.rot("hn", [128, D], F32, 2)
            sqr = P.rot("fsq", [128, 2], F32, 2)
            for b in range(NB):
                nT = nTr.next()
                norm_transpose(lambda t: hD[t * 128:(t + 1) * 128, :], fb, identb, nT, b, ht_keep=hts)
                hm = hmr.next()
                for f in range(22):
                    wg = wgr.next()
                    wv = wvr.next()
                    P.dma(LD, wg[:], wb["wup"][l, :, f * 128:(f + 1) * 128].rearrange("(k p) n -> p k n", p=128), writes=[wg])
                    P.dma(LD, wv[:], wb["wup"][l, :, 2816 + f * 128:2816 + (f + 1) * 128].rearrange("(k p) n -> p k n", p=128),
                          writes=[wv])
                    pg = pgr.next()
                    pv = pvr.next()
                    for kc in range(8):
                        P.mm(pg[:], wg[:, kc, :], nT[:, kc, :], kc == 0, kc == 7, [wg, nT], [pg])
                    for kc in range(8):
                        P.mm(pv[:], wv[:, kc, :], nT[:, kc, :], kc == 0, kc == 7, [wv, nT], [pv])
                    ug = ugr.next()
                    P.cp("pool", ug[:, 0:2], carry[:, f, :], [carry], [ug])
                    P.cp("act", ug[:, 2:514], pg[:], [pg], [ug])
                    P.cp("pool", carry[:, f, :], ug[:, 512:514], [ug], [carry])
                    cv = cvr.next()
                    cw = lambda j: cols[:, C_CW + j * 22 + f:C_CW + j * 22 + f + 1]
                    P.ts("dve", cv[:], ug[:, 2:514], cw(2), cols[:, C_CB + f:C_CB + f + 1], ALU.mult, ALU.add, [ug, cols], [cv])
                    P.stt("dve", cv[:], ug[:, 1:513], cw(1), cv[:], ALU.mult, ALU.add, [ug, cols, cv], [cv])
                    P.stt("dve", cv[:], ug[:, 0:512], cw(0), cv[:], ALU.mult, ALU.add, [ug, cols, cv], [cv])
                    P.act(cv[:], cv[:], AF.Gelu, [cv], [cv])
                    P.tt("dve", hm[:, f, :], cv[:], pv[:], ALU.mult, [cv, pv], [hm])
                for i in range(4):
                    t0 = b * 512 + i * 128
                    hn = hnr.next()
                    for half in range(2):
                        pd = pdr.next()
                        for f in range(22):
                            P.mm(pd[:], hm[:, f, i * 128:(i + 1) * 128], wdn[:, f, half * 512:(half + 1) * 512], f == 0, f == 21,
                                 [hm, wdn], [pd])
                        P.tt("dve", hn[:, half * 512:(half + 1) * 512], hts[i][:, half * 512:(half + 1) * 512], pd[:], ALU.add,
                             [hts[i], pd], [hn])
                    if not lastl:
                        P.dma(ST, hD[t0:t0 + 128, :], hn[:], reads=[hn])
                    else:
                        junk = norm_transpose.junk.next()
                        sq = sqr.next()
                        P.act(junk[:], hn[:], AF.Square, [hn], [junk, sq], accum_out=sq[:, 0:1])
                        P.rsqrt(sq[:, 0:1], sq[:, 0:1], [sq], [sq], scale=1.0 / D, bias=EPS)
                        P.stt("dve", hn[:], hn[:], sq[:, 0:1], fnb[:], ALU.mult, ALU.mult, [hn, sq, fnb], [hn])
                        P.dma(ST, out_ap[t0:t0 + 128, :], hn[:], reads=[hn])

        P.phase(phase_ffn)
        if done("ffn_%d" % l):
            break

    P.root.close()
    return nc, P


def make_consts():
    c = np.zeros((128, NCST), np.float32)
    i = np.arange(128)
    c[:, K_ID:K_ID + 128] = np.eye(128)
    strict = (i[:, None] < i[None, :]).astype(np.float32)
    incl = (i[:, None] <= i[None, :]).astype(np.float32)
    c[:, K_MS:K_MS + 128] = strict
    c[:, K_M3:K_M3 + 128] = incl
    c[:, K_M3 + 128:K_M3 + 256] = strict
    c[:, K_M3 + 256:K_M3 + 384] = incl
    c[:, K_MST:K_MST + 128] = strict.T
    c[:, K_CM:K_CM + 128] = np.where(i[None, :] <= i[:, None], 0.0, NEG)
    blk = (i[:, None] // 64 == i[None, :] // 64).astype(np.float32)
    c[:, K_BO:K_BO + 128] = blk
    c[:, K_ON:K_ON + 128] = 1.0
    c[:, K_BA:K_BA + 128] = blk / 64.0
    rm = np.ones(512, np.float32)
    rm[::128] = 0.0
    c[:, K_RM:K_RM + 512] = rm[None, :]
    inv_freq = np.power(np.float32(10000.0), -np.arange(0, 32, 2, dtype=np.float32) / np.float32(32)).astype(np.float32)
    c[:, K_IF] = inv_freq[i % 16]
    c[:, K_SG] = np.where((i % 32) < 16, -1.0, 1.0)
    return c


def _colpack(v, n):
    v = np.asarray(v, np.float32).reshape(-1)
    out = np.zeros((128, n), np.float32)
    if v.size == 64:
        out[:64, 0] = v
    else:
        out[:, :] = v.reshape(n, 128).T
    return out


def prep_inputs(inp, depth=2):
    f = lambda k: np.asarray(inp[k], np.float32)
    cols = np.zeros((depth, 128, NCOL), np.float32)
    rows = np.zeros((2 * depth + 1, D), np.float32)
    w = {}
    w_in = f("w_in")
    kpe = w_in[:, :, 2176:2208]
    kpes = np.concatenate([kpe[:, :, 16:32], kpe[:, :, 0:16]], axis=-1)
    w["win"] = np.ascontiguousarray(np.concatenate([w_in[:, :, :2208], kpes, w_in[:, :, 2208:]], axis=-1))
    w["wdu"] = f("w_decay_up")
    w["wiu"] = f("w_iclr_up")
    w["wgu"] = f("w_gate_up")
    w["woa"] = f("w_out_rwkv")
    wq = f("w_q_up").reshape(depth, 256, 8, 96)
    nope = wq[..., :64].reshape(depth, 256, 512)
    pe = wq[..., 64:].reshape(depth, 256, 256)
    pes = np.concatenate([wq[..., 80:96], wq[..., 64:80]], axis=-1).reshape(depth, 256, 256)
    w["wqu"] = np.ascontiguousarray(np.concatenate([nope, pe, pes], axis=-1))
    wkv = f("w_kv_up").reshape(depth, 128, 8, 128)
    w["wkvu"] = np.ascontiguousarray(np.concatenate([wkv[..., :64].reshape(depth, 128, 512),
                                                     wkv[..., 64:].reshape(depth, 128, 512)], axis=-1))
    w["wob"] = f("w_out_mla")
    w["wout"] = f("w_out")
    w["wup"] = f("w_ffn_up")
    w["wdn"] = f("w_ffn_down")
    for l in range(depth):
        mu = f("mu_shift")[l]
        cols[l, :, C_MU:C_MU + 12] = _colpack(mu[:1536], 12)
        cols[l, :, C_MU + 12:C_MU + 13] = _colpack(mu[1536:1600], 1)
        cols[l, :, C_MU + 13:C_MU + 14] = _colpack(mu[1600:1664], 1)
        cols[l, :, C_MU + 14:C_MU + 15] = _colpack(mu[1664:1792], 1)
        cols[l, :, C_DB:C_DB + 4] = _colpack(f("decay_base")[l], 4)
        cols[l, :, C_IB:C_IB + 4] = _colpack(f("iclr_base")[l], 4)
        cols[l, :, C_KK:C_KK + 4] = _colpack(f("k_k")[l], 4)
        cols[l, :, C_KA:C_KA + 4] = _colpack(f("k_a")[l], 4)
        cols[l, :, C_RK:C_RK + 4] = _colpack(f("r_k")[l], 4)
        cols[l, :, C_LW:C_LW + 4] = _colpack(f("lnx_w")[l], 4)
        cols[l, :, C_LB:C_LB + 4] = _colpack(f("lnx_b")[l], 4)
        cols[l, :, C_QN:C_QN + 2] = _colpack(f("q_norm")[l], 2)
        cols[l, :, C_KVN:C_KVN + 1] = _colpack(f("kv_norm")[l], 1)
        for j in range(3):
            cols[l, :, C_CW + j * 22:C_CW + (j + 1) * 22] = _colpack(f("conv_w")[l, j], 22)
        cols[l, :, C_CB:C_CB + 22] = _colpack(f("conv_b")[l], 22)
        rows[2 * l] = f("attn_norm")[l]
        rows[2 * l + 1] = f("ffn_norm")[l]
    rows[2 * depth] = f("final_norm")
    shared = {"cst": make_consts(), "cols": cols, "rows": rows}
    shared.update(w)
    return shared


_CACHE = {}


def kernel(**inputs):
    x = np.asarray(inputs["x"], np.float32)
    pos = np.asarray(inputs["positions"], np.int32)
    B, T, _ = x.shape
    shared = prep_inputs(inputs)
    key = ("nc", T)
    if key not in _CACHE:
        _CACHE[key] = build(T)[0]
    nc = _CACHE[key]
    in_maps = []
    for core in range(8):
        b = core // 2
        m = dict(shared)
        m["x"] = np.ascontiguousarray(x[b])
        m["pos"] = np.ascontiguousarray(pos[b])
        in_maps.append(m)
    res = run_bass_kernel_spmd(nc, in_maps, core_ids=list(range(8)))
    out = np.empty((B, T, D), np.float32)
    half = T // 2
    for b in range(B):
        out[b, :half] = res.results[2 * b]["out"][:half]
        out[b, half:] = res.results[2 * b + 1]["out"][half:]
    return out
```

```python
import math
from contextlib import ExitStack

import numpy as np
import concourse.bass as bass
import concourse.mybir as mybir
from concourse.bass_utils import run_bass_kernel_spmd

F32 = mybir.dt.float32
BF16 = mybir.dt.bfloat16
I32 = mybir.dt.int32
AF = mybir.ActivationFunctionType
ALU = mybir.AluOpType
AX = mybir.AxisListType

ENGS = ("pe", "act", "dve", "pool", "sp")

D = 1024
NCOL = 134
NCST = 1794
WIN_COLS = 4288
EPS = 1e-6
LNX_EPS = 64e-5
SCALE = 96.0 ** -0.5
NEG = -30000.0
R_R, R_K, R_V, R_W, R_A, R_G = 0, 512, 1024, 1536, 1600, 1664
R_QL, R_KVL, R_KPE, R_KPES, R_GA, R_GB = 1792, 2048, 2176, 2208, 2240, 3264
C_MU, C_DB, C_IB, C_KK, C_KA, C_RK, C_LW, C_LB, C_QN, C_KVN, C_CW, C_CB = 0, 15, 19, 23, 27, 31, 35, 39, 43, 45, 46, 112
K_ID, K_MS, K_M3, K_MST, K_CM, K_BO, K_ON, K_BA, K_RM, K_IF, K_SG = 0, 128, 256, 640, 768, 896, 1024, 1152, 1280, 1792, 1793


class Buf:
    __slots__ = ("name", "w", "r", "ap")

    def __init__(self, name, ap=None):
        self.name = name
        self.w = None
        self.r = []
        self.ap = ap

    def __getitem__(self, k):
        return self.ap[k]


class Ev:
    __slots__ = ("key", "val", "clock")

    def __init__(self, key, val, clock):
        self.key = key
        self.val = val
        self.clock = clock


class Rot:
    def __init__(self, bufs):
        self.bufs = bufs
        self.i = 0

    def next(self):
        b = self.bufs[self.i % len(self.bufs)]
        self.i += 1
        return b


class Prog:
    def __init__(self, nc, ndma=(("sp", 24), ("pool", 24), ("act", 8))):
        self.nc = nc
        self.root = ExitStack()
        self.esem = {}
        for e in ENGS:
            self.esem[e] = self.root.enter_context(nc.semaphore("es_" + e))
        self.dsem = {}
        self.dnext = {}
        self.dlast = {}
        for q, n in ndma:
            self.dsem[q] = [self.root.enter_context(nc.semaphore("ds_%s%d" % (q, i))) for i in range(n)]
            self.dnext[q] = 0
            self.dlast[q] = [None] * n
        self.cnt = {e: 0 for e in ENGS}
        self.clock = {e: {} for e in ENGS}
        self.ops = {e: [] for e in ENGS}
        self.pending_dma = []
        self.es = None
        self.uid = 0
        self.ninst = 0
        self.cache = {}
        self.nep = 0

    def sb(self, name, shape, dt=F32):
        self.uid += 1
        t = self.es.enter_context(self.nc.sbuf_tensor("%s_%d" % (name, self.uid), list(shape), dt))
        return Buf(name, t)

    def ps(self, name, shape, dt=F32):
        self.uid += 1
        t = self.es.enter_context(self.nc.psum_tensor("%s_%d" % (name, self.uid), list(shape), dt))
        return Buf(name, t)

    def sbc(self, name, shape, dt=F32, n=2):
        if name not in self.cache:
            self.cache[name] = self.rot(name, shape, dt, n)
        return self.cache[name].next()

    def rot(self, name, shape, dt, n, psum=False):
        return Rot([(self.ps if psum else self.sb)(name, shape, dt) for _ in range(n)])

    def _deps(self, e, reads, writes):
        deps = []
        for b in reads:
            if b.w is not None:
                deps.append(b.w)
        for b in writes:
            if b.w is not None:
                deps.append(b.w)
            deps.extend(b.r)
        ck = self.clock[e]
        best = {}
        for ev in deps:
            if ev.key == "pe" and e == "pe":
                continue
            if ck.get(ev.key, 0) >= ev.val:
                continue
            best[ev.key] = max(best.get(ev.key, 0), ev.val)
            for k, v in ev.clock.items():
                if ck.get(k, 0) < v:
                    ck[k] = v
            ck[ev.key] = ev.val
        return list(best.items())

    def _commit(self, ev, reads, writes):
        for b in reads:
            b.r.append(ev)
        for b in writes:
            b.w = ev
            b.r = []

    def op(self, e, fn, reads=(), writes=(), sig=True):
        waits = self._deps(e, reads, writes)
        if sig:
            self.cnt[e] += 1
            ev = Ev(e, self.cnt[e], dict(self.clock[e]))
            self.ops[e].append((waits, fn, ("e", e)))
        else:
            ev = Ev(e, self.cnt[e] + 1, dict(self.clock[e]))
            self.ops[e].append((waits, fn, ("n",)))
        self._commit(ev, reads, writes)
        return ev

    def dma(self, q, out, in_, reads=(), writes=(), slow=False):
        waits = self._deps(q, reads, writes)
        i = self.dnext[q]
        n = len(self.dsem[q])
        self.dnext[q] = (i + 1) % n
        prev = self.dlast[q][i]
        key = ("d", q, i)
        if prev is not None and self.clock[q].get(key, 0) < prev:
            waits.append((key, prev))
            self.clock[q][key] = prev
        val = (prev or 0) + 16
        self.dlast[q][i] = val
        ev = Ev(key, val, dict(self.clock[q]))
        self._commit(ev, reads, writes)
        if slow:
            self.ops[q].append((waits, lambda eng: eng.dma_start(out=out, in_=in_, allow_slow_non_contiguous=True), ("d", q, i)))
        else:
            self.ops[q].append((waits, lambda eng: eng.dma_start(out=out, in_=in_), ("d", q, i)))
        self.pending_dma.append(ev)
        return ev

    def _sem(self, key, esems=None):
        if isinstance(key, tuple):
            return self.dsem[key[1]][key[2]]
        return (esems or self.esem)[key]

    def emit(self):
        nc = self.nc
        best = {}
        for ev in self.pending_dma:
            best[ev.key] = max(best.get(ev.key, 0), ev.val)
        final_waits = list(best.items())
        for e in ("pe", "act", "dve", "pool"):
            if self.cnt[e] > 0:
                final_waits.append((e, self.cnt[e]))
        self.pending_dma = []
        ops = self.ops
        self.ops = {e: [] for e in ENGS}
        engmap = {"pe": "tensor", "act": "scalar", "dve": "vector", "pool": "gpsimd", "sp": "sync"}
        with nc.Block() as block:
            for e in ENGS:
                lst = ops[e]
                if e == "sp":
                    lst = lst + [(final_waits, None, None)]
                if not lst:
                    continue
                self.ninst += len(lst)

                def body(eng, lst=lst, e=e, esem_e=self.esem[e], esems=dict(self.esem)):
                    for waits, fn, kind in lst:
                        for k, v in waits:
                            eng.wait_ge(self._sem(k, esems), v)
                        if fn is None:
                            continue
                        ins = fn(eng)
                        if kind[0] == "e":
                            ins.then_inc(esem_e, 1)
                        elif kind[0] == "d":
                            ins.then_inc(self.dsem[kind[1]][kind[2]], 16)

                getattr(block, engmap[e])(body)
        full = {}
        for e in ENGS:
            full[e] = self.cnt[e]
        for q in self.dsem:
            for i, v in enumerate(self.dlast[q]):
                if v:
                    full[("d", q, i)] = v
        for e in ENGS:
            self.clock[e] = dict(full)

    def new_epoch(self):
        for e in ENGS:
            self.esem[e] = self.root.enter_context(self.nc.semaphore("es%d_%s" % (self.nep, e)))
            self.cnt[e] = 0
        self.nep += 1
        for e in ENGS:
            ck = {k: v for k, v in self.clock[e].items() if isinstance(k, tuple)}
            self.clock[e] = ck

    def phase(self, body):
        with ExitStack() as es:
            self.es = es
            self.cache = {}
            body()
            self.emit()
        self.es = None

    def act(self, out, in_, func, r, w, **kw):
        return self.op("act", lambda e: e.activation(out=out, in_=in_, func=func, **kw), r, w)

    def cp(self, eng, out, in_, r, w):
        if eng == "act":
            return self.op("act", lambda e: e.copy(out=out, in_=in_), r, w)
        return self.op(eng, lambda e: e.tensor_copy(out=out, in_=in_), r, w)

    def tt(self, eng, out, a, b, op, r, w):
        return self.op(eng, lambda e: e.tensor_tensor(out=out, in0=a, in1=b, op=op), r, w)

    def ts(self, eng, out, a, s1, s2, op0, op1, r, w):
        if s2 is None:
            return self.op(eng, lambda e: e.tensor_scalar(out=out, in0=a, scalar1=s1, scalar2=None, op0=op0), r, w)
        return self.op(eng, lambda e: e.tensor_scalar(out=out, in0=a, scalar1=s1, scalar2=s2, op0=op0, op1=op1), r, w)

    def stt(self, eng, out, a, s, b, op0, op1, r, w):
        return self.op(eng, lambda e: e.scalar_tensor_tensor(out=out, in0=a, scalar=s, in1=b, op0=op0, op1=op1), r, w)

    def mm(self, out, lhsT, rhs, start, stop, r, w):
        return self.op("pe", lambda e: e.matmul(out, lhsT=lhsT, rhs=rhs, start=start, stop=stop), r, w, sig=bool(stop))

    def tr(self, out, in_, ident, r, w, sig=True):
        return self.op("pe", lambda e: e.transpose(out=out, in_=in_, identity=ident), r, w, sig=sig)

    def memset(self, eng, ap, val, w):
        return self.op(eng, lambda e: e.memset(ap, val), [], w)

    def recip(self, out, in_, r, w):
        return self.op("dve", lambda e: e.reciprocal(out=out, in_=in_), r, w)

    def rsqrt(self, out, in_, r, w, scale=1.0, bias=0.0):
        self.act(out, in_, AF.Sqrt, r, w, bias=bias, scale=scale)
        self.recip(out, out, w, w)


def build(T, depth=2, dbg=(), stop=None, reps=1, pad_mb=0):
    nc = bass.Bass("TRN2", target_bir_lowering=False)
    NT = T // 128
    NB = T // 512
    assert T % 512 == 0

    def din(name, shape, dt=F32):
        return nc.dram_tensor(name, list(shape), dt, kind="ExternalInput").ap()

    def dscr(name, shape, dt=F32):
        kind = "ExternalOutput" if name in dbg else "Internal"
        return nc.dram_tensor(name, list(shape), dt, kind=kind).ap()

    x_in = din("x", [T, D])
    pos_in = din("pos", [T], I32)
    cst_in = din("cst", [128, NCST])
    cols_in = din("cols", [depth, 128, NCOL])
    rows_in = din("rows", [2 * depth + 1, D])
    wspec = [("win", D, WIN_COLS), ("wdu", 64, 512), ("wiu", 64, 512), ("wgu", 128, 512), ("woa", 512, D),
             ("wqu", 256, 1024), ("wkvu", 128, 1024), ("wob", 512, D), ("wout", D, D), ("wup", D, 5632),
             ("wdn", 2816, D)]
    wf = {}
    wb = {}
    for n, r, c in wspec:
        wf[n] = din(n, [depth, r, c])
        wb[n] = dscr(n + "_b", [depth, r, c], BF16)
    out_ap = nc.dram_tensor("out", [T, D], F32, kind="ExternalOutput").ap()

    hD = dscr("h", [T, D])
    pT = dscr("pT", [WIN_COLS, T + 1])
    CCd = dscr("CC", [128, T])
    SSd = dscr("SS", [128, T])
    rtD = dscr("rt", [512, T], BF16)
    atD = dscr("at", [512, T], BF16)
    btD = dscr("bt", [512, T], BF16)
    ktD = dscr("kt", [512, T], BF16)
    BhD = dscr("Bh", [T, 512], BF16)
    KhD = dscr("Kh", [T, 512], BF16)
    VtD = dscr("Vt", [T, 512], BF16)
    gCD = dscr("gC", [512, NT])
    bonD = dscr("bon", [512, T])
    gateD = dscr("gate", [512, T])
    yTD = dscr("yT", [512, T])
    qnD = dscr("qn", [512, T], BF16)
    qrD = dscr("qr", [256, T], BF16)
    knD = dscr("kn", [512, T], BF16)
    krD = dscr("kr", [32, T], BF16)
    vtokD = dscr("vtok", [T, 512], BF16)
    oTD = dscr("oT", [512, T], BF16)

    P = Prog(nc)
    padD = dscr("padD", [pad_mb * 2048, 128]) if pad_mb else None
    LD = "sp"
    ST = "pool"

    def done(name):
        return stop == name

    def phase_prep():
        stg = P.rot("stg", [128, 2048], F32, 3)
        ob = P.rot("ob", [128, 2048], BF16, 3)
        engs = ["dve", "act", "pool"]
        k = 0
        for l in range(depth):
            for n, r, c in wspec:
                for r0 in range(0, r, 128):
                    rr = min(128, r - r0)
                    for c0 in range(0, c, 2048):
                        cc = min(2048, c - c0)
                        s = stg.next()
                        o = ob.next()
                        P.dma(LD, s[0:rr, 0:cc], wf[n][l, r0:r0 + rr, c0:c0 + cc], writes=[s])
                        P.cp(engs[k % 3], o[0:rr, 0:cc], s[0:rr, 0:cc], [s], [o])
                        k += 1
                        P.dma(ST, wb[n][l, r0:r0 + rr, c0:c0 + cc], o[0:rr, 0:cc], reads=[o])
        if padD is not None:
            zt = P.sb("zt", [128, 128])
            P.memset("dve", zt[:], 0.0, [zt])
            P.dma(ST, padD[pad_mb * 2048 - 128:pad_mb * 2048, :], zt[:], reads=[zt])
        cst = P.sb("cst", [128, 2])
        P.dma(LD, cst[:], cst_in[:, K_IF:K_IF + 2], writes=[cst])
        C1 = 6.28125
        C2 = 2 * math.pi - C1
        for c0 in range(0, T, 512):
            pi_ = P.sbc("pi", [128, 512], I32)
            pf = P.sbc("pf", [128, 512])
            P.dma(LD, pi_[:], pos_in[c0:c0 + 512].partition_broadcast(128), writes=[pi_])
            P.cp("dve", pf[:], pi_[:], [pi_], [pf])
            ang = P.sbc("ang", [128, 512])
            P.ts("dve", ang[:], pf[:], cst[:, 0:1], None, ALU.mult, None, [pf, cst], [ang])
            for which, dst in ((0, SSd), (1, CCd)):
                a2 = P.sbc("a2", [128, 512])
                ki = P.sbc("ki", [128, 512], I32)
                kf = P.sbc("kf", [128, 512])
                m = P.sbc("m", [128, 512])
                w_ = P.sbc("w_", [128, 512])
                res = P.sbc("res", [128, 512])
                P.ts("dve", a2[:], ang[:], (math.pi / 2) if which else 0.0, None, ALU.add, None, [ang], [a2])
                P.ts("dve", ki[:], a2[:], 1.0 / (2 * math.pi), None, ALU.mult, None, [a2], [ki])
                P.cp("dve", kf[:], ki[:], [ki], [kf])
                P.stt("dve", m[:], kf[:], -C1, a2[:], ALU.mult, ALU.add, [kf, a2], [m])
                P.stt("dve", m[:], kf[:], -C2, m[:], ALU.mult, ALU.add, [kf, m], [m])
                P.ts("dve", w_[:], m[:], math.pi, -2 * math.pi, ALU.is_gt, ALU.mult, [m], [w_])
                P.tt("dve", m[:], m[:], w_[:], ALU.add, [m, w_], [m])
                P.ts("dve", w_[:], m[:], -math.pi, 2 * math.pi, ALU.is_lt, ALU.mult, [m], [w_])
                P.tt("dve", m[:], m[:], w_[:], ALU.add, [m, w_], [m])
                P.act(res[:], m[:], AF.Sin, [m], [res])
                if which == 0:
                    P.ts("dve", res[:], res[:], cst[:, 1:2], None, ALU.mult, None, [res, cst], [res])
                P.dma(ST, dst[:, c0:c0 + 512], res[:], reads=[res])

    P.phase(phase_prep)

    def load_consts(names):
        spec = {"ident": (K_ID, 128), "mstrict": (K_MS, 128), "mask3": (K_M3, 384), "mstrictT": (K_MST, 128),
                "cmask": (K_CM, 128), "bones": (K_BO, 128), "ones": (K_ON, 128), "bavg": (K_BA, 128),
                "rmask": (K_RM, 512)}
        out = {}
        for n in names:
            bf = n.endswith("_b")
            base = n[:-2] if bf else n
            o, w = spec[base]
            t = P.sb("k_" + base, [128, w])
            P.dma(LD, t[:], cst_in[:, o:o + w], writes=[t])
            if bf:
                tb = P.sb("kb_" + base, [128, w], BF16)
                P.cp("dve", tb[:], t[:], [t], [tb])
                out[n] = tb
            else:
                out[n] = t
        return out

    def norm_transpose(src_ap_fn, gb, identb, nT, b, ht_keep=None):
        for i in range(4):
            ht = ht_keep[i] if ht_keep is not None else norm_transpose.ht.next()
            P.dma(LD, ht[:], src_ap_fn(b * 4 + i), writes=[ht])
            junk = norm_transpose.junk.next()
            ssq = norm_transpose.ssq.next()
            P.act(junk[:], ht[:], AF.Square, [ht], [junk, ssq], accum_out=ssq[:, 0:1])
            P.rsqrt(ssq[:, 0:1], ssq[:, 0:1], [ssq], [ssq], scale=1.0 / D, bias=EPS)
            hs = norm_transpose.hs.next()
            P.stt("dve", hs[:], ht[:], ssq[:, 0:1], gb[:], ALU.mult, ALU.mult, [ht, ssq, gb], [hs])
            pt = norm_transpose.pt.next()
            for kc in range(8):
                P.tr(pt[:, kc * 128:(kc + 1) * 128], hs[:, kc * 128:(kc + 1) * 128], identb[:], [hs, identb], [pt], sig=(kc == 7))
            P.cp("act", nT[:, :, i * 128:(i + 1) * 128], pt[:].rearrange("p (k t) -> p k t", t=128), [pt], [nT])

    def norm_alloc():
        norm_transpose.ht = P.rot("ht", [128, D], F32, 2)
        norm_transpose.junk = P.rot("junk", [128, D], BF16, 1)
        norm_transpose.ssq = P.rot("ssq", [128, 1], F32, 2)
        norm_transpose.hs = P.rot("hs", [128, D], BF16, 2)
        norm_transpose.pt = P.rot("pt", [128, D], BF16, 2, psum=True)

    for lidx in range(depth * reps):
        l = lidx % depth
        h_src = x_in if lidx == 0 else hD
        is_last = (lidx == depth * reps - 1)
        P.new_epoch()

        def phase_a1():
            K = load_consts(["ident_b"])
            Wb = P.sb("Wb", [128, 8, WIN_COLS], BF16)
            for kc in range(8):
                P.dma(LD, Wb[:, kc, :], wb["win"][l, kc * 128:(kc + 1) * 128, :], writes=[Wb])
            gb = P.sb("gb", [128, D])
            P.dma(LD, gb[:], rows_in[2 * l, :].partition_broadcast(128), writes=[gb])
            zc = P.sb("zc", [128, 1])
            P.memset("dve", zc[:], 0.0, [zc])
            for r0 in range(0, 1792, 128):
                P.dma(ST, pT[r0:r0 + 128, 0:1], zc[:], reads=[zc], slow=True)
            norm_alloc()
            nTr = P.rot("nT", [128, 8, 512], BF16, 2)
            psr = P.rot("ps", [128, 512], F32, 4, psum=True)
            stg = P.rot("stg", [128, 512], F32, 4)
            chunks = []
            for r0 in range(0, 1536, 128):
                chunks.append((r0, 128, 0))
            chunks += [(R_W, 64, 0), (R_A, 64, 0), (R_G, 128, 0), (R_QL, 128, 0), (R_QL + 128, 128, 0),
                       (R_KVL, 128, 0), (R_KPE, 32, 0), (R_KPES, 32, 0)]
            for r0 in range(R_GA, WIN_COLS, 128):
                chunks.append((r0, 128, 1))
            k = 0
            for b in range(NB):
                nT = nTr.next()
                norm_transpose(lambda t: h_src[t * 128:(t + 1) * 128, :], gb, K["ident_b"], nT, b)
                for r0, M, sig in chunks:
                    ps = psr.next()
                    for kc in range(8):
                        P.mm(ps[0:M, :], Wb[:, kc, r0:r0 + M], nT[:, kc, :], kc == 0, kc == 7, [Wb, nT], [ps])
                    s = stg.next()
                    if sig:
                        P.act(s[0:M, :], ps[0:M, :], AF.Sigmoid, [ps], [s])
                    elif k % 2 == 0:
                        P.cp("dve", s[0:M, :], ps[0:M, :], [ps], [s])
                    else:
                        P.cp("act", s[0:M, :], ps[0:M, :], [ps], [s])
                    k += 1
                    P.dma(ST, pT[r0:r0 + M, 1 + b * 512:1 + (b + 1) * 512], s[0:M, :], reads=[s])

        P.phase(phase_a1)
        if done("a1_%d" % l):
            break

        def phase_a2():
            K = load_consts(["ident_b", "bones", "rmask"])
            identb = K["ident_b"]
            cols = P.sb("cols", [128, NCOL])
            P.dma(LD, cols[:], cols_in[l, :, :], writes=[cols])
            omk = P.sb("omk", [128, 4])
            P.ts("dve", omk[:], cols[:, C_KA:C_KA + 4], -1.0, 1.0, ALU.mult, ALU.add, [cols], [omk])
            wdu = P.sb("wdu", [64, 512], BF16)
            wiu = P.sb("wiu", [64, 512], BF16)
            wgu = P.sb("wgu", [128, 512], BF16)
            P.dma(LD, wdu[:], wb["wdu"][l, :, :], writes=[wdu])
            P.dma(LD, wiu[:], wb["wiu"][l, :, :], writes=[wiu])
            P.dma(LD, wgu[:], wb["wgu"][l, :, :], writes=[wgu])
            shr = P.rot("sh", [128, 513], F32, 4)
            dtmp = P.rot("dtmp", [128, 512], F32, 2)
            psr = P.rot("ps", [128, 512], F32, 4, psum=True)
            ptr = P.rot("ptT", [128, 512], BF16, 2, psum=True)
            stb = P.rot("stb", [128, 512], BF16, 6)
            stf = P.rot("stf", [128, 512], F32, 4)
            sttok = P.rot("sttok", [128, 4, 128], BF16, 3)

            def shift_mix(row0, M, mucol, b, out):
                sh = shr.next()
                P.dma(LD, sh[0:M, :], pT[row0:row0 + M, b * 512:b * 512 + 513], writes=[sh])
                d = dtmp.next()
                P.tt("dve", d[0:M, :], sh[0:M, 0:512], sh[0:M, 1:513], ALU.subtract, [sh], [d])
                P.stt("dve", out[0:M, :], d[0:M, :], cols[0:M, mucol:mucol + 1], sh[0:M, 1:513], ALU.mult, ALU.add,
                      [d, cols, sh], [out])

            def store_feat(dst, c, b, src):
                P.dma(ST, dst[c * 128:(c + 1) * 128, b * 512:(b + 1) * 512], src[:], reads=[src])

            def store_tok(dst, c, b, srcb):
                pt = ptr.next()
                for i in range(4):
                    P.tr(pt[:, i * 128:(i + 1) * 128], srcb[:, i * 128:(i + 1) * 128], identb[:], [srcb, identb], [pt], sig=(i == 3))
                s = sttok.next()
                P.cp("act", s[:], pt[:].rearrange("p (i c) -> p i c", c=128), [pt], [s])
                P.dma(ST, dst[b * 512:(b + 1) * 512, c * 128:(c + 1) * 128].rearrange("(i p) c -> p i c", p=128), s[:],
                      reads=[s])

            for b in range(NB):
                mw = P.sbc("mw", [64, 512])
                ma = P.sbc("ma", [64, 512])
                mg = P.sbc("mg", [128, 512])
                shift_mix(R_W, 64, C_MU + 12, b, mw)
                shift_mix(R_A, 64, C_MU + 13, b, ma)
                shift_mix(R_G, 128, C_MU + 14, b, mg)
                tw = P.sbc("tw", [64, 512], BF16)
                pab = P.sbc("pab", [64, 512], BF16)
                sgg = P.sbc("sgg", [128, 512], BF16)
                P.act(tw[:], mw[:], AF.Tanh, [mw], [tw])
                P.cp("dve", pab[:], ma[:], [ma], [pab])
                P.act(sgg[:], mg[:], AF.Sigmoid, [mg], [sgg])
                for c in range(4):
                    cs = slice(c * 128, (c + 1) * 128)
                    rm = P.sbc("rm", [128, 512])
                    km = P.sbc("km", [128, 512])
                    vm = P.sbc("vm", [128, 512])
                    shift_mix(R_R + c * 128, 128, C_MU + c, b, rm)
                    shift_mix(R_K + c * 128, 128, C_MU + 4 + c, b, km)
                    shift_mix(R_V + c * 128, 128, C_MU + 8 + c, b, vm)
                    ps = psr.next()
                    P.mm(ps[:], wdu[:, cs], tw[:], True, True, [wdu, tw], [ps])
                    ld = P.sbc("ld", [128, 512])
                    P.act(ld[:], ps[:], AF.Sigmoid, [ps, cols], [ld], bias=cols[:, C_DB + c:C_DB + c + 1], scale=1.0)
                    P.ts("dve", ld[:], ld[:], -math.exp(-0.5), None, ALU.mult, None, [ld], [ld])
                    cum = P.sbc("cum", [128, 512])
                    P.op("dve", lambda e, cum=cum, ld=ld: e.tensor_tensor_scan(
                        out=cum[:], data0=K["rmask"][:], data1=ld[:], initial=0.0, op0=ALU.mult, op1=ALU.add),
                        [K["rmask"], ld], [cum])
                    G = P.sbc("G", [128, 512])
                    Gi = P.sbc("Gi", [128, 512])
                    Gp = P.sbc("Gp", [128, 512])
                    Ge = P.sbc("Ge", [128, 512])
                    P.act(G[:], cum[:], AF.Exp, [cum], [G])
                    P.act(Gi[:], cum[:], AF.Exp, [cum], [Gi], scale=-1.0)
                    P.tt("dve", Gp[:], cum[:], ld[:], ALU.subtract, [cum, ld], [Gp])
                    P.act(Gp[:], Gp[:], AF.Exp, [Gp], [Gp])
                    cum3 = cum[:].rearrange("p (a t) -> p a t", t=128)
                    P.tt("dve", Ge[:].rearrange("p (a t) -> p a t", t=128), cum3[:, :, 127:128].to_broadcast([128, 4, 128]),
                         cum3, ALU.subtract, [cum], [Ge])
                    P.act(Ge[:], Ge[:], AF.Exp, [Ge], [Ge])
                    gc = P.sbc("gc", [128, 4])
                    P.act(gc[:].rearrange("p (a o) -> p a o", o=1), cum3[:, :, 127:128], AF.Exp, [cum], [gc])
                    P.dma(ST, gCD[cs, b * 4:(b + 1) * 4], gc[:], reads=[gc])
                    ps = psr.next()
                    P.mm(ps[:], wiu[:, cs], pab[:], True, True, [wiu, pab], [ps])
                    ic = P.sbc("ic", [128, 512])
                    P.act(ic[:], ps[:], AF.Sigmoid, [ps, cols], [ic], bias=cols[:, C_IB + c:C_IB + c + 1], scale=1.0)
                    ps = psr.next()
                    P.mm(ps[:], wgu[:, cs], sgg[:], True, True, [wgu, sgg], [ps])
                    gt = stf.next()
                    P.cp("act", gt[:], ps[:], [ps], [gt])
                    store_feat(gateD, c, b, gt)
                    kk = P.sbc("kk", [128, 512])
                    sq = P.sbc("sq", [128, 512])
                    P.ts("dve", kk[:], km[:], cols[:, C_KK + c:C_KK + c + 1], None, ALU.mult, None, [km, cols], [kk])
                    P.tt("pool", sq[:], kk[:], kk[:], ALU.mult, [kk], [sq])
                    ps = psr.next()
                    P.mm(ps[:], K["bones"][:], sq[:], True, True, [K["bones"], sq], [ps])
                    rn = P.sbc("rn", [128, 512])
                    P.rsqrt(rn[:], ps[:], [ps], [rn], scale=1.0, bias=1e-12)
                    P.tt("dve", kk[:], kk[:], rn[:], ALU.mult, [kk, rn], [kk])
                    tk = P.sbc("tk", [128, 512])
                    P.ts("dve", tk[:], ic[:], cols[:, C_KA + c:C_KA + c + 1], omk[:, c:c + 1], ALU.mult, ALU.add,
                         [ic, cols, omk], [tk])
                    kf = P.sbc("kf", [128, 512])
                    P.tt("dve", kf[:], km[:], tk[:], ALU.mult, [km, tk], [kf])
                    bb = P.sbc("bb", [128, 512])
                    P.tt("pool", bb[:], kk[:], ic[:], ALU.mult, [kk, ic], [bb])
                    o = stb.next()
                    P.tt("dve", o[:], rm[:], G[:], ALU.mult, [rm, G], [o])
                    store_feat(rtD, c, b, o)
                    o = stb.next()
                    P.stt("dve", o[:], kk[:], -1.0, Gp[:], ALU.mult, ALU.mult, [kk, Gp], [o])
                    store_feat(atD, c, b, o)
                    o = stb.next()
                    P.tt("dve", o[:], bb[:], Gi[:], ALU.mult, [bb, Gi], [o])
                    store_feat(btD, c, b, o)
                    o = stb.next()
                    P.tt("dve", o[:], kf[:], Gi[:], ALU.mult, [kf, Gi], [o])
                    store_feat(ktD, c, b, o)
                    o = stb.next()
                    P.tt("dve", o[:], bb[:], Ge[:], ALU.mult, [bb, Ge], [o])
                    store_tok(BhD, c, b, o)
                    o = stb.next()
                    P.tt("pool", o[:], kf[:], Ge[:], ALU.mult, [kf, Ge], [o])
                    store_tok(KhD, c, b, o)
                    o = stb.next()
                    P.cp("pool", o[:], vm[:], [vm], [o])
                    store_tok(VtD, c, b, o)
                    rk = P.sbc("rk", [128, 512])
                    P.stt("dve", rk[:], rm[:], cols[:, C_RK + c:C_RK + c + 1], kf[:], ALU.mult, ALU.mult, [rm, cols, kf], [rk])
                    ps = psr.next()
                    P.mm(ps[:], K["bones"][:], rk[:], True, True, [K["bones"], rk], [ps])
                    bo = stf.next()
                    P.tt("dve", bo[:], ps[:], vm[:], ALU.mult, [ps, vm], [bo])
                    store_feat(bonD, c, b, bo)

        P.phase(phase_a2)
        if done("a2_%d" % l):
            break

        def phase_scan():
            K = load_consts(["ident", "mstrict", "mask3", "mstrictT"])
            Hf = P.sb("Hf", [64, 8, 64])
            Hb = P.sb("Hb", [64, 8, 64], BF16)
            Ht = P.sb("Htmp", [64, 8, 64])
            P.memset("dve", Hf[:], 0.0, [Hf])
            P.memset("dve", Hb[:], 0.0, [Hb])
            artr = P.rot("art", [64, 8, 256], BF16, 2)
            btr = P.rot("btl", [64, 8, 128], BF16, 2)
            ktr = P.rot("ktl", [64, 8, 128], BF16, 2)
            Bhr = P.rot("Bht", [128, 512], BF16, 2)
            Khr = P.rot("Kht", [128, 512], BF16, 2)
            Vr = P.rot("Vtt", [128, 512], BF16, 2)
            gcr = P.rot("gct", [64, 8], F32, 2)
            PAr = P.rot("PA", [128, 512], F32, 2, psum=True)
            PX = P.ps("PX", [128, 512])
            PDt = P.ps("PD", [128, 512])
            PD = PDt
            PB = PDt
            PW = P.ps("PW", [128, 512])
            PU = P.ps("PU", [128, 512])
            PY = P.ps("PY", [64, 512])
            PH = P.ps("PH", [64, 512])
            AX3r = P.rot("AX3", [128, 8, 384], BF16, 2)
            Ttr = P.rot("Tt", [128, 8, 128], BF16, 2)
            Nfr = P.rot("Nf", [128, 128], F32, 2)
            Afr = P.rot("Af", [128, 128], F32, 2)
            NAr = P.rot("NA", [128, 256], F32, 3)
            Pmr = P.rot("Pm", [128, 128], F32, 3)
            Wsr = P.rot("Wsb", [128, 512], BF16, 2)
            Ubr = P.rot("Ub", [128, 512], BF16, 2)
            Ysr = P.rot("Ysb", [64, 8, 128], F32, 2)

            for n in range(NT):
                ts_ = slice(n * 128, (n + 1) * 128)
                art = artr.next()
                btl = btr.next()
                ktl = ktr.next()
                Bht = Bhr.next()
                Kht = Khr.next()
                Vtt = Vr.next()
                gct = gcr.next()
                P.dma(LD, art[:, :, 0:128], atD[:, ts_].rearrange("(h j) t -> j h t", j=64), writes=[art])
                P.dma(LD, art[:, :, 128:256], rtD[:, ts_].rearrange("(h j) t -> j h t", j=64), writes=[art])
                P.dma(LD, btl[:], btD[:, ts_].rearrange("(h j) t -> j h t", j=64), writes=[btl])
                P.dma(LD, ktl[:], ktD[:, ts_].rearrange("(h j) t -> j h t", j=64), writes=[ktl])
                P.dma(LD, Bht[:], BhD[ts_, :], writes=[Bht])
                P.dma(LD, Kht[:], KhD[ts_, :], writes=[Kht])
                P.dma(LD, Vtt[:], VtD[ts_, :], writes=[Vtt])
                P.dma(LD, gct[:], gCD[:, n:n + 1].rearrange("(h j) o -> j (h o)", j=64), writes=[gct], slow=True)
                AX3 = AX3r.next()
                Tt = Ttr.next()
                for h in range(8):
                    PA = PAr.next()
                    P.mm(PA[:, 0:256], btl[:, h, :], art[:, h, :], True, True, [btl, art], [PA])
                    P.mm(PA[:, 256:512], ktl[:, h, :], art[:, h, :], True, True, [ktl, art], [PA])
                    P.mm(PB[:, 128:256], art[:, h, 0:128], btl[:, h, :], True, True, [art, btl], [PB])
                    Nf = Nfr.next()
                    Af = Afr.next()
                    P.tt("dve", Nf[:], PA[:, 0:128], K["mstrict"][:], ALU.mult, [PA, K["mstrict"]], [Nf])
                    P.tt("dve", AX3[:, h, :], PA[:, 128:512], K["mask3"][:], ALU.mult, [PA, K["mask3"]], [AX3])
                    P.tt("dve", Af[:], PB[:, 128:256], K["mstrictT"][:], ALU.mult, [PB, K["mstrictT"]], [Af])
                    Pm = Pmr.next()
                    P.tt("pool", Pm[:], Nf[:], K["ident"][:], ALU.add, [Nf, K["ident"]], [Pm])
                    Nc, Ac, Nb_, Ab_ = Nf[:], Af[:], Nf, Af
                    for lv in range(6):
                        last = lv == 5
                        if not last:
                            P.mm(PX[:, 0:128], Ac, Nc, True, True, [Ab_, Nb_], [PX])
                        P.mm(PX[:, 128:256], Nc, Ac, True, True, [Ab_, Nb_], [PX])
                        NA = NAr.next()
                        if last:
                            P.cp("act", NA[:, 128:256], PX[:, 128:256], [PX], [NA])
                        else:
                            P.cp("act", NA[:, 0:256], PX[:, 0:256], [PX], [NA])
                        P.mm(PD[:, 0:128], NA[:, 128:256], Pm[:], True, True, [NA, Pm], [PD])
                        Pn = Pmr.next()
                        if last:
                            P.tt("dve", Tt[:, h, :], Pm[:], PD[:, 0:128], ALU.add, [Pm, PD], [Tt])
                        else:
                            P.tt("dve", Pn[:], Pm[:], PD[:, 0:128], ALU.add, [Pm, PD], [Pn])
                        Pm = Pn
                        Nc, Ac, Nb_, Ab_ = NA[:, 0:128], NA[:, 128:256], NA, NA
                for h in range(8):
                    hs = slice(h * 64, (h + 1) * 64)
                    P.mm(PW[:, hs], art[:, h, 0:128], Hb[:, h, :], True, False, [art, Hb], [PW])
                    P.mm(PW[:, hs], AX3[:, h, 128:256], Vtt[:, hs], False, True, [AX3, Vtt], [PW])
                Wsb = Wsr.next()
                P.cp("act", Wsb[:], PW[:], [PW], [Wsb])
                for h in range(8):
                    hs = slice(h * 64, (h + 1) * 64)
                    P.mm(PU[:, hs], Tt[:, h, :], Wsb[:, hs], True, True, [Tt, Wsb], [PU])
                Ub = Ubr.next()
                P.cp("dve", Ub[:], PU[:], [PU], [Ub])
                Ysb = Ysr.next()
                for half in range(2):
                    for hh in range(4):
                        h = half * 4 + hh
                        hs = slice(h * 64, (h + 1) * 64)
                        yo = PY[:, hh * 128:(hh + 1) * 128]
                        P.mm(yo, Hb[:, h, :], art[:, h, 128:256], True, False, [Hb, art], [PY])
                        P.mm(yo, Ub[:, hs], AX3[:, h, 0:128], False, False, [Ub, AX3], [PY])
                        P.mm(yo, Vtt[:, hs], AX3[:, h, 256:384], False, True, [Vtt, AX3], [PY])
                    P.cp("act", Ysb[:, half * 4:(half + 1) * 4, :], PY[:].rearrange("p (h t) -> p h t", t=128), [PY], [Ysb])
                P.dma(ST, yTD[:, ts_].rearrange("(h i) t -> i h t", i=64), Ysb[:], reads=[Ysb])
                for h in range(8):
                    hs = slice(h * 64, (h + 1) * 64)
                    P.mm(PH[:, hs], Bht[:, hs], Ub[:, hs], True, False, [Bht, Ub], [PH])
                    P.mm(PH[:, hs], Kht[:, hs], Vtt[:, hs], False, True, [Kht, Vtt], [PH])
                P.tt("dve", Ht[:], Hf[:], gct[:].rearrange("p (h o) -> p h o", o=1).to_broadcast([64, 8, 64]), ALU.mult,
                     [Hf, gct], [Ht])
                P.tt("dve", Hf[:], Ht[:], PH[:].rearrange("p (h i) -> p h i", i=64), ALU.add, [Ht, PH], [Hf])
                P.cp("act", Hb[:], Hf[:], [Hf], [Hb])

        P.phase(phase_scan)
        if done("scan_%d" % l):
            break

        def phase_m0():
            K = load_consts(["ones"])
            ones = K["ones"]
            cols = P.sb("cols", [128, NCOL])
            P.dma(LD, cols[:], cols_in[l, :, :], writes=[cols])
            wqu = P.sb("wqu", [128, 2, 1024], BF16)
            wkvu = P.sb("wkvu", [128, 1024], BF16)
            for c in range(2):
                P.dma(LD, wqu[:, c, :], wb["wqu"][l, c * 128:(c + 1) * 128, :], writes=[wqu])
            P.dma(LD, wkvu[:], wb["wkvu"][l, :, :], writes=[wkvu])
            psr = P.rot("ps", [128, 512], F32, 6, psum=True)
            stb = P.rot("stb", [128, 512], BF16, 4)
            for b in range(NB):
                bs = slice(1 + b * 512, 1 + (b + 1) * 512)
                bs0 = slice(b * 512, (b + 1) * 512)
                CC = P.sbc("CC", [128, 512])
                SS = P.sbc("SS", [128, 512])
                P.dma(LD, CC[:], CCd[:, bs0], writes=[CC])
                P.dma(LD, SS[:], SSd[:, bs0], writes=[SS])
                ql = [P.sbc("ql%d" % c, [128, 512]) for c in range(2)]
                kvl = P.sbc("kvl", [128, 512])
                kpe = P.sbc("kpe", [32, 512])
                kpes = P.sbc("kpes", [32, 512])
                for c in range(2):
                    P.dma(LD, ql[c][:], pT[R_QL + c * 128:R_QL + (c + 1) * 128, bs], writes=[ql[c]])
                P.dma(LD, kvl[:], pT[R_KVL:R_KVL + 128, bs], writes=[kvl])
                P.dma(LD, kpe[:], pT[R_KPE:R_KPE + 32, bs], writes=[kpe])
                P.dma(LD, kpes[:], pT[R_KPES:R_KPES + 32, bs], writes=[kpes])
                ps = psr.next()
                for c in range(2):
                    sq = P.sbc("sq", [128, 512])
                    P.act(sq[:], ql[c][:], AF.Square, [ql[c]], [sq])
                    P.mm(ps[:], ones[:], sq[:], c == 0, c == 1, [ones, sq], [ps])
                rs = P.sbc("rs", [128, 512])
                P.rsqrt(rs[:], ps[:], [ps], [rs], scale=1.0 / 256, bias=EPS)
                qn = [P.sbc("qn%d" % c, [128, 512], BF16) for c in range(2)]
                for c in range(2):
                    P.stt("dve", qn[c][:], ql[c][:], cols[:, C_QN + c:C_QN + c + 1], rs[:], ALU.mult, ALU.mult,
                          [ql[c], cols, rs], [qn[c]])
                for hp in range(4):
                    ps = psr.next()
                    for c in range(2):
                        P.mm(ps[:], wqu[:, c, hp * 128:(hp + 1) * 128], qn[c][:], c == 0, c == 1, [wqu, qn[c]], [ps])
                    o = stb.next()
                    P.cp("act", o[:], ps[:], [ps], [o])
                    P.dma(ST, qnD[hp * 128:(hp + 1) * 128, bs0], o[:], reads=[o])
                for g in range(2):
                    ps1 = psr.next()
                    ps2 = psr.next()
                    for c in range(2):
                        P.mm(ps1[:], wqu[:, c, 512 + g * 128:512 + (g + 1) * 128], qn[c][:], c == 0, c == 1, [wqu, qn[c]], [ps1])
                    for c in range(2):
                        P.mm(ps2[:], wqu[:, c, 768 + g * 128:768 + (g + 1) * 128], qn[c][:], c == 0, c == 1, [wqu, qn[c]], [ps2])
                    t1 = P.sbc("t1", [128, 512])
                    t2 = P.sbc("t2", [128, 512])
                    P.tt("dve", t1[:], ps1[:], CC[:], ALU.mult, [ps1, CC], [t1])
                    P.tt("dve", t2[:], ps2[:], SS[:], ALU.mult, [ps2, SS], [t2])
                    o = stb.next()
                    P.tt("pool", o[:], t1[:], t2[:], ALU.add, [t1, t2], [o])
                    P.dma(ST, qrD[g * 128:(g + 1) * 128, bs0], o[:], reads=[o])
                sq = P.sbc("sq", [128, 512])
                P.act(sq[:], kvl[:], AF.Square, [kvl], [sq])
                ps = psr.next()
                P.mm(ps[:], ones[:], sq[:], True, True, [ones, sq], [ps])
                rs2 = P.sbc("rs2", [128, 512])
                P.rsqrt(rs2[:], ps[:], [ps], [rs2], scale=1.0 / 128, bias=EPS)
                kvn = P.sbc("kvn", [128, 512], BF16)
                P.stt("dve", kvn[:], kvl[:], cols[:, C_KVN:C_KVN + 1], rs2[:], ALU.mult, ALU.mult, [kvl, cols, rs2], [kvn])
                for hp in range(4):
                    ps = psr.next()
                    P.mm(ps[:], wkvu[:, hp * 128:(hp + 1) * 128], kvn[:], True, True, [wkvu, kvn], [ps])
                    o = stb.next()
                    P.cp("act", o[:], ps[:], [ps], [o])
                    P.dma(ST, knD[hp * 128:(hp + 1) * 128, bs0], o[:], reads=[o])
                for i in range(4):
                    ps = psr.next()
                    P.mm(ps[:], kvn[:, i * 128:(i + 1) * 128], wkvu[:, 512:1024], True, True, [kvn, wkvu], [ps])
                    o = stb.next()
                    P.cp("act", o[:], ps[:], [ps], [o])
                    P.dma(ST, vtokD[b * 512 + i * 128:b * 512 + (i + 1) * 128, :], o[:], reads=[o])
                t1 = P.sbc("t1", [128, 512])
                t2 = P.sbc("t2", [128, 512])
                P.tt("dve", t1[0:32, :], kpe[:], CC[0:32, :], ALU.mult, [kpe, CC], [t1])
                P.tt("dve", t2[0:32, :], kpes[:], SS[0:32, :], ALU.mult, [kpes, SS], [t2])
                o = stb.next()
                P.tt("pool", o[0:32, :], t1[0:32, :], t2[0:32, :], ALU.add, [t1, t2], [o])
                P.dma(ST, krD[:, bs0], o[0:32, :], reads=[o])

        P.phase(phase_m0)
        if done("m0_%d" % l):
            break

        def phase_attn():
            K = load_consts(["ident_b", "cmask_b"])
            identb = K["ident_b"]
            cmaskb = K["cmask_b"]
            Kn = P.sb("Kn", [64, 8, T], BF16)
            Kr = P.sb("Kr", [32, T], BF16)
            V = P.sb("V", [128, NT, 512], BF16)
            for h in range(8):
                P.dma(LD, Kn[:, h, :], knD[h * 64:(h + 1) * 64, :], writes=[Kn])
            P.dma(LD, Kr[:], krD[:, :], writes=[Kr])
            for n in range(NT):
                P.dma(LD, V[:, n, :], vtokD[n * 128:(n + 1) * 128, :], writes=[V])
            Qnr = P.rot("Qn", [64, 8, 128], BF16, 2)
            Qrr = P.rot("Qr", [32, 8, 128], BF16, 2)
            PS1 = P.rot("PS1", [128, 512], F32, 2, psum=True)
            PS2 = P.rot("PS2", [128, 512], F32, 2, psum=True)
            PTr = P.rot("PT", [128, 512], BF16, 2, psum=True)
            POr = P.rot("PO", [128, 64], F32, 1, psum=True)
            PTo = P.ps("PTo", [128, 512], BF16)
            Pbr = P.rot("Pb", [128, 512], BF16, 3)
            PTsr = P.rot("PTs", [128, 512], BF16, 3)
            mxr = P.rot("mx", [128, 8], F32, 2)
            rsr = P.rot("rs", [128, 8], F32, 2)
            smr = P.rot("sm", [128, 4], F32, 2)
            Osr = P.rot("Os", [128, 512], F32, 2)
            Obr = P.rot("Obf", [128, 512], BF16, 2)
            OTr = P.rot("OT", [128, 4, 128], BF16, 2)

            def scores(ps, Qn, Qr, h, qi, ch):
                k0 = ch * 4
                k1 = min(k0 + 4, qi + 1)
                w = (k1 - k0) * 128
                diag = (k1 == qi + 1)
                P.mm(ps[:, 0:w], Qn[:, h, :], Kn[:, h, k0 * 128:k0 * 128 + w], True, False, [Qn, Kn], [ps])
                P.mm(ps[:, 0:w], Qr[:, h, :], Kr[:, k0 * 128:k0 * 128 + w], False, not diag, [Qr, Kr], [ps])
                if diag:
                    P.mm(ps[:, w - 128:w], identb[:], cmaskb[:], False, True, [identb, cmaskb], [ps])
                return w, k0, k1

            for qi in range(NT):
                Qn = Qnr.next()
                Qr = Qrr.next()
                qs = slice(qi * 128, (qi + 1) * 128)
                P.dma(LD, Qn[:], qnD[:, qs].rearrange("(h c) t -> c h t", c=64), writes=[Qn])
                P.dma(LD, Qr[:], qrD[:, qs].rearrange("(h c) t -> c h t", c=32), writes=[Qr])
                nch = (qi + 4) // 4
                Os = Osr.next()
                for h in range(8):
                    mx = mxr.next()
                    rs = rsr.next()
                    sm = smr.next()
                    for ch in range(nch):
                        ps = PS1.next()
                        w, k0, k1 = scores(ps, Qn, Qr, h, qi, ch)
                        P.op("dve", lambda e, ps=ps, mx=mx, ch=ch, w=w: e.reduce_max(out=mx[:, ch:ch + 1], in_=ps[:, 0:w], axis=AX.X),
                             [ps], [mx])
                    P.op("dve", lambda e, mx=mx, sm=sm, nch=nch: e.reduce_max(out=sm[:, 0:1], in_=mx[:, 0:nch], axis=AX.X), [mx], [sm])
                    P.ts("dve", sm[:, 1:2], sm[:, 0:1], -SCALE, None, ALU.mult, None, [sm], [sm])
                    PO = POr.next()
                    for ch in range(nch):
                        ps = PS2.next()
                        w, k0, k1 = scores(ps, Qn, Qr, h, qi, ch)
                        Pb = Pbr.next()
                        P.act(Pb[:, 0:w], ps[:, 0:w], AF.Exp, [ps, sm], [Pb, rs], bias=sm[:, 1:2], scale=SCALE,
                              accum_out=rs[:, ch:ch + 1])
                        PT = PTr.next()
                        for j in range(k1 - k0):
                            P.tr(PT[:, j * 128:(j + 1) * 128], Pb[:, j * 128:(j + 1) * 128], identb[:], [Pb, identb], [PT], sig=(j == k1 - k0 - 1))
                        PTs = PTsr.next()
                        if ch % 2 == 0:
                            P.cp("dve", PTs[:, 0:w], PT[:, 0:w], [PT], [PTs])
                        else:
                            P.cp("act", PTs[:, 0:w], PT[:, 0:w], [PT], [PTs])
                        for j in range(k1 - k0):
                            kt = k0 + j
                            P.mm(PO[:, :], PTs[:, j * 128:(j + 1) * 128], V[:, kt, h * 64:(h + 1) * 64], kt == 0, kt == qi,
                                 [PTs, V], [PO])
                    P.op("dve", lambda e, rs=rs, sm=sm, nch=nch: e.reduce_sum(out=sm[:, 2:3], in_=rs[:, 0:nch], axis=AX.X), [rs], [sm])
                    P.recip(sm[:, 3:4], sm[:, 2:3], [sm], [sm])
                    P.ts("dve", Os[:, h * 64:(h + 1) * 64], PO[:, :], sm[:, 3:4], None, ALU.mult, None, [PO, sm], [Os])
                Ob = Obr.next()
                P.cp("act", Ob[:], Os[:], [Os], [Ob])
                for c in range(4):
                    P.tr(PTo[:, c * 128:(c + 1) * 128], Ob[:, c * 128:(c + 1) * 128], identb[:], [Ob, identb], [PTo], sig=(c == 3))
                OT = OTr.next()
                P.cp("dve", OT[:], PTo[:].rearrange("p (c t) -> p c t", t=128), [PTo], [OT])
                P.dma(ST, oTD[:, qs].rearrange("(c e) t -> e c t", e=128), OT[:], reads=[OT])

        P.phase(phase_attn)
        if done("attn_%d" % l):
            break

        def phase_out():
            K = load_consts(["bavg"])
            bavg = K["bavg"]
            cols = P.sb("cols", [128, NCOL])
            P.dma(LD, cols[:], cols_in[l, :, :], writes=[cols])
            woa = P.sb("woa", [128, 4, D], BF16)
            wob = P.sb("wob", [128, 4, D], BF16)
            wout = P.sb("wout", [128, 8, D], BF16)
            for c in range(4):
                P.dma(LD, woa[:, c, :], wb["woa"][l, c * 128:(c + 1) * 128, :], writes=[woa])
                P.dma(LD, wob[:, c, :], wb["wob"][l, c * 128:(c + 1) * 128, :], writes=[wob])
            for c in range(8):
                P.dma(LD, wout[:, c, :], wb["wout"][l, c * 128:(c + 1) * 128, :], writes=[wout])
            psr = P.rot("ps", [128, 512], F32, 6, psum=True)
            ldr = P.rot("ldf", [128, 512], F32, 6)
            tmpr = P.rot("tmp", [128, 512], F32, 6)
            roTr = P.rot("roT", [128, 4, 512], BF16, 2)
            oTr = P.rot("oTb", [128, 4, 512], BF16, 2)
            mgr = P.rot("mg", [128, 8, 512], BF16, 2)
            htr = P.rot("ht", [128, D], F32, 2)
            hnr = P.rot("hn", [128, D], F32, 2)
            for b in range(NB):
                bs0 = slice(b * 512, (b + 1) * 512)
                bs = slice(1 + b * 512, 1 + (b + 1) * 512)
                roT = roTr.next()
                for c in range(4):
                    cs = slice(c * 128, (c + 1) * 128)
                    y = ldr.next()
                    bo = ldr.next()
                    gt = ldr.next()
                    P.dma(LD, y[:], yTD[cs, bs0], writes=[y])
                    P.dma(LD, bo[:], bonD[cs, bs0], writes=[bo])
                    P.dma(LD, gt[:], gateD[cs, bs0], writes=[gt])
                    ps = psr.next()
                    P.mm(ps[:], bavg[:], y[:], True, True, [bavg, y], [ps])
                    cen = tmpr.next()
                    P.tt("dve", cen[:], y[:], ps[:], ALU.subtract, [y, ps], [cen])
                    sq = tmpr.next()
                    P.act(sq[:], cen[:], AF.Square, [cen], [sq])
                    ps = psr.next()
                    P.mm(ps[:], bavg[:], sq[:], True, True, [bavg, sq], [ps])
                    rs = tmpr.next()
                    P.rsqrt(rs[:], ps[:], [ps], [rs], scale=1.0, bias=LNX_EPS)
                    P.tt("dve", cen[:], cen[:], rs[:], ALU.mult, [cen, rs], [cen])
                    P.ts("dve", cen[:], cen[:], cols[:, C_LW + c:C_LW + c + 1], cols[:, C_LB + c:C_LB + c + 1], ALU.mult, ALU.add,
                         [cen, cols], [cen])
                    P.tt("pool", cen[:], cen[:], bo[:], ALU.add, [cen, bo], [cen])
                    P.tt("dve", roT[:, c, :], cen[:], gt[:], ALU.mult, [cen, gt], [roT])
                oTb = oTr.next()
                for c in range(4):
                    P.dma(LD, oTb[:, c, :], oTD[c * 128:(c + 1) * 128, bs0], writes=[oTb])
                mg = mgr.next()
                for dc in range(8):
                    ds_ = slice(dc * 128, (dc + 1) * 128)
                    sga = ldr.next()
                    sgb = ldr.next()
                    P.dma(LD, sga[:], pT[R_GA + dc * 128:R_GA + (dc + 1) * 128, bs], writes=[sga])
                    P.dma(LD, sgb[:], pT[R_GB + dc * 128:R_GB + (dc + 1) * 128, bs], writes=[sgb])
                    pa = psr.next()
                    pb = psr.next()
                    for c in range(4):
                        P.mm(pa[:], woa[:, c, ds_], roT[:, c, :], c == 0, c == 3, [woa, roT], [pa])
                    for c in range(4):
                        P.mm(pb[:], wob[:, c, ds_], oTb[:, c, :], c == 0, c == 3, [wob, oTb], [pb])
                    m1 = tmpr.next()
                    m2 = tmpr.next()
                    P.tt("dve", m1[:], pa[:], sga[:], ALU.mult, [pa, sga], [m1])
                    P.tt("dve", m2[:], pb[:], sgb[:], ALU.mult, [pb, sgb], [m2])
                    P.tt("pool", mg[:, dc, :], m1[:], m2[:], ALU.add, [m1, m2], [mg])
                for i in range(4):
                    t0 = b * 512 + i * 128
                    ht = htr.next()
                    hn = hnr.next()
                    P.dma(LD, ht[:], h_src[t0:t0 + 128, :], writes=[ht])
                    for half in range(2):
                        pd = psr.next()
                        for dc in range(8):
                            P.mm(pd[:], mg[:, dc, i * 128:(i + 1) * 128], wout[:, dc, half * 512:(half + 1) * 512], dc == 0, dc == 7,
                                 [mg, wout], [pd])
                        P.tt("dve", hn[:, half * 512:(half + 1) * 512], ht[:, half * 512:(half + 1) * 512], pd[:], ALU.add,
                             [ht, pd], [hn])
                    P.dma(ST, hD[t0:t0 + 128, :], hn[:], reads=[hn])

        P.phase(phase_out)
        if done("out_%d" % l):
            break

        def phase_ffn():
            lastl = is_last
            K = load_consts(["ident_b"])
            identb = K["ident_b"]
            cols = P.sb("cols", [128, NCOL])
            P.dma(LD, cols[:], cols_in[l, :, :], writes=[cols])
            wdn = P.sb("wdn", [128, 22, D], BF16)
            for f in range(22):
                P.dma(LD, wdn[:, f, :], wb["wdn"][l, f * 128:(f + 1) * 128, :], writes=[wdn])
            fb = P.sb("fb", [128, D])
            P.dma(LD, fb[:], rows_in[2 * l + 1, :].partition_broadcast(128), writes=[fb])
            fnb = None
            if lastl:
                fnb = P.sb("fnb", [128, D])
                P.dma(LD, fnb[:], rows_in[2 * depth, :].partition_broadcast(128), writes=[fnb])
            carry = P.sb("carry", [128, 22, 2])
            P.memset("dve", carry[:], 0.0, [carry])
            norm_alloc()
            hts = [P.sb("htk%d" % i, [128, D]) for i in range(4)]
            nTr = P.rot("nT", [128, 8, 512], BF16, 1)
            hmr = P.rot("hm", [128, 22, 512], BF16, 1)
            wgr = P.rot("wg", [128, 8, 128], BF16, 3)
            wvr = P.rot("wv", [128, 8, 128], BF16, 3)
            pgr = P.rot("pg", [128, 512], F32, 2, psum=True)
            pvr = P.rot("pv", [128, 512], F32, 2, psum=True)
            pdr = P.rot("pd", [128, 512], F32, 2, psum=True)
            ugr = P.rot("ug", [128, 514], F32, 2)
            cvr = P.rot("cv", [128, 512], F32, 2)
            hnr = P.rot("hn", [128, D], F32, 2)
            sqr = P.rot("fsq", [128, 2], F32, 2)
            for b in range(NB):
                nT = nTr.next()
                norm_transpose(lambda t: hD[t * 128:(t + 1) * 128, :], fb, identb, nT, b, ht_keep=hts)
                hm = hmr.next()
                for f in range(22):
                    wg = wgr.next()
                    wv = wvr.next()
                    P.dma(LD, wg[:], wb["wup"][l, :, f * 128:(f + 1) * 128].rearrange("(k p) n -> p k n", p=128), writes=[wg])
                    P.dma(LD, wv[:], wb["wup"][l, :, 2816 + f * 128:2816 + (f + 1) * 128].rearrange("(k p) n -> p k n", p=128),
                          writes=[wv])
                    pg = pgr.next()
                    pv = pvr.next()
                    for kc in range(8):
                        P.mm(pg[:], wg[:, kc, :], nT[:, kc, :], kc == 0, kc == 7, [wg, nT], [pg])
                    for kc in range(8):
                        P.mm(pv[:], wv[:, kc, :], nT[:, kc, :], kc == 0, kc == 7, [wv, nT], [pv])
                    ug = ugr.next()
                    P.cp("pool", ug[:, 0:2], carry[:, f, :], [carry], [ug])
                    P.cp("act", ug[:, 2:514], pg[:], [pg], [ug])
                    P.cp("pool", carry[:, f, :], ug[:, 512:514], [ug], [carry])
                    cv = cvr.next()
                    cw = lambda j: cols[:, C_CW + j * 22 + f:C_CW + j * 22 + f + 1]
                    P.ts("dve", cv[:], ug[:, 2:514], cw(2), cols[:, C_CB + f:C_CB + f + 1], ALU.mult, ALU.add, [ug, cols], [cv])
                    P.stt("dve", cv[:], ug[:, 1:513], cw(1), cv[:], ALU.mult, ALU.add, [ug, cols, cv], [cv])
                    P.stt("dve", cv[:], ug[:, 0:512], cw(0), cv[:], ALU.mult, ALU.add, [ug, cols, cv], [cv])
                    P.act(cv[:], cv[:], AF.Gelu, [cv], [cv])
                    P.tt("dve", hm[:, f, :], cv[:], pv[:], ALU.mult, [cv, pv], [hm])
                for i in range(4):
                    t0 = b * 512 + i * 128
                    hn = hnr.next()
                    for half in range(2):
                        pd = pdr.next()
                        for f in range(22):
                            P.mm(pd[:], hm[:, f, i * 128:(i + 1) * 128], wdn[:, f, half * 512:(half + 1) * 512], f == 0, f == 21,
                                 [hm, wdn], [pd])
                        P.tt("dve", hn[:, half * 512:(half + 1) * 512], hts[i][:, half * 512:(half + 1) * 512], pd[:], ALU.add,
                             [hts[i], pd], [hn])
                    if not lastl:
                        P.dma(ST, hD[t0:t0 + 128, :], hn[:], reads=[hn])
                    else:
                        junk = norm_transpose.junk.next()
                        sq = sqr.next()
                        P.act(junk[:], hn[:], AF.Square, [hn], [junk, sq], accum_out=sq[:, 0:1])
                        P.rsqrt(sq[:, 0:1], sq[:, 0:1], [sq], [sq], scale=1.0 / D, bias=EPS)
                        P.stt("dve", hn[:], hn[:], sq[:, 0:1], fnb[:], ALU.mult, ALU.mult, [hn, sq, fnb], [hn])
                        P.dma(ST, out_ap[t0:t0 + 128, :], hn[:], reads=[hn])

        P.phase(phase_ffn)
        if done("ffn_%d" % l):
            break

    P.root.close()
    return nc, P


def make_consts():
    c = np.zeros((128, NCST), np.float32)
    i = np.arange(128)
    c[:, K_ID:K_ID + 128] = np.eye(128)
    strict = (i[:, None] < i[None, :]).astype(np.float32)
    incl = (i[:, None] <= i[None, :]).astype(np.float32)
    c[:, K_MS:K_MS + 128] = strict
    c[:, K_M3:K_M3 + 128] = incl
    c[:, K_M3 + 128:K_M3 + 256] = strict
    c[:, K_M3 + 256:K_M3 + 384] = incl
    c[:, K_MST:K_MST + 128] = strict.T
    c[:, K_CM:K_CM + 128] = np.where(i[None, :] <= i[:, None], 0.0, NEG)
    blk = (i[:, None] // 64 == i[None, :] // 64).astype(np.float32)
    c[:, K_BO:K_BO + 128] = blk
    c[:, K_ON:K_ON + 128] = 1.0
    c[:, K_BA:K_BA + 128] = blk / 64.0
    rm = np.ones(512, np.float32)
    rm[::128] = 0.0
    c[:, K_RM:K_RM + 512] = rm[None, :]
    inv_freq = np.power(np.float32(10000.0), -np.arange(0, 32, 2, dtype=np.float32) / np.float32(32)).astype(np.float32)
    c[:, K_IF] = inv_freq[i % 16]
    c[:, K_SG] = np.where((i % 32) < 16, -1.0, 1.0)
    return c


def _colpack(v, n):
    v = np.asarray(v, np.float32).reshape(-1)
    out = np.zeros((128, n), np.float32)
    if v.size == 64:
        out[:64, 0] = v
    else:
        out[:, :] = v.reshape(n, 128).T
    return out


def prep_inputs(inp, depth=2):
    f = lambda k: np.asarray(inp[k], np.float32)
    cols = np.zeros((depth, 128, NCOL), np.float32)
    rows = np.zeros((2 * depth + 1, D), np.float32)
    w = {}
    w_in = f("w_in")
    kpe = w_in[:, :, 2176:2208]
    kpes = np.concatenate([kpe[:, :, 16:32], kpe[:, :, 0:16]], axis=-1)
    w["win"] = np.ascontiguousarray(np.concatenate([w_in[:, :, :2208], kpes, w_in[:, :, 2208:]], axis=-1))
    w["wdu"] = f("w_decay_up")
    w["wiu"] = f("w_iclr_up")
    w["wgu"] = f("w_gate_up")
    w["woa"] = f("w_out_rwkv")
    wq = f("w_q_up").reshape(depth, 256, 8, 96)
    nope = wq[..., :64].reshape(depth, 256, 512)
    pe = wq[..., 64:].reshape(depth, 256, 256)
    pes = np.concatenate([wq[..., 80:96], wq[..., 64:80]], axis=-1).reshape(depth, 256, 256)
    w["wqu"] = np.ascontiguousarray(np.concatenate([nope, pe, pes], axis=-1))
    wkv = f("w_kv_up").reshape(depth, 128, 8, 128)
    w["wkvu"] = np.ascontiguousarray(np.concatenate([wkv[..., :64].reshape(depth, 128, 512),
                                                     wkv[..., 64:].reshape(depth, 128, 512)], axis=-1))
    w["wob"] = f("w_out_mla")
    w["wout"] = f("w_out")
    w["wup"] = f("w_ffn_up")
    w["wdn"] = f("w_ffn_down")
    for l in range(depth):
        mu = f("mu_shift")[l]
        cols[l, :, C_MU:C_MU + 12] = _colpack(mu[:1536], 12)
        cols[l, :, C_MU + 12:C_MU + 13] = _colpack(mu[1536:1600], 1)
        cols[l, :, C_MU + 13:C_MU + 14] = _colpack(mu[1600:1664], 1)
        cols[l, :, C_MU + 14:C_MU + 15] = _colpack(mu[1664:1792], 1)
        cols[l, :, C_DB:C_DB + 4] = _colpack(f("decay_base")[l], 4)
        cols[l, :, C_IB:C_IB + 4] = _colpack(f("iclr_base")[l], 4)
        cols[l, :, C_KK:C_KK + 4] = _colpack(f("k_k")[l], 4)
        cols[l, :, C_KA:C_KA + 4] = _colpack(f("k_a")[l], 4)
        cols[l, :, C_RK:C_RK + 4] = _colpack(f("r_k")[l], 4)
        cols[l, :, C_LW:C_LW + 4] = _colpack(f("lnx_w")[l], 4)
        cols[l, :, C_LB:C_LB + 4] = _colpack(f("lnx_b")[l], 4)
        cols[l, :, C_QN:C_QN + 2] = _colpack(f("q_norm")[l], 2)
        cols[l, :, C_KVN:C_KVN + 1] = _colpack(f("kv_norm")[l], 1)
        for j in range(3):
            cols[l, :, C_CW + j * 22:C_CW + (j + 1) * 22] = _colpack(f("conv_w")[l, j], 22)
        cols[l, :, C_CB:C_CB + 22] = _colpack(f("conv_b")[l], 22)
        rows[2 * l] = f("attn_norm")[l]
        rows[2 * l + 1] = f("ffn_norm")[l]
    rows[2 * depth] = f("final_norm")
    shared = {"cst": make_consts(), "cols": cols, "rows": rows}
    shared.update(w)
    return shared


_CACHE = {}


def kernel(**inputs):
    x = np.asarray(inputs["x"], np.float32)
    pos = np.asarray(inputs["positions"], np.int32)
    B, T, _ = x.shape
    shared = prep_inputs(inputs)
    key = ("nc", T)
    if key not in _CACHE:
        _CACHE[key] = build(T)[0]
    nc = _CACHE[key]
    in_maps = []
    for core in range(8):
        b = core // 2
        m = dict(shared)
        m["x"] = np.ascontiguousarray(x[b])
        m["pos"] = np.ascontiguousarray(pos[b])
        in_maps.append(m)
    res = run_bass_kernel_spmd(nc, in_maps, core_ids=list(range(8)))
    out = np.empty((B, T, D), np.float32)
    half = T // 2
    for b in range(B):
        out[b, :half] = res.results[2 * b]["out"][:half]
        out[b, half:] = res.results[2 * b + 1]["out"][half:]
    return out
```

```python
import math
from contextlib import ExitStack

import numpy as np
import concourse.bass as bass
import concourse.mybir as mybir
from concourse.bass_utils import run_bass_kernel_spmd

F32 = mybir.dt.float32
BF16 = mybir.dt.bfloat16
I32 = mybir.dt.int32
AF = mybir.ActivationFunctionType
ALU = mybir.AluOpType
AX = mybir.AxisListType

ENGS = ("pe", "act", "dve", "pool", "sp")

D = 1024
NCOL = 134
NCST = 1794
WIN_COLS = 4288
EPS = 1e-6
LNX_EPS = 64e-5
SCALE = 96.0 ** -0.5
NEG = -30000.0
R_R, R_K, R_V, R_W, R_A, R_G = 0, 512, 1024, 1536, 1600, 1664
R_QL, R_KVL, R_KPE, R_KPES, R_GA, R_GB = 1792, 2048, 2176, 2208, 2240, 3264
C_MU, C_DB, C_IB, C_KK, C_KA, C_RK, C_LW, C_LB, C_QN, C_KVN, C_CW, C_CB = 0, 15, 19, 23, 27, 31, 35, 39, 43, 45, 46, 112
K_ID, K_MS, K_M3, K_MST, K_CM, K_BO, K_ON, K_BA, K_RM, K_IF, K_SG = 0, 128, 256, 640, 768, 896, 1024, 1152, 1280, 1792, 1793


class Buf:
    __slots__ = ("name", "w", "r", "ap")

    def __init__(self, name, ap=None):
        self.name = name
        self.w = None
        self.r = []
        self.ap = ap

    def __getitem__(self, k):
        return self.ap[k]


class Ev:
    __slots__ = ("key", "val", "clock")

    def __init__(self, key, val, clock):
        self.key = key
        self.val = val
        self.clock = clock


class Rot:
    def __init__(self, bufs):
        self.bufs = bufs
        self.i = 0

    def next(self):
        b = self.bufs[self.i % len(self.bufs)]
        self.i += 1
        return b


class Prog:
    def __init__(self, nc, ndma=(("sp", 24), ("pool", 24), ("act", 8))):
        self.nc = nc
        self.root = ExitStack()
        self.esem = {}
        for e in ENGS:
            self.esem[e] = self.root.enter_context(nc.semaphore("es_" + e))
        self.dsem = {}
        self.dnext = {}
        self.dlast = {}
        for q, n in ndma:
            self.dsem[q] = [self.root.enter_context(nc.semaphore("ds_%s%d" % (q, i))) for i in range(n)]
            self.dnext[q] = 0
            self.dlast[q] = [None] * n
        self.cnt = {e: 0 for e in ENGS}
        self.clock = {e: {} for e in ENGS}
        self.ops = {e: [] for e in ENGS}
        self.pending_dma = []
        self.es = None
        self.uid = 0
        self.ninst = 0
        self.cache = {}
        self.nep = 0

    def sb(self, name, shape, dt=F32):
        self.uid += 1
        t = self.es.enter_context(self.nc.sbuf_tensor("%s_%d" % (name, self.uid), list(shape), dt))
        return Buf(name, t)

    def ps(self, name, shape, dt=F32):
        self.uid += 1
        t = self.es.enter_context(self.nc.psum_tensor("%s_%d" % (name, self.uid), list(shape), dt))
        return Buf(name, t)

    def sbc(self, name, shape, dt=F32, n=2):
        if name not in self.cache:
            self.cache[name] = self.rot(name, shape, dt, n)
        return self.cache[name].next()

    def rot(self, name, shape, dt, n, psum=False):
        return Rot([(self.ps if psum else self.sb)(name, shape, dt) for _ in range(n)])

    def _deps(self, e, reads, writes):
        deps = []
        for b in reads:
            if b.w is not None:
                deps.append(b.w)
        for b in writes:
            if b.w is not None:
                deps.append(b.w)
            deps.extend(b.r)
        ck = self.clock[e]
        best = {}
        for ev in deps:
            if ev.key == "pe" and e == "pe":
                continue
            if ck.get(ev.key, 0) >= ev.val:
                continue
            best[ev.key] = max(best.get(ev.key, 0), ev.val)
            for k, v in ev.clock.items():
                if ck.get(k, 0) < v:
                    ck[k] = v
            ck[ev.key] = ev.val
        return list(best.items())

    def _commit(self, ev, reads, writes):
        for b in reads:
            b.r.append(ev)
        for b in writes:
            b.w = ev
            b.r = []

    def op(self, e, fn, reads=(), writes=(), sig=True):
        waits = self._deps(e, reads, writes)
        if sig:
            self.cnt[e] += 1
            ev = Ev(e, self.cnt[e], dict(self.clock[e]))
            self.ops[e].append((waits, fn, ("e", e)))
        else:
            ev = Ev(e, self.cnt[e] + 1, dict(self.clock[e]))
            self.ops[e].append((waits, fn, ("n",)))
        self._commit(ev, reads, writes)
        return ev

    def dma(self, q, out, in_, reads=(), writes=(), slow=False):
        waits = self._deps(q, reads, writes)
        i = self.dnext[q]
        n = len(self.dsem[q])
        self.dnext[q] = (i + 1) % n
        prev = self.dlast[q][i]
        key = ("d", q, i)
        if prev is not None and self.clock[q].get(key, 0) < prev:
            waits.append((key, prev))
            self.clock[q][key] = prev
        val = (prev or 0) + 16
        self.dlast[q][i] = val
        ev = Ev(key, val, dict(self.clock[q]))
        self._commit(ev, reads, writes)
        if slow:
            self.ops[q].append((waits, lambda eng: eng.dma_start(out=out, in_=in_, allow_slow_non_contiguous=True), ("d", q, i)))
        else:
            self.ops[q].append((waits, lambda eng: eng.dma_start(out=out, in_=in_), ("d", q, i)))
        self.pending_dma.append(ev)
        return ev

    def _sem(self, key, esems=None):
        if isinstance(key, tuple):
            return self.dsem[key[1]][key[2]]
        return (esems or self.esem)[key]

    def emit(self):
        nc = self.nc
        best = {}
        for ev in self.pending_dma:
            best[ev.key] = max(best.get(ev.key, 0), ev.val)
        final_waits = list(best.items())
        for e in ("pe", "act", "dve", "pool"):
            if self.cnt[e] > 0:
                final_waits.append((e, self.cnt[e]))
        self.pending_dma = []
        ops = self.ops
        self.ops = {e: [] for e in ENGS}
        engmap = {"pe": "tensor", "act": "scalar", "dve": "vector", "pool": "gpsimd", "sp": "sync"}
        with nc.Block() as block:
            for e in ENGS:
                lst = ops[e]
                if e == "sp":
                    lst = lst + [(final_waits, None, None)]
                if not lst:
                    continue
                self.ninst += len(lst)

                def body(eng, lst=lst, e=e, esem_e=self.esem[e], esems=dict(self.esem)):
                    for waits, fn, kind in lst:
                        for k, v in waits:
                            eng.wait_ge(self._sem(k, esems), v)
                        if fn is None:
                            continue
                        ins = fn(eng)
                        if kind[0] == "e":
                            ins.then_inc(esem_e, 1)
                        elif kind[0] == "d":
                            ins.then_inc(self.dsem[kind[1]][kind[2]], 16)

                getattr(block, engmap[e])(body)
        full = {}
        for e in ENGS:
            full[e] = self.cnt[e]
        for q in self.dsem:
            for i, v in enumerate(self.dlast[q]):
                if v:
                    full[("d", q, i)] = v
        for e in ENGS:
            self.clock[e] = dict(full)

    def new_epoch(self):
        for e in ENGS:
            self.esem[e] = self.root.enter_context(self.nc.semaphore("es%d_%s" % (self.nep, e)))
            self.cnt[e] = 0
        self.nep += 1
        for e in ENGS:
            ck = {k: v for k, v in self.clock[e].items() if isinstance(k, tuple)}
            self.clock[e] = ck

    def phase(self, body):
        with ExitStack() as es:
            self.es = es
            self.cache = {}
            body()
            self.emit()
        self.es = None

    def act(self, out, in_, func, r, w, **kw):
        return self.op("act", lambda e: e.activation(out=out, in_=in_, func=func, **kw), r, w)

    def cp(self, eng, out, in_, r, w):
        if eng == "act":
            return self.op("act", lambda e: e.copy(out=out, in_=in_), r, w)
        return self.op(eng, lambda e: e.tensor_copy(out=out, in_=in_), r, w)

    def tt(self, eng, out, a, b, op, r, w):
        return self.op(eng, lambda e: e.tensor_tensor(out=out, in0=a, in1=b, op=op), r, w)

    def ts(self, eng, out, a, s1, s2, op0, op1, r, w):
        if s2 is None:
            return self.op(eng, lambda e: e.tensor_scalar(out=out, in0=a, scalar1=s1, scalar2=None, op0=op0), r, w)
        return self.op(eng, lambda e: e.tensor_scalar(out=out, in0=a, scalar1=s1, scalar2=s2, op0=op0, op1=op1), r, w)

    def stt(self, eng, out, a, s, b, op0, op1, r, w):
        return self.op(eng, lambda e: e.scalar_tensor_tensor(out=out, in0=a, scalar=s, in1=b, op0=op0, op1=op1), r, w)

    def mm(self, out, lhsT, rhs, start, stop, r, w):
        return self.op("pe", lambda e: e.matmul(out, lhsT=lhsT, rhs=rhs, start=start, stop=stop), r, w, sig=bool(stop))

    def tr(self, out, in_, ident, r, w, sig=True):
        return self.op("pe", lambda e: e.transpose(out=out, in_=in_, identity=ident), r, w, sig=sig)

    def memset(self, eng, ap, val, w):
        return self.op(eng, lambda e: e.memset(ap, val), [], w)

    def recip(self, out, in_, r, w):
        return self.op("dve", lambda e: e.reciprocal(out=out, in_=in_), r, w)

    def rsqrt(self, out, in_, r, w, scale=1.0, bias=0.0):
        self.act(out, in_, AF.Sqrt, r, w, bias=bias, scale=scale)
        self.recip(out, out, w, w)


def build(T, depth=2, dbg=(), stop=None, reps=1, pad_mb=0):
    nc = bass.Bass("TRN2", target_bir_lowering=False)
    NT = T // 128
    NB = T // 512
    assert T % 512 == 0

    def din(name, shape, dt=F32):
        return nc.dram_tensor(name, list(shape), dt, kind="ExternalInput").ap()

    def dscr(name, shape, dt=F32):
        kind = "ExternalOutput" if name in dbg else "Internal"
        return nc.dram_tensor(name, list(shape), dt, kind=kind).ap()

    x_in = din("x", [T, D])
    pos_in = din("pos", [T], I32)
    cst_in = din("cst", [128, NCST])
    cols_in = din("cols", [depth, 128, NCOL])
    rows_in = din("rows", [2 * depth + 1, D])
    wspec = [("win", D, WIN_COLS), ("wdu", 64, 512), ("wiu", 64, 512), ("wgu", 128, 512), ("woa", 512, D),
             ("wqu", 256, 1024), ("wkvu", 128, 1024), ("wob", 512, D), ("wout", D, D), ("wup", D, 5632),
             ("wdn", 2816, D)]
    wf = {}
    wb = {}
    for n, r, c in wspec:
        wf[n] = din(n, [depth, r, c])
        wb[n] = dscr(n + "_b", [depth, r, c], BF16)
    out_ap = nc.dram_tensor("out", [T, D], F32, kind="ExternalOutput").ap()

    hD = dscr("h", [T, D])
    pT = dscr("pT", [WIN_COLS, T + 1])
    CCd = dscr("CC", [128, T])
    SSd = dscr("SS", [128, T])
    rtD = dscr("rt", [512, T], BF16)
    atD = dscr("at", [512, T], BF16)
    btD = dscr("bt", [512, T], BF16)
    ktD = dscr("kt", [512, T], BF16)
    BhD = dscr("Bh", [T, 512], BF16)
    KhD = dscr("Kh", [T, 512], BF16)
    VtD = dscr("Vt", [T, 512], BF16)
    gCD = dscr("gC", [512, NT])
    bonD = dscr("bon", [512, T])
    gateD = dscr("gate", [512, T])
    yTD = dscr("yT", [512, T])
    qnD = dscr("qn", [512, T], BF16)
    qrD = dscr("qr", [256, T], BF16)
    knD = dscr("kn", [512, T], BF16)
    krD = dscr("kr", [32, T], BF16)
    vtokD = dscr("vtok", [T, 512], BF16)
    oTD = dscr("oT", [512, T], BF16)

    P = Prog(nc)
    padD = dscr("padD", [pad_mb * 2048, 128]) if pad_mb else None
    LD = "sp"
    ST = "pool"

    def done(name):
        return stop == name

    def phase_prep():
        stg = P.rot("stg", [128, 2048], F32, 3)
        ob = P.rot("ob", [128, 2048], BF16, 3)
        engs = ["dve", "act", "pool"]
        k = 0
        for l in range(depth):
            for n, r, c in wspec:
                for r0 in range(0, r, 128):
                    rr = min(128, r - r0)
                    for c0 in range(0, c, 2048):
                        cc = min(2048, c - c0)
                        s = stg.next()
                        o = ob.next()
                        P.dma(LD, s[0:rr, 0:cc], wf[n][l, r0:r0 + rr, c0:c0 + cc], writes=[s])
                        P.cp(engs[k % 3], o[0:rr, 0:cc], s[0:rr, 0:cc], [s], [o])
                        k += 1
                        P.dma(ST, wb[n][l, r0:r0 + rr, c0:c0 + cc], o[0:rr, 0:cc], reads=[o])
        if padD is not None:
            zt = P.sb("zt", [128, 128])
            P.memset("dve", zt[:], 0.0, [zt])
            P.dma(ST, padD[pad_mb * 2048 - 128:pad_mb * 2048, :], zt[:], reads=[zt])
        cst = P.sb("cst", [128, 2])
        P.dma(LD, cst[:], cst_in[:, K_IF:K_IF + 2], writes=[cst])
        C1 = 6.28125
        C2 = 2 * math.pi - C1
        for c0 in range(0, T, 512):
            pi_ = P.sbc("pi", [128, 512], I32)
            pf = P.sbc("pf", [128, 512])
            P.dma(LD, pi_[:], pos_in[c0:c0 + 512].partition_broadcast(128), writes=[pi_])
            P.cp("dve", pf[:], pi_[:], [pi_], [pf])
            ang = P.sbc("ang", [128, 512])
            P.ts("dve", ang[:], pf[:], cst[:, 0:1], None, ALU.mult, None, [pf, cst], [ang])
            for which, dst in ((0, SSd), (1, CCd)):
                a2 = P.sbc("a2", [128, 512])
                ki = P.sbc("ki", [128, 512], I32)
                kf = P.sbc("kf", [128, 512])
                m = P.sbc("m", [128, 512])
                w_ = P.sbc("w_", [128, 512])
                res = P.sbc("res", [128, 512])
                P.ts("dve", a2[:], ang[:], (math.pi / 2) if which else 0.0, None, ALU.add, None, [ang], [a2])
                P.ts("dve", ki[:], a2[:], 1.0 / (2 * math.pi), None, ALU.mult, None, [a2], [ki])
                P.cp("dve", kf[:], ki[:], [ki], [kf])
                P.stt("dve", m[:], kf[:], -C1, a2[:], ALU.mult, ALU.add, [kf, a2], [m])
                P.stt("dve", m[:], kf[:], -C2, m[:], ALU.mult, ALU.add, [kf, m], [m])
                P.ts("dve", w_[:], m[:], math.pi, -2 * math.pi, ALU.is_gt, ALU.mult, [m], [w_])
                P.tt("dve", m[:], m[:], w_[:], ALU.add, [m, w_], [m])
                P.ts("dve", w_[:], m[:], -math.pi, 2 * math.pi, ALU.is_lt, ALU.mult, [m], [w_])
                P.tt("dve", m[:], m[:], w_[:], ALU.add, [m, w_], [m])
                P.act(res[:], m[:], AF.Sin, [m], [res])
                if which == 0:
                    P.ts("dve", res[:], res[:], cst[:, 1:2], None, ALU.mult, None, [res, cst], [res])
                P.dma(ST, dst[:, c0:c0 + 512], res[:], reads=[res])

    P.phase(phase_prep)

    def load_consts(names):
        spec = {"ident": (K_ID, 128), "mstrict": (K_MS, 128), "mask3": (K_M3, 384), "mstrictT": (K_MST, 128),
                "cmask": (K_CM, 128), "bones": (K_BO, 128), "ones": (K_ON, 128), "bavg": (K_BA, 128),
                "rmask": (K_RM, 512)}
        out = {}
        for n in names:
            bf = n.endswith("_b")
            base = n[:-2] if bf else n
            o, w = spec[base]
            t = P.sb("k_" + base, [128, w])
            P.dma(LD, t[:], cst_in[:, o:o + w], writes=[t])
            if bf:
                tb = P.sb("kb_" + base, [128, w], BF16)
                P.cp("dve", tb[:], t[:], [t], [tb])
                out[n] = tb
            else:
                out[n] = t
        return out

    def norm_transpose(src_ap_fn, gb, identb, nT, b, ht_keep=None):
        for i in range(4):
            ht = ht_keep[i] if ht_keep is not None else norm_transpose.ht.next()
            P.dma(LD, ht[:], src_ap_fn(b * 4 + i), writes=[ht])
            junk = norm_transpose.junk.next()
            ssq = norm_transpose.ssq.next()
            P.act(junk[:], ht[:], AF.Square, [ht], [junk, ssq], accum_out=ssq[:, 0:1])
            P.rsqrt(ssq[:, 0:1], ssq[:, 0:1], [ssq], [ssq], scale=1.0 / D, bias=EPS)
            hs = norm_transpose.hs.next()
            P.stt("dve", hs[:], ht[:], ssq[:, 0:1], gb[:], ALU.mult, ALU.mult, [ht, ssq, gb], [hs])
            pt = norm_transpose.pt.next()
            for kc in range(8):
                P.tr(pt[:, kc * 128:(kc + 1) * 128], hs[:, kc * 128:(kc + 1) * 128], identb[:], [hs, identb], [pt], sig=(kc == 7))
            P.cp("act", nT[:, :, i * 128:(i + 1) * 128], pt[:].rearrange("p (k t) -> p k t", t=128), [pt], [nT])

    def norm_alloc():
        norm_transpose.ht = P.rot("ht", [128, D], F32, 2)
        norm_transpose.junk = P.rot("junk", [128, D], BF16, 1)
        norm_transpose.ssq = P.rot("ssq", [128, 1], F32, 2)
        norm_transpose.hs = P.rot("hs", [128, D], BF16, 2)
        norm_transpose.pt = P.rot("pt", [128, D], BF16, 2, psum=True)

    for lidx in range(depth * reps):
        l = lidx % depth
        h_src = x_in if lidx == 0 else hD
        is_last = (lidx == depth * reps - 1)
        P.new_epoch()

        def phase_a1():
            K = load_consts(["ident_b"])
            Wb = P.sb("Wb", [128, 8, WIN_COLS], BF16)
            for kc in range(8):
                P.dma(LD, Wb[:, kc, :], wb["win"][l, kc * 128:(kc + 1) * 128, :], writes=[Wb])
            gb = P.sb("gb", [128, D])
            P.dma(LD, gb[:], rows_in[2 * l, :].partition_broadcast(128), writes=[gb])
            zc = P.sb("zc", [128, 1])
            P.memset("dve", zc[:], 0.0, [zc])
            for r0 in range(0, 1792, 128):
                P.dma(ST, pT[r0:r0 + 128, 0:1], zc[:], reads=[zc], slow=True)
            norm_alloc()
            nTr = P.rot("nT", [128, 8, 512], BF16, 2)
            psr = P.rot("ps", [128, 512], F32, 4, psum=True)
            stg = P.rot("stg", [128, 512], F32, 4)
            chunks = []
            for r0 in range(0, 1536, 128):
                chunks.append((r0, 128, 0))
            chunks += [(R_W, 64, 0), (R_A, 64, 0), (R_G, 128, 0), (R_QL, 128, 0), (R_QL + 128, 128, 0),
                       (R_KVL, 128, 0), (R_KPE, 32, 0), (R_KPES, 32, 0)]
            for r0 in range(R_GA, WIN_COLS, 128):
                chunks.append((r0, 128, 1))
            k = 0
            for b in range(NB):
                nT = nTr.next()
                norm_transpose(lambda t: h_src[t * 128:(t + 1) * 128, :], gb, K["ident_b"], nT, b)
                for r0, M, sig in chunks:
                    ps = psr.next()
                    for kc in range(8):
                        P.mm(ps[0:M, :], Wb[:, kc, r0:r0 + M], nT[:, kc, :], kc == 0, kc == 7, [Wb, nT], [ps])
                    s = stg.next()
                    if sig:
                        P.act(s[0:M, :], ps[0:M, :], AF.Sigmoid, [ps], [s])
                    elif k % 2 == 0:
                        P.cp("dve", s[0:M, :], ps[0:M, :], [ps], [s])
                    else:
                        P.cp("act", s[0:M, :], ps[0:M, :], [ps], [s])
                    k += 1
                    P.dma(ST, pT[r0:r0 + M, 1 + b * 512:1 + (b + 1) * 512], s[0:M, :], reads=[s])

        P.phase(phase_a1)
        if done("a1_%d" % l):
            break

        def phase_a2():
            K = load_consts(["ident_b", "bones", "rmask"])
            identb = K["ident_b"]
            cols = P.sb("cols", [128, NCOL])
            P.dma(LD, cols[:], cols_in[l, :, :], writes=[cols])
            omk = P.sb("omk", [128, 4])
            P.ts("dve", omk[:], cols[:, C_KA:C_KA + 4], -1.0, 1.0, ALU.mult, ALU.add, [cols], [omk])
            wdu = P.sb("wdu", [64, 512], BF16)
            wiu = P.sb("wiu", [64, 512], BF16)
            wgu = P.sb("wgu", [128, 512], BF16)
            P.dma(LD, wdu[:], wb["wdu"][l, :, :], writes=[wdu])
            P.dma(LD, wiu[:], wb["wiu"][l, :, :], writes=[wiu])
            P.dma(LD, wgu[:], wb["wgu"][l, :, :], writes=[wgu])
            shr = P.rot("sh", [128, 513], F32, 4)
            dtmp = P.rot("dtmp", [128, 512], F32, 2)
            psr = P.rot("ps", [128, 512], F32, 4, psum=True)
            ptr = P.rot("ptT", [128, 512], BF16, 2, psum=True)
            stb = P.rot("stb", [128, 512], BF16, 6)
            stf = P.rot("stf", [128, 512], F32, 4)
            sttok = P.rot("sttok", [128, 4, 128], BF16, 3)

            def shift_mix(row0, M, mucol, b, out):
                sh = shr.next()
                P.dma(LD, sh[0:M, :], pT[row0:row0 + M, b * 512:b * 512 + 513], writes=[sh])
                d = dtmp.next()
                P.tt("dve", d[0:M, :], sh[0:M, 0:512], sh[0:M, 1:513], ALU.subtract, [sh], [d])
                P.stt("dve", out[0:M, :], d[0:M, :], cols[0:M, mucol:mucol + 1], sh[0:M, 1:513], ALU.mult, ALU.add,
                      [d, cols, sh], [out])

            def store_feat(dst, c, b, src):
                P.dma(ST, dst[c * 128:(c + 1) * 128, b * 512:(b + 1) * 512], src[:], reads=[src])

            def store_tok(dst, c, b, srcb):
                pt = ptr.next()
                for i in range(4):
                    P.tr(pt[:, i * 128:(i + 1) * 128], srcb[:, i * 128:(i + 1) * 128], identb[:], [srcb, identb], [pt], sig=(i == 3))
                s = sttok.next()
                P.cp("act", s[:], pt[:].rearrange("p (i c) -> p i c", c=128), [pt], [s])
                P.dma(ST, dst[b * 512:(b + 1) * 512, c * 128:(c + 1) * 128].rearrange("(i p) c -> p i c", p=128), s[:],
                      reads=[s])

            for b in range(NB):
                mw = P.sbc("mw", [64, 512])
                ma = P.sbc("ma", [64, 512])
                mg = P.sbc("mg", [128, 512])
                shift_mix(R_W, 64, C_MU + 12, b, mw)
                shift_mix(R_A, 64, C_MU + 13, b, ma)
                shift_mix(R_G, 128, C_MU + 14, b, mg)
                tw = P.sbc("tw", [64, 512], BF16)
                pab = P.sbc("pab", [64, 512], BF16)
                sgg = P.sbc("sgg", [128, 512], BF16)
                P.act(tw[:], mw[:], AF.Tanh, [mw], [tw])
                P.cp("dve", pab[:], ma[:], [ma], [pab])
                P.act(sgg[:], mg[:], AF.Sigmoid, [mg], [sgg])
                for c in range(4):
                    cs = slice(c * 128, (c + 1) * 128)
                    rm = P.sbc("rm", [128, 512])
                    km = P.sbc("km", [128, 512])
                    vm = P.sbc("vm", [128, 512])
                    shift_mix(R_R + c * 128, 128, C_MU + c, b, rm)
                    shift_mix(R_K + c * 128, 128, C_MU + 4 + c, b, km)
                    shift_mix(R_V + c * 128, 128, C_MU + 8 + c, b, vm)
                    ps = psr.next()
                    P.mm(ps[:], wdu[:, cs], tw[:], True, True, [wdu, tw], [ps])
                    ld = P.sbc("ld", [128, 512])
                    P.act(ld[:], ps[:], AF.Sigmoid, [ps, cols], [ld], bias=cols[:, C_DB + c:C_DB + c + 1], scale=1.0)
                    P.ts("dve", ld[:], ld[:], -math.exp(-0.5), None, ALU.mult, None, [ld], [ld])
                    cum = P.sbc("cum", [128, 512])
                    P.op("dve", lambda e, cum=cum, ld=ld: e.tensor_tensor_scan(
                        out=cum[:], data0=K["rmask"][:], data1=ld[:], initial=0.0, op0=ALU.mult, op1=ALU.add),
                        [K["rmask"], ld], [cum])
                    G = P.sbc("G", [128, 512])
                    Gi = P.sbc("Gi", [128, 512])
                    Gp = P.sbc("Gp", [128, 512])
                    Ge = P.sbc("Ge", [128, 512])
                    P.act(G[:], cum[:], AF.Exp, [cum], [G])
                    P.act(Gi[:], cum[:], AF.Exp, [cum], [Gi], scale=-1.0)
                    P.tt("dve", Gp[:], cum[:], ld[:], ALU.subtract, [cum, ld], [Gp])
                    P.act(Gp[:], Gp[:], AF.Exp, [Gp], [Gp])
                    cum3 = cum[:].rearrange("p (a t) -> p a t", t=128)
                    P.tt("dve", Ge[:].rearrange("p (a t) -> p a t", t=128), cum3[:, :, 127:128].to_broadcast([128, 4, 128]),
                         cum3, ALU.subtract, [cum], [Ge])
                    P.act(Ge[:], Ge[:], AF.Exp, [Ge], [Ge])
                    gc = P.sbc("gc", [128, 4])
                    P.act(gc[:].rearrange("p (a o) -> p a o", o=1), cum3[:, :, 127:128], AF.Exp, [cum], [gc])
                    P.dma(ST, gCD[cs, b * 4:(b + 1) * 4], gc[:], reads=[gc])
                    ps = psr.next()
                    P.mm(ps[:], wiu[:, cs], pab[:], True, True, [wiu, pab], [ps])
                    ic = P.sbc("ic", [128, 512])
                    P.act(ic[:], ps[:], AF.Sigmoid, [ps, cols], [ic], bias=cols[:, C_IB + c:C_IB + c + 1], scale=1.0)
                    ps = psr.next()
                    P.mm(ps[:], wgu[:, cs], sgg[:], True, True, [wgu, sgg], [ps])
                    gt = stf.next()
                    P.cp("act", gt[:], ps[:], [ps], [gt])
                    store_feat(gateD, c, b, gt)
                    kk = P.sbc("kk", [128, 512])
                    sq = P.sbc("sq", [128, 512])
                    P.ts("dve", kk[:], km[:], cols[:, C_KK + c:C_KK + c + 1], None, ALU.mult, None, [km, cols], [kk])
                    P.tt("pool", sq[:], kk[:], kk[:], ALU.mult, [kk], [sq])
                    ps = psr.next()
                    P.mm(ps[:], K["bones"][:], sq[:], True, True, [K["bones"], sq], [ps])
                    rn = P.sbc("rn", [128, 512])
                    P.rsqrt(rn[:], ps[:], [ps], [rn], scale=1.0, bias=1e-12)
                    P.tt("dve", kk[:], kk[:], rn[:], ALU.mult, [kk, rn], [kk])
                    tk = P.sbc("tk", [128, 512])
                    P.ts("dve", tk[:], ic[:], cols[:, C_KA + c:C_KA + c + 1], omk[:, c:c + 1], ALU.mult, ALU.add,
                         [ic, cols, omk], [tk])
                    kf = P.sbc("kf", [128, 512])
                    P.tt("dve", kf[:], km[:], tk[:], ALU.mult, [km, tk], [kf])
                    bb = P.sbc("bb", [128, 512])
                    P.tt("pool", bb[:], kk[:], ic[:], ALU.mult, [kk, ic], [bb])
                    o = stb.next()
                    P.tt("dve", o[:], rm[:], G[:], ALU.mult, [rm, G], [o])
                    store_feat(rtD, c, b, o)
                    o = stb.next()
                    P.stt("dve", o[:], kk[:], -1.0, Gp[:], ALU.mult, ALU.mult, [kk, Gp], [o])
                    store_feat(atD, c, b, o)
                    o = stb.next()
                    P.tt("dve", o[:], bb[:], Gi[:], ALU.mult, [bb, Gi], [o])
                    store_feat(btD, c, b, o)
                    o = stb.next()
                    P.tt("dve", o[:], kf[:], Gi[:], ALU.mult, [kf, Gi], [o])
                    store_feat(ktD, c, b, o)
                    o = stb.next()
                    P.tt("dve", o[:], bb[:], Ge[:], ALU.mult, [bb, Ge], [o])
                    store_tok(BhD, c, b, o)
                    o = stb.next()
                    P.tt("pool", o[:], kf[:], Ge[:], ALU.mult, [kf, Ge], [o])
                    store_tok(KhD, c, b, o)
                    o = stb.next()
                    P.cp("pool", o[:], vm[:], [vm], [o])
                    store_tok(VtD, c, b, o)
                    rk = P.sbc("rk", [128, 512])
                    P.stt("dve", rk[:], rm[:], cols[:, C_RK + c:C_RK + c + 1], kf[:], ALU.mult, ALU.mult, [rm, cols, kf], [rk])
                    ps = psr.next()
                    P.mm(ps[:], K["bones"][:], rk[:], True, True, [K["bones"], rk], [ps])
                    bo = stf.next()
                    P.tt("dve", bo[:], ps[:], vm[:], ALU.mult, [ps, vm], [bo])
                    store_feat(bonD, c, b, bo)

        P.phase(phase_a2)
        if done("a2_%d" % l):
            break

        def phase_scan():
            K = load_consts(["ident", "mstrict", "mask3", "mstrictT"])
            Hf = P.sb("Hf", [64, 8, 64])
            Hb = P.sb("Hb", [64, 8, 64], BF16)
            Ht = P.sb("Htmp", [64, 8, 64])
            P.memset("dve", Hf[:], 0.0, [Hf])
            P.memset("dve", Hb[:], 0.0, [Hb])
            artr = P.rot("art", [64, 8, 256], BF16, 2)
            btr = P.rot("btl", [64, 8, 128], BF16, 2)
            ktr = P.rot("ktl", [64, 8, 128], BF16, 2)
            Bhr = P.rot("Bht", [128, 512], BF16, 2)
            Khr = P.rot("Kht", [128, 512], BF16, 2)
            Vr = P.rot("Vtt", [128, 512], BF16, 2)
            gcr = P.rot("gct", [64, 8], F32, 2)
            PA = P.ps("PA", [128, 512])
            PXs = [P.ps("PX%d" % i, [128, 512]) for i in range(2)]
            PD = P.ps("PD", [128, 512])
            PW = P.ps("PW", [128, 512])
            PU = P.ps("PU", [128, 512])
            PY = P.ps("PY", [64, 512])
            PH = P.ps("PH", [64, 512])
            AX3r = P.rot("AX3", [128, 8, 384], BF16, 2)
            Ttr = P.rot("Tt", [128, 8, 128], BF16, 2)
            Nfs = [P.sb("Nf%d" % i, [128, 128]) for i in range(4)]
            Afs = [P.sb("Af%d" % i, [128, 128]) for i in range(4)]
            NAs = [[P.sb("NA%d_%d" % (i, k), [128, 256]) for k in range(2)] for i in range(4)]
            Pms = [[P.sb("Pm%d_%d" % (i, k), [128, 128]) for k in range(2)] for i in range(4)]
            Wsr = P.rot("Wsb", [128, 512], BF16, 2)
            Ubr = P.rot("Ub", [128, 512], BF16, 2)
            Ysr = P.rot("Ysb", [64, 8, 128], F32, 2)

            for n in range(NT):
                ts_ = slice(n * 128, (n + 1) * 128)
                art = artr.next()
                btl = btr.next()
                ktl = ktr.next()
                Bht = Bhr.next()
                Kht = Khr.next()
                Vtt = Vr.next()
                gct = gcr.next()
                P.dma(LD, art[:, :, 0:128], atD[:, ts_].rearrange("(h j) t -> j h t", j=64), writes=[art])
                P.dma(LD, art[:, :, 128:256], rtD[:, ts_].rearrange("(h j) t -> j h t", j=64), writes=[art])
                P.dma(LD, btl[:], btD[:, ts_].rearrange("(h j) t -> j h t", j=64), writes=[btl])
                P.dma(LD, ktl[:], ktD[:, ts_].rearrange("(h j) t -> j h t", j=64), writes=[ktl])
                P.dma(LD, Bht[:], BhD[ts_, :], writes=[Bht])
                P.dma(LD, Kht[:], KhD[ts_, :], writes=[Kht])
                P.dma(LD, Vtt[:], VtD[ts_, :], writes=[Vtt])
                P.dma(LD, gct[:], gCD[:, n:n + 1].rearrange("(h j) o -> j (h o)", j=64), writes=[gct], slow=True)
                AX3 = AX3r.next()
                Tt = Ttr.next()
                for g in range(2):
                    for hh in range(4):
                        h = g * 4 + hh
                        PXb = PXs[hh // 2]
                        xo = (hh % 2) * 256
                        P.mm(PA[:, 0:256], btl[:, h, :], art[:, h, :], True, True, [btl, art], [PA])
                        P.mm(PA[:, 256:512], ktl[:, h, :], art[:, h, :], True, True, [ktl, art], [PA])
                        P.mm(PXb[:, xo:xo + 128], art[:, h, 0:128], btl[:, h, :], True, True, [art, btl], [PXb])
                        P.tt("dve", Nfs[hh][:], PA[:, 0:128], K["mstrict"][:], ALU.mult, [PA, K["mstrict"]], [Nfs[hh]])
                        P.tt("dve", AX3[:, h, :], PA[:, 128:512], K["mask3"][:], ALU.mult, [PA, K["mask3"]], [AX3])
                        P.tt("dve", Afs[hh][:], PXb[:, xo:xo + 128], K["mstrictT"][:], ALU.mult, [PXb, K["mstrictT"]], [Afs[hh]])
                        P.tt("pool", Pms[hh][0][:], Nfs[hh][:], K["ident"][:], ALU.add, [Nfs[hh], K["ident"]], [Pms[hh][0]])
                    curN = [Nfs[hh][:] for hh in range(4)]
                    curA = [Afs[hh][:] for hh in range(4)]
                    curB = [(Nfs[hh], Afs[hh]) for hh in range(4)]
                    for lv in range(6):
                        last = lv == 5
                        for hh in range(4):
                            PXb = PXs[hh // 2]
                            xo = (hh % 2) * 256
                            if not last:
                                P.mm(PXb[:, xo:xo + 128], curA[hh], curN[hh], True, True, list(curB[hh]), [PXb])
                            P.mm(PXb[:, xo + 128:xo + 256], curN[hh], curA[hh], True, True, list(curB[hh]), [PXb])
                        for hh in range(4):
                            PXb = PXs[hh // 2]
                            xo = (hh % 2) * 256
                            NA = NAs[hh][lv % 2]
                            if last:
                                P.cp("act", NA[:, 128:256], PXb[:, xo + 128:xo + 256], [PXb], [NA])
                            else:
                                P.cp("act", NA[:, 0:256], PXb[:, xo:xo + 256], [PXb], [NA])
                        for hh in range(4):
                            NA = NAs[hh][lv % 2]
                            P.mm(PD[:, hh * 128:(hh + 1) * 128], NA[:, 128:256], Pms[hh][lv % 2][:], True, True,
                                 [NA, Pms[hh][lv % 2]], [PD])
                        for hh in range(4):
                            h = g * 4 + hh
                            NA = NAs[hh][lv % 2]
                            if last:
                                P.tt("dve", Tt[:, h, :], Pms[hh][lv % 2][:], PD[:, hh * 128:(hh + 1) * 128], ALU.add,
                                     [Pms[hh][lv % 2], PD], [Tt])
                            else:
                                P.tt("dve", Pms[hh][(lv + 1) % 2][:], Pms[hh][lv % 2][:], PD[:, hh * 128:(hh + 1) * 128], ALU.add,
                                     [Pms[hh][lv % 2], PD], [Pms[hh][(lv + 1) % 2]])
                            curN[hh] = NA[:, 0:128]
                            curA[hh] = NA[:, 128:256]
                            curB[hh] = (NA, NA)
                for h in range(8):
                    hs = slice(h * 64, (h + 1) * 64)
                    P.mm(PW[:, hs], art[:, h, 0:128], Hb[:, h, :], True, False, [art, Hb], [PW])
                    P.mm(PW[:, hs], AX3[:, h, 128:256], Vtt[:, hs], False, True, [AX3, Vtt], [PW])
                Wsb = Wsr.next()
                P.cp("act", Wsb[:], PW[:], [PW], [Wsb])
                for h in range(8):
                    hs = slice(h * 64, (h + 1) * 64)
                    P.mm(PU[:, hs], Tt[:, h, :], Wsb[:, hs], True, True, [Tt, Wsb], [PU])
                Ub = Ubr.next()
                P.cp("dve", Ub[:], PU[:], [PU], [Ub])
                Ysb = Ysr.next()
                for half in range(2):
                    for hh in range(4):
                        h = half * 4 + hh
                        hs = slice(h * 64, (h + 1) * 64)
                        yo = PY[:, hh * 128:(hh + 1) * 128]
                        P.mm(yo, Hb[:, h, :], art[:, h, 128:256], True, False, [Hb, art], [PY])
                        P.mm(yo, Ub[:, hs], AX3[:, h, 0:128], False, False, [Ub, AX3], [PY])
                        P.mm(yo, Vtt[:, hs], AX3[:, h, 256:384], False, True, [Vtt, AX3], [PY])
                    P.cp("act", Ysb[:, half * 4:(half + 1) * 4, :], PY[:].rearrange("p (h t) -> p h t", t=128), [PY], [Ysb])
                P.dma(ST, yTD[:, ts_].rearrange("(h i) t -> i h t", i=64), Ysb[:], reads=[Ysb])
                for h in range(8):
                    hs = slice(h * 64, (h + 1) * 64)
                    P.mm(PH[:, hs], Bht[:, hs], Ub[:, hs], True, False, [Bht, Ub], [PH])
                    P.mm(PH[:, hs], Kht[:, hs], Vtt[:, hs], False, True, [Kht, Vtt], [PH])
                P.tt("dve", Ht[:], Hf[:], gct[:].rearrange("p (h o) -> p h o", o=1).to_broadcast([64, 8, 64]), ALU.mult,
                     [Hf, gct], [Ht])
                P.tt("dve", Hf[:], Ht[:], PH[:].rearrange("p (h i) -> p h i", i=64), ALU.add, [Ht, PH], [Hf])
                P.cp("act", Hb[:], Hf[:], [Hf], [Hb])

        P.phase(phase_scan)
        if done("scan_%d" % l):
            break

        def phase_m0():
            K = load_consts(["ones"])
            ones = K["ones"]
            cols = P.sb("cols", [128, NCOL])
            P.dma(LD, cols[:], cols_in[l, :, :], writes=[cols])
            wqu = P.sb("wqu", [128, 2, 1024], BF16)
            wkvu = P.sb("wkvu", [128, 1024], BF16)
            for c in range(2):
                P.dma(LD, wqu[:, c, :], wb["wqu"][l, c * 128:(c + 1) * 128, :], writes=[wqu])
            P.dma(LD, wkvu[:], wb["wkvu"][l, :, :], writes=[wkvu])
            psr = P.rot("ps", [128, 512], F32, 6, psum=True)
            stb = P.rot("stb", [128, 512], BF16, 4)
            for b in range(NB):
                bs = slice(1 + b * 512, 1 + (b + 1) * 512)
                bs0 = slice(b * 512, (b + 1) * 512)
                CC = P.sbc("CC", [128, 512])
                SS = P.sbc("SS", [128, 512])
                P.dma(LD, CC[:], CCd[:, bs0], writes=[CC])
                P.dma(LD, SS[:], SSd[:, bs0], writes=[SS])
                ql = [P.sbc("ql%d" % c, [128, 512]) for c in range(2)]
                kvl = P.sbc("kvl", [128, 512])
                kpe = P.sbc("kpe", [32, 512])
                kpes = P.sbc("kpes", [32, 512])
                for c in range(2):
                    P.dma(LD, ql[c][:], pT[R_QL + c * 128:R_QL + (c + 1) * 128, bs], writes=[ql[c]])
                P.dma(LD, kvl[:], pT[R_KVL:R_KVL + 128, bs], writes=[kvl])
                P.dma(LD, kpe[:], pT[R_KPE:R_KPE + 32, bs], writes=[kpe])
                P.dma(LD, kpes[:], pT[R_KPES:R_KPES + 32, bs], writes=[kpes])
                ps = psr.next()
                for c in range(2):
                    sq = P.sbc("sq", [128, 512])
                    P.act(sq[:], ql[c][:], AF.Square, [ql[c]], [sq])
                    P.mm(ps[:], ones[:], sq[:], c == 0, c == 1, [ones, sq], [ps])
                rs = P.sbc("rs", [128, 512])
                P.rsqrt(rs[:], ps[:], [ps], [rs], scale=1.0 / 256, bias=EPS)
                qn = [P.sbc("qn%d" % c, [128, 512], BF16) for c in range(2)]
                for c in range(2):
                    P.stt("dve", qn[c][:], ql[c][:], cols[:, C_QN + c:C_QN + c + 1], rs[:], ALU.mult, ALU.mult,
                          [ql[c], cols, rs], [qn[c]])
                for hp in range(4):
                    ps = psr.next()
                    for c in range(2):
                        P.mm(ps[:], wqu[:, c, hp * 128:(hp + 1) * 128], qn[c][:], c == 0, c == 1, [wqu, qn[c]], [ps])
                    o = stb.next()
                    P.cp("act", o[:], ps[:], [ps], [o])
                    P.dma(ST, qnD[hp * 128:(hp + 1) * 128, bs0], o[:], reads=[o])
                for g in range(2):
                    ps1 = psr.next()
                    ps2 = psr.next()
                    for c in range(2):
                        P.mm(ps1[:], wqu[:, c, 512 + g * 128:512 + (g + 1) * 128], qn[c][:], c == 0, c == 1, [wqu, qn[c]], [ps1])
                    for c in range(2):
                        P.mm(ps2[:], wqu[:, c, 768 + g * 128:768 + (g + 1) * 128], qn[c][:], c == 0, c == 1, [wqu, qn[c]], [ps2])
                    t1 = P.sbc("t1", [128, 512])
                    t2 = P.sbc("t2", [128, 512])
                    P.tt("dve", t1[:], ps1[:], CC[:], ALU.mult, [ps1, CC], [t1])
                    P.tt("dve", t2[:], ps2[:], SS[:], ALU.mult, [ps2, SS], [t2])
                    o = stb.next()
                    P.tt("pool", o[:], t1[:], t2[:], ALU.add, [t1, t2], [o])
                    P.dma(ST, qrD[g * 128:(g + 1) * 128, bs0], o[:], reads=[o])
                sq = P.sbc("sq", [128, 512])
                P.act(sq[:], kvl[:], AF.Square, [kvl], [sq])
                ps = psr.next()
                P.mm(ps[:], ones[:], sq[:], True, True, [ones, sq], [ps])
                rs2 = P.sbc("rs2", [128, 512])
                P.rsqrt(rs2[:], ps[:], [ps], [rs2], scale=1.0 / 128, bias=EPS)
                kvn = P.sbc("kvn", [128, 512], BF16)
                P.stt("dve", kvn[:], kvl[:], cols[:, C_KVN:C_KVN + 1], rs2[:], ALU.mult, ALU.mult, [kvl, cols, rs2], [kvn])
                for hp in range(4):
                    ps = psr.next()
                    P.mm(ps[:], wkvu[:, hp * 128:(hp + 1) * 128], kvn[:], True, True, [wkvu, kvn], [ps])
                    o = stb.next()
                    P.cp("act", o[:], ps[:], [ps], [o])
                    P.dma(ST, knD[hp * 128:(hp + 1) * 128, bs0], o[:], reads=[o])
                for i in range(4):
                    ps = psr.next()
                    P.mm(ps[:], kvn[:, i * 128:(i + 1) * 128], wkvu[:, 512:1024], True, True, [kvn, wkvu], [ps])
                    o = stb.next()
                    P.cp("act", o[:], ps[:], [ps], [o])
                    P.dma(ST, vtokD[b * 512 + i * 128:b * 512 + (i + 1) * 128, :], o[:], reads=[o])
                t1 = P.sbc("t1", [128, 512])
                t2 = P.sbc("t2", [128, 512])
                P.tt("dve", t1[0:32, :], kpe[:], CC[0:32, :], ALU.mult, [kpe, CC], [t1])
                P.tt("dve", t2[0:32, :], kpes[:], SS[0:32, :], ALU.mult, [kpes, SS], [t2])
                o = stb.next()
                P.tt("pool", o[0:32, :], t1[0:32, :], t2[0:32, :], ALU.add, [t1, t2], [o])
                P.dma(ST, krD[:, bs0], o[0:32, :], reads=[o])

        P.phase(phase_m0)
        if done("m0_%d" % l):
            break

        def phase_attn():
            K = load_consts(["ident_b", "cmask_b"])
            identb = K["ident_b"]
            cmaskb = K["cmask_b"]
            Kn = P.sb("Kn", [64, 8, T], BF16)
            Kr = P.sb("Kr", [32, T], BF16)
            V = P.sb("V", [128, NT, 512], BF16)
            for h in range(8):
                P.dma(LD, Kn[:, h, :], knD[h * 64:(h + 1) * 64, :], writes=[Kn])
            P.dma(LD, Kr[:], krD[:, :], writes=[Kr])
            for n in range(NT):
                P.dma(LD, V[:, n, :], vtokD[n * 128:(n + 1) * 128, :], writes=[V])
            Qnr = P.rot("Qn", [64, 8, 128], BF16, 2)
            Qrr = P.rot("Qr", [32, 8, 128], BF16, 2)
            PS1 = P.rot("PS1", [128, 512], F32, 2, psum=True)
            PS2 = P.rot("PS2", [128, 512], F32, 2, psum=True)
            PTr = P.rot("PT", [128, 512], BF16, 2, psum=True)
            POr = P.rot("PO", [128, 64], F32, 1, psum=True)
            PTo = P.ps("PTo", [128, 512], BF16)
            Pbr = P.rot("Pb", [128, 512], BF16, 3)
            PTsr = P.rot("PTs", [128, 512], BF16, 3)
            mxr = P.rot("mx", [128, 8], F32, 2)
            rsr = P.rot("rs", [128, 8], F32, 2)
            smr = P.rot("sm", [128, 4], F32, 2)
            Osr = P.rot("Os", [128, 512], F32, 2)
            Obr = P.rot("Obf", [128, 512], BF16, 2)
            OTr = P.rot("OT", [128, 4, 128], BF16, 2)

            def scores(ps, Qn, Qr, h, qi, ch):
                k0 = ch * 4
                k1 = min(k0 + 4, qi + 1)
                w = (k1 - k0) * 128
                diag = (k1 == qi + 1)
                P.mm(ps[:, 0:w], Qn[:, h, :], Kn[:, h, k0 * 128:k0 * 128 + w], True, False, [Qn, Kn], [ps])
                P.mm(ps[:, 0:w], Qr[:, h, :], Kr[:, k0 * 128:k0 * 128 + w], False, not diag, [Qr, Kr], [ps])
                if diag:
                    P.mm(ps[:, w - 128:w], identb[:], cmaskb[:], False, True, [identb, cmaskb], [ps])
                return w, k0, k1

            for qi in range(NT):
                Qn = Qnr.next()
                Qr = Qrr.next()
                qs = slice(qi * 128, (qi + 1) * 128)
                P.dma(LD, Qn[:], qnD[:, qs].rearrange("(h c) t -> c h t", c=64), writes=[Qn])
                P.dma(LD, Qr[:], qrD[:, qs].rearrange("(h c) t -> c h t", c=32), writes=[Qr])
                nch = (qi + 4) // 4
                Os = Osr.next()
                for h in range(8):
                    mx = mxr.next()
                    rs = rsr.next()
                    sm = smr.next()
                    for ch in range(nch):
                        ps = PS1.next()
                        w, k0, k1 = scores(ps, Qn, Qr, h, qi, ch)
                        P.op("dve", lambda e, ps=ps, mx=mx, ch=ch, w=w: e.reduce_max(out=mx[:, ch:ch + 1], in_=ps[:, 0:w], axis=AX.X),
                             [ps], [mx])
                    P.op("dve", lambda e, mx=mx, sm=sm, nch=nch: e.reduce_max(out=sm[:, 0:1], in_=mx[:, 0:nch], axis=AX.X), [mx], [sm])
                    P.ts("dve", sm[:, 1:2], sm[:, 0:1], -SCALE, None, ALU.mult, None, [sm], [sm])
                    PO = POr.next()
                    for ch in range(nch):
                        ps = PS2.next()
                        w, k0, k1 = scores(ps, Qn, Qr, h, qi, ch)
                        Pb = Pbr.next()
                        P.act(Pb[:, 0:w], ps[:, 0:w], AF.Exp, [ps, sm], [Pb, rs], bias=sm[:, 1:2], scale=SCALE,
                              accum_out=rs[:, ch:ch + 1])
                        PT = PTr.next()
                        for j in range(k1 - k0):
                            P.tr(PT[:, j * 128:(j + 1) * 128], Pb[:, j * 128:(j + 1) * 128], identb[:], [Pb, identb], [PT], sig=(j == k1 - k0 - 1))
                        PTs = PTsr.next()
                        if ch % 2 == 0:
                            P.cp("dve", PTs[:, 0:w], PT[:, 0:w], [PT], [PTs])
                        else:
                            P.cp("act", PTs[:, 0:w], PT[:, 0:w], [PT], [PTs])
                        for j in range(k1 - k0):
                            kt = k0 + j
                            P.mm(PO[:, :], PTs[:, j * 128:(j + 1) * 128], V[:, kt, h * 64:(h + 1) * 64], kt == 0, kt == qi,
                                 [PTs, V], [PO])
                    P.op("dve", lambda e, rs=rs, sm=sm, nch=nch: e.reduce_sum(out=sm[:, 2:3], in_=rs[:, 0:nch], axis=AX.X), [rs], [sm])
                    P.recip(sm[:, 3:4], sm[:, 2:3], [sm], [sm])
                    P.ts("dve", Os[:, h * 64:(h + 1) * 64], PO[:, :], sm[:, 3:4], None, ALU.mult, None, [PO, sm], [Os])
                Ob = Obr.next()
                P.cp("act", Ob[:], Os[:], [Os], [Ob])
                for c in range(4):
                    P.tr(PTo[:, c * 128:(c + 1) * 128], Ob[:, c * 128:(c + 1) * 128], identb[:], [Ob, identb], [PTo], sig=(c == 3))
                OT = OTr.next()
                P.cp("dve", OT[:], PTo[:].rearrange("p (c t) -> p c t", t=128), [PTo], [OT])
                P.dma(ST, oTD[:, qs].rearrange("(c e) t -> e c t", e=128), OT[:], reads=[OT])

        P.phase(phase_attn)
        if done("attn_%d" % l):
            break

        def phase_out():
            K = load_consts(["bavg"])
            bavg = K["bavg"]
            cols = P.sb("cols", [128, NCOL])
            P.dma(LD, cols[:], cols_in[l, :, :], writes=[cols])
            woa = P.sb("woa", [128, 4, D], BF16)
            wob = P.sb("wob", [128, 4, D], BF16)
            wout = P.sb("wout", [128, 8, D], BF16)
            for c in range(4):
                P.dma(LD, woa[:, c, :], wb["woa"][l, c * 128:(c + 1) * 128, :], writes=[woa])
                P.dma(LD, wob[:, c, :], wb["wob"][l, c * 128:(c + 1) * 128, :], writes=[wob])
            for c in range(8):
                P.dma(LD, wout[:, c, :], wb["wout"][l, c * 128:(c + 1) * 128, :], writes=[wout])
            psr = P.rot("ps", [128, 512], F32, 6, psum=True)
            ldr = P.rot("ldf", [128, 512], F32, 6)
            tmpr = P.rot("tmp", [128, 512], F32, 6)
            roTr = P.rot("roT", [128, 4, 512], BF16, 2)
            oTr = P.rot("oTb", [128, 4, 512], BF16, 2)
            mgr = P.rot("mg", [128, 8, 512], BF16, 2)
            htr = P.rot("ht", [128, D], F32, 2)
            hnr = P.rot("hn", [128, D], F32, 2)
            for b in range(NB):
                bs0 = slice(b * 512, (b + 1) * 512)
                bs = slice(1 + b * 512, 1 + (b + 1) * 512)
                roT = roTr.next()
                for c in range(4):
                    cs = slice(c * 128, (c + 1) * 128)
                    y = ldr.next()
                    bo = ldr.next()
                    gt = ldr.next()
                    P.dma(LD, y[:], yTD[cs, bs0], writes=[y])
                    P.dma(LD, bo[:], bonD[cs, bs0], writes=[bo])
                    P.dma(LD, gt[:], gateD[cs, bs0], writes=[gt])
                    ps = psr.next()
                    P.mm(ps[:], bavg[:], y[:], True, True, [bavg, y], [ps])
                    cen = tmpr.next()
                    P.tt("dve", cen[:], y[:], ps[:], ALU.subtract, [y, ps], [cen])
                    sq = tmpr.next()
                    P.act(sq[:], cen[:], AF.Square, [cen], [sq])
                    ps = psr.next()
                    P.mm(ps[:], bavg[:], sq[:], True, True, [bavg, sq], [ps])
                    rs = tmpr.next()
                    P.rsqrt(rs[:], ps[:], [ps], [rs], scale=1.0, bias=LNX_EPS)
                    P.tt("dve", cen[:], cen[:], rs[:], ALU.mult, [cen, rs], [cen])
                    P.ts("dve", cen[:], cen[:], cols[:, C_LW + c:C_LW + c + 1], cols[:, C_LB + c:C_LB + c + 1], ALU.mult, ALU.add,
                         [cen, cols], [cen])
                    P.tt("pool", cen[:], cen[:], bo[:], ALU.add, [cen, bo], [cen])
                    P.tt("dve", roT[:, c, :], cen[:], gt[:], ALU.mult, [cen, gt], [roT])
                oTb = oTr.next()
                for c in range(4):
                    P.dma(LD, oTb[:, c, :], oTD[c * 128:(c + 1) * 128, bs0], writes=[oTb])
                mg = mgr.next()
                for dc in range(8):
                    ds_ = slice(dc * 128, (dc + 1) * 128)
                    sga = ldr.next()
                    sgb = ldr.next()
                    P.dma(LD, sga[:], pT[R_GA + dc * 128:R_GA + (dc + 1) * 128, bs], writes=[sga])
                    P.dma(LD, sgb[:], pT[R_GB + dc * 128:R_GB + (dc + 1) * 128, bs], writes=[sgb])
                    pa = psr.next()
                    pb = psr.next()
                    for c in range(4):
                        P.mm(pa[:], woa[:, c, ds_], roT[:, c, :], c == 0, c == 3, [woa, roT], [pa])
                    for c in range(4):
                        P.mm(pb[:], wob[:, c, ds_], oTb[:, c, :], c == 0, c == 3, [wob, oTb], [pb])
                    m1 = tmpr.next()
                    m2 = tmpr.next()
                    P.tt("dve", m1[:], pa[:], sga[:], ALU.mult, [pa, sga], [m1])
                    P.tt("dve", m2[:], pb[:], sgb[:], ALU.mult, [pb, sgb], [m2])
                    P.tt("pool", mg[:, dc, :], m1[:], m2[:], ALU.add, [m1, m2], [mg])
                for i in range(4):
                    t0 = b * 512 + i * 128
                    ht = htr.next()
                    hn = hnr.next()
                    P.dma(LD, ht[:], h_src[t0:t0 + 128, :], writes=[ht])
                    for half in range(2):
                        pd = psr.next()
                        for dc in range(8):
                            P.mm(pd[:], mg[:, dc, i * 128:(i + 1) * 128], wout[:, dc, half * 512:(half + 1) * 512], dc == 0, dc == 7,
                                 [mg, wout], [pd])
                        P.tt("dve", hn[:, half * 512:(half + 1) * 512], ht[:, half * 512:(half + 1) * 512], pd[:], ALU.add,
                             [ht, pd], [hn])
                    P.dma(ST, hD[t0:t0 + 128, :], hn[:], reads=[hn])

        P.phase(phase_out)
        if done("out_%d" % l):
            break

        def phase_ffn():
            lastl = is_last
            K = load_consts(["ident_b"])
            identb = K["ident_b"]
            cols = P.sb("cols", [128, NCOL])
            P.dma(LD, cols[:], cols_in[l, :, :], writes=[cols])
            wdn = P.sb("wdn", [128, 22, D], BF16)
            for f in range(22):
                P.dma(LD, wdn[:, f, :], wb["wdn"][l, f * 128:(f + 1) * 128, :], writes=[wdn])
            fb = P.sb("fb", [128, D])
            P.dma(LD, fb[:], rows_in[2 * l + 1, :].partition_broadcast(128), writes=[fb])
            fnb = None
            if lastl:
                fnb = P.sb("fnb", [128, D])
                P.dma(LD, fnb[:], rows_in[2 * depth, :].partition_broadcast(128), writes=[fnb])
            carry = P.sb("carry", [128, 22, 2])
            P.memset("dve", carry[:], 0.0, [carry])
            norm_alloc()
            hts = [P.sb("htk%d" % i, [128, D]) for i in range(4)]
            nTr = P.rot("nT", [128, 8, 512], BF16, 1)
            hmr = P.rot("hm", [128, 22, 512], BF16, 1)
            wgr = P.rot("wg", [128, 8, 128], BF16, 3)
            wvr = P.rot("wv", [128, 8, 128], BF16, 3)
            pgr = P.rot("pg", [128, 512], F32, 2, psum=True)
            pvr = P.rot("pv", [128, 512], F32, 2, psum=True)
            pdr = P.rot("pd", [128, 512], F32, 2, psum=True)
            ugr = P.rot("ug", [128, 514], F32, 2)
            cvr = P.rot("cv", [128, 512], F32, 2)
            hnr = P.rot("hn", [128, D], F32, 2)
            sqr = P.rot("fsq", [128, 2], F32, 2)
            for b in range(NB):
                nT = nTr.next()
                norm_transpose(lambda t: hD[t * 128:(t + 1) * 128, :], fb, identb, nT, b, ht_keep=hts)
                hm = hmr.next()
                for f in range(22):
                    wg = wgr.next()
                    wv = wvr.next()
                    P.dma(LD, wg[:], wb["wup"][l, :, f * 128:(f + 1) * 128].rearrange("(k p) n -> p k n", p=128), writes=[wg])
                    P.dma(LD, wv[:], wb["wup"][l, :, 2816 + f * 128:2816 + (f + 1) * 128].rearrange("(k p) n -> p k n", p=128),
                          writes=[wv])
                    pg = pgr.next()
                    pv = pvr.next()
                    for kc in range(8):
                        P.mm(pg[:], wg[:, kc, :], nT[:, kc, :], kc == 0, kc == 7, [wg, nT], [pg])
                    for kc in range(8):
                        P.mm(pv[:], wv[:, kc, :], nT[:, kc, :], kc == 0, kc == 7, [wv, nT], [pv])
                    ug = ugr.next()
                    P.cp("pool", ug[:, 0:2], carry[:, f, :], [carry], [ug])
                    P.cp("act", ug[:, 2:514], pg[:], [pg], [ug])
                    P.cp("pool", carry[:, f, :], ug[:, 512:514], [ug], [carry])
                    cv = cvr.next()
                    cw = lambda j: cols[:, C_CW + j * 22 + f:C_CW + j * 22 + f + 1]
                    P.ts("dve", cv[:], ug[:, 2:514], cw(2), cols[:, C_CB + f:C_CB + f + 1], ALU.mult, ALU.add, [ug, cols], [cv])
                    P.stt("dve", cv[:], ug[:, 1:513], cw(1), cv[:], ALU.mult, ALU.add, [ug, cols, cv], [cv])
                    P.stt("dve", cv[:], ug[:, 0:512], cw(0), cv[:], ALU.mult, ALU.add, [ug, cols, cv], [cv])
                    P.act(cv[:], cv[:], AF.Gelu, [cv], [cv])
                    P.tt("dve", hm[:, f, :], cv[:], pv[:], ALU.mult, [cv, pv], [hm])
                for i in range(4):
                    t0 = b * 512 + i * 128
                    hn = hnr.next()
                    for half in range(2):
                        pd = pdr.next()
                        for f in range(22):
                            P.mm(pd[:], hm[:, f, i * 128:(i + 1) * 128], wdn[:, f, half * 512:(half + 1) * 512], f == 0, f == 21,
                                 [hm, wdn], [pd])
                        P.tt("dve", hn[:, half * 512:(half + 1) * 512], hts[i][:, half * 512:(half + 1) * 512], pd[:], ALU.add,
                             [hts[i], pd], [hn])
                    if not lastl:
                        P.dma(ST, hD[t0:t0 + 128, :], hn[:], reads=[hn])
                    else:
                        junk = norm_transpose.junk.next()
                        sq = sqr.next()
                        P.act(junk[:], hn[:], AF.Square, [hn], [junk, sq], accum_out=sq[:, 0:1])
                        P.rsqrt(sq[:, 0:1], sq[:, 0:1], [sq], [sq], scale=1.0 / D, bias=EPS)
                        P.stt("dve", hn[:], hn[:], sq[:, 0:1], fnb[:], ALU.mult, ALU.mult, [hn, sq, fnb], [hn])
                        P.dma(ST, out_ap[t0:t0 + 128, :], hn[:], reads=[hn])

        P.phase(phase_ffn)
        if done("ffn_%d" % l):
            break

    P.root.close()
    return nc, P


def make_consts():
    c = np.zeros((128, NCST), np.float32)
    i = np.arange(128)
    c[:, K_ID:K_ID + 128] = np.eye(128)
    strict = (i[:, None] < i[None, :]).astype(np.float32)
    incl = (i[:, None] <= i[None, :]).astype(np.float32)
    c[:, K_MS:K_MS + 128] = strict
    c[:, K_M3:K_M3 + 128] = incl
    c[:, K_M3 + 128:K_M3 + 256] = strict
    c[:, K_M3 + 256:K_M3 + 384] = incl
    c[:, K_MST:K_MST + 128] = strict.T
    c[:, K_CM:K_CM + 128] = np.where(i[None, :] <= i[:, None], 0.0, NEG)
    blk = (i[:, None] // 64 == i[None, :] // 64).astype(np.float32)
    c[:, K_BO:K_BO + 128] = blk
    c[:, K_ON:K_ON + 128] = 1.0
    c[:, K_BA:K_BA + 128] = blk / 64.0
    rm = np.ones(512, np.float32)
    rm[::128] = 0.0
    c[:, K_RM:K_RM + 512] = rm[None, :]
    inv_freq = np.power(np.float32(10000.0), -np.arange(0, 32, 2, dtype=np.float32) / np.float32(32)).astype(np.float32)
    c[:, K_IF] = inv_freq[i % 16]
    c[:, K_SG] = np.where((i % 32) < 16, -1.0, 1.0)
    return c


def _colpack(v, n):
    v = np.asarray(v, np.float32).reshape(-1)
    out = np.zeros((128, n), np.float32)
    if v.size == 64:
        out[:64, 0] = v
    else:
        out[:, :] = v.reshape(n, 128).T
    return out


def prep_inputs(inp, depth=2):
    f = lambda k: np.asarray(inp[k], np.float32)
    cols = np.zeros((depth, 128, NCOL), np.float32)
    rows = np.zeros((2 * depth + 1, D), np.float32)
    w = {}
    w_in = f("w_in")
    kpe = w_in[:, :, 2176:2208]
    kpes = np.concatenate([kpe[:, :, 16:32], kpe[:, :, 0:16]], axis=-1)
    w["win"] = np.ascontiguousarray(np.concatenate([w_in[:, :, :2208], kpes, w_in[:, :, 2208:]], axis=-1))
    w["wdu"] = f("w_decay_up")
    w["wiu"] = f("w_iclr_up")
    w["wgu"] = f("w_gate_up")
    w["woa"] = f("w_out_rwkv")
    wq = f("w_q_up").reshape(depth, 256, 8, 96)
    nope = wq[..., :64].reshape(depth, 256, 512)
    pe = wq[..., 64:].reshape(depth, 256, 256)
    pes = np.concatenate([wq[..., 80:96], wq[..., 64:80]], axis=-1).reshape(depth, 256, 256)
    w["wqu"] = np.ascontiguousarray(np.concatenate([nope, pe, pes], axis=-1))
    wkv = f("w_kv_up").reshape(depth, 128, 8, 128)
    w["wkvu"] = np.ascontiguousarray(np.concatenate([wkv[..., :64].reshape(depth, 128, 512),
                                                     wkv[..., 64:].reshape(depth, 128, 512)], axis=-1))
    w["wob"] = f("w_out_mla")
    w["wout"] = f("w_out")
    w["wup"] = f("w_ffn_up")
    w["wdn"] = f("w_ffn_down")
    for l in range(depth):
        mu = f("mu_shift")[l]
        cols[l, :, C_MU:C_MU + 12] = _colpack(mu[:1536], 12)
        cols[l, :, C_MU + 12:C_MU + 13] = _colpack(mu[1536:1600], 1)
        cols[l, :, C_MU + 13:C_MU + 14] = _colpack(mu[1600:1664], 1)
        cols[l, :, C_MU + 14:C_MU + 15] = _colpack(mu[1664:1792], 1)
        cols[l, :, C_DB:C_DB + 4] = _colpack(f("decay_base")[l], 4)
        cols[l, :, C_IB:C_IB + 4] = _colpack(f("iclr_base")[l], 4)
        cols[l, :, C_KK:C_KK + 4] = _colpack(f("k_k")[l], 4)
        cols[l, :, C_KA:C_KA + 4] = _colpack(f("k_a")[l], 4)
        cols[l, :, C_RK:C_RK + 4] = _colpack(f("r_k")[l], 4)
        cols[l, :, C_LW:C_LW + 4] = _colpack(f("lnx_w")[l], 4)
        cols[l, :, C_LB:C_LB + 4] = _colpack(f("lnx_b")[l], 4)
        cols[l, :, C_QN:C_QN + 2] = _colpack(f("q_norm")[l], 2)
        cols[l, :, C_KVN:C_KVN + 1] = _colpack(f("kv_norm")[l], 1)
        for j in range(3):
            cols[l, :, C_CW + j * 22:C_CW + (j + 1) * 22] = _colpack(f("conv_w")[l, j], 22)
        cols[l, :, C_CB:C_CB + 22] = _colpack(f("conv_b")[l], 22)
        rows[2 * l] = f("attn_norm")[l]
        rows[2 * l + 1] = f("ffn_norm")[l]
    rows[2 * depth] = f("final_norm")
    shared = {"cst": make_consts(), "cols": cols, "rows": rows}
    shared.update(w)
    return shared


_CACHE = {}


def kernel(**inputs):
    x = np.asarray(inputs["x"], np.float32)
    pos = np.asarray(inputs["positions"], np.int32)
    B, T, _ = x.shape
    shared = prep_inputs(inputs)
    key = ("nc", T)
    if key not in _CACHE:
        _CACHE[key] = build(T)[0]
    nc = _CACHE[key]
    in_maps = []
    for core in range(8):
        b = core // 2
        m = dict(shared)
        m["x"] = np.ascontiguousarray(x[b])
        m["pos"] = np.ascontiguousarray(pos[b])
        in_maps.append(m)
    res = run_bass_kernel_spmd(nc, in_maps, core_ids=list(range(8)))
    out = np.empty((B, T, D), np.float32)
    half = T // 2
    for b in range(B):
        out[b, :half] = res.results[2 * b]["out"][:half]
        out[b, half:] = res.results[2 * b + 1]["out"][half:]
    return out
```
